# Optimizing a Trainium2 kernel written in Bass

```python
import jax, jax.numpy as jnp
from jax import lax
import numpy as np

D_MODEL = 1024
BATCH = 4
SEQ = 8192
DEPTH = 1

NORM_EPS = 1e-6

RWKV_HEAD = 64
RWKV_HEADS = D_MODEL // RWKV_HEAD
RWKV_DIM = RWKV_HEADS * RWKV_HEAD
DECAY_LORA = 64
ICLR_LORA = 64
GATE_LORA = 160
GN_EPS = 64e-5
RWKV_SPLIT = (RWKV_DIM, RWKV_DIM, RWKV_DIM, DECAY_LORA, ICLR_LORA, GATE_LORA)
RWKV_COLS = sum(RWKV_SPLIT)

NSA_HEAD = 64
NSA_HEADS = D_MODEL // NSA_HEAD
NSA_GROUPS = 4
NSA_HPG = NSA_HEADS // NSA_GROUPS
NSA_KV = NSA_GROUPS * NSA_HEAD
CMP_BLOCK = 32
CMP_STRIDE = 16
CMP_HIDDEN = 128
SLC_BLOCK = 64
N_SELECT = 16
WINDOW = 512
Q_BLOCK = 128
NSA_SPLIT = (NSA_HEADS * NSA_HEAD,) + (NSA_KV,) * 6 + (3 * NSA_HEADS,)
NSA_COLS = sum(NSA_SPLIT)

GATE_COLS = 2 * D_MODEL
IN_COLS = RWKV_COLS + NSA_COLS + GATE_COLS

D_FF = ((8 * D_MODEL // 3 + 255) // 256) * 256
CONV_W = 3

NEG_INF = -1e30
TINY = 1e-30
FORCE = 1e9

kernel_name = "hybrid_rwkv7_nsa_convffn"


def rms_norm(x, g):
    xf = x.astype(jnp.float32)
    y = xf * lax.rsqrt(jnp.mean(xf * xf, axis=-1, keepdims=True) + NORM_EPS)
    return (y * g.astype(jnp.float32)).astype(x.dtype)


def masked_softmax(s, mask, axis):
    s = jnp.where(mask, s, NEG_INF)
    e = jnp.where(mask, jnp.exp(s - jnp.max(s, axis=axis, keepdims=True)), 0.0)
    return e / jnp.maximum(jnp.sum(e, axis=axis, keepdims=True), TINY)


def alibi_slopes():
    h = jnp.arange(1, NSA_HEADS + 1, dtype=jnp.float32)
    return jnp.exp2(-8.0 * h / NSA_HEADS).reshape(NSA_GROUPS, NSA_HPG)


def split_cols(z, widths):
    return jnp.split(z, [int(c) for c in np.cumsum(widths)[:-1]], axis=-1)


def wkv7_scan(r, w, k, v, a, b):
    B, _, H, N = r.shape

    def step(state, inp):
        r_t, w_t, k_t, v_t, a_t, b_t = inp
        sa = jnp.einsum('bhvk,bhk->bhv', state, a_t)
        state = (state * w_t[:, :, None, :] + sa[..., None] * b_t[:, :, None, :]
                 + v_t[..., None] * k_t[:, :, None, :])
        return state, jnp.einsum('bhvk,bhk->bhv', state, r_t)

    tm = lambda t: jnp.moveaxis(t, 1, 0)
    state0 = jnp.zeros((B, H, N, N), jnp.float32)
    _, y = lax.scan(step, state0, (tm(r), tm(w), tm(k), tm(v), tm(a), tm(b)))
    return jnp.moveaxis(y, 0, 1)


def rwkv7_time_mix(z, mu, w0, w2, a0, a2, g2, k_k, k_a, r_k, ln_w, ln_b):
    B, S, _ = z.shape
    f32 = jnp.float32
    prev = jnp.pad(z, ((0, 0), (1, 0), (0, 0)))[:, :-1]
    z = z + (prev - z) * mu
    r, k, v, wd, ad, gd = split_cols(z, RWKV_SPLIT)
    w_log = -jax.nn.softplus(-(w0 + jnp.tanh(wd) @ w2)) - 0.5
    decay = jnp.exp(-jnp.exp(w_log.astype(f32)))
    a = jax.nn.sigmoid(a0 + ad @ a2)
    g = jax.nn.sigmoid(gd) @ g2
    heads = lambda t: t.reshape(B, S, RWKV_HEADS, RWKV_HEAD).astype(f32)
    kk = heads(k * k_k)
    kk = kk / jnp.maximum(jnp.sqrt(jnp.sum(kk * kk, axis=-1, keepdims=True)), 1e-12)
    k = k * (1.0 + (a - 1.0) * k_a)
    r_h, k_h, v_h, a_h = heads(r), heads(k), heads(v), heads(a)
    y = wkv7_scan(r_h, heads(decay), k_h, v_h, -kk, kk * a_h)
    mean = jnp.mean(y, axis=-1, keepdims=True)
    var = jnp.mean(jnp.square(y - mean), axis=-1, keepdims=True)
    y = ((y - mean) * lax.rsqrt(var + GN_EPS)).reshape(B, S, RWKV_DIM)
    y = y * ln_w.astype(f32) + ln_b.astype(f32)
    bonus = jnp.sum(r_h * k_h * r_k.astype(f32), axis=-1, keepdims=True) * v_h
    y = y + bonus.reshape(B, S, RWKV_DIM)
    return (y * g.astype(f32)).astype(z.dtype)


def nsa_attention(z, cmp_pos_k, cmp_w1_k, cmp_w2_k, cmp_pos_v, cmp_w1_v, cmp_w2_v):
    B, S, _ = z.shape
    f32 = jnp.float32
    G, HG, DH = NSA_GROUPS, NSA_HPG, NSA_HEAD
    q, kc, vc, ks, vs, kw, vw, gl = split_cols(z, NSA_SPLIT)
    q = (q.astype(f32) * DH ** -0.5).reshape(B, S, G, HG, DH).transpose(0, 2, 3, 1, 4)
    gates = jax.nn.sigmoid(gl.astype(f32)).reshape(B, S, G, HG, 3).transpose(0, 2, 3, 1, 4)
    kv = lambda t: t.astype(f32).reshape(B, S, G, DH).transpose(0, 2, 1, 3)

    n_cmp = (S - CMP_BLOCK) // CMP_STRIDE + 1
    cidx = np.arange(n_cmp)[:, None] * CMP_STRIDE + np.arange(CMP_BLOCK)[None, :]

    def compress(t, pos, w1, w2):
        blk = kv(t)[:, :, cidx] + pos.astype(f32)
        hid = jax.nn.gelu(blk.reshape(B, G, n_cmp, CMP_BLOCK * DH) @ w1.astype(f32))
        return hid @ w2.astype(f32)

    k_cmp = compress(kc, cmp_pos_k, cmp_w1_k, cmp_w2_k)
    v_cmp = compress(vc, cmp_pos_v, cmp_w1_v, cmp_w2_v)
    cmp_end = jnp.asarray(cidx[:, -1], jnp.int32)
    cmp_mid = jnp.asarray(cidx.mean(axis=1), jnp.float32)

    n_sel = S // SLC_BLOCK
    top = min(N_SELECT, n_sel)
    k_slc = kv(ks).reshape(B, G, n_sel, SLC_BLOCK, DH)
    v_slc = kv(vs).reshape(B, G, n_sel, SLC_BLOCK, DH)
    r1, r2 = SLC_BLOCK // CMP_STRIDE, CMP_BLOCK // CMP_STRIDE
    off = np.array([m - n for m in range(r1) for n in range(r2)])
    imap_np = np.arange(n_sel)[:, None] * r1 + off[None, :]
    imap_ok = jnp.asarray((imap_np >= 0) & (imap_np < n_cmp))
    imap = jnp.asarray(np.clip(imap_np, 0, n_cmp - 1), jnp.int32)
    bi = jnp.arange(B)[:, None, None, None]
    gi = jnp.arange(G)[None, :, None, None]

    k_win = jnp.pad(kv(kw), ((0, 0), (0, 0), (WINDOW, 0), (0, 0)))
    v_win = jnp.pad(kv(vw), ((0, 0), (0, 0), (WINDOW, 0), (0, 0)))

    slopes = alibi_slopes()[None, :, :, None, None]

    def block(i):
        t0 = i * Q_BLOCK
        qb = lax.dynamic_slice_in_dim(q, t0, Q_BLOCK, axis=3)
        gb = lax.dynamic_slice_in_dim(gates, t0, Q_BLOCK, axis=3)
        t = t0 + jnp.arange(Q_BLOCK, dtype=jnp.int32)
        tf = t.astype(f32)

        s_c = jnp.einsum('bghtd,bgnd->bghtn', qb, k_cmp) - slopes * (tf[:, None] - cmp_mid[None, :])
        p_c = masked_softmax(s_c, cmp_end[None, :] <= t[:, None], -1)
        o_c = jnp.einsum('bghtn,bgnd->bghtd', p_c, v_cmp)

        imp_c = jnp.sum(p_c, axis=2)
        imp = jnp.sum(jnp.where(imap_ok, imp_c[..., imap], 0.0), axis=-1)
        cur = (t // SLC_BLOCK)[:, None]
        jb = jnp.arange(n_sel, dtype=jnp.int32)[None, :]
        forced = (jb == 0) | (jb == cur) | (jb == cur - 1)
        score = jnp.where(jb <= cur, jnp.where(forced, FORCE, imp), NEG_INF)
        vals, idx = lax.top_k(score, top)
        kg = k_slc[bi, gi, idx]
        vg = v_slc[bi, gi, idx]
        pos = idx[..., None] * SLC_BLOCK + jnp.arange(SLC_BLOCK, dtype=jnp.int32)
        dist_s = t[:, None, None] - pos
        mask_s = ((vals > 0.5 * NEG_INF)[..., None] & (dist_s >= 0))[:, :, None]
        s_s = (jnp.einsum('bghtd,bgtnld->bghtnl', qb, kg)
               - slopes[..., None] * dist_s.astype(f32)[:, :, None])
        p_s = masked_softmax(s_s, mask_s, (-2, -1))
        o_s = jnp.einsum('bghtnl,bgtnld->bghtd', p_s, vg)

        kwb = lax.dynamic_slice_in_dim(k_win, t0, Q_BLOCK + WINDOW, axis=2)
        vwb = lax.dynamic_slice_in_dim(v_win, t0, Q_BLOCK + WINDOW, axis=2)
        spos = t0 - WINDOW + jnp.arange(Q_BLOCK + WINDOW, dtype=jnp.int32)
        dist_w = t[:, None] - spos[None, :]
        mask_w = (spos[None, :] >= 0) & (dist_w >= 0) & (dist_w < WINDOW)
        s_w = jnp.einsum('bghtd,bgsd->bghts', qb, kwb) - slopes * dist_w.astype(f32)
        p_w = masked_softmax(s_w, mask_w, -1)
        o_w = jnp.einsum('bghts,bgsd->bghtd', p_w, vwb)

        return gb[..., 0:1] * o_c + gb[..., 1:2] * o_s + gb[..., 2:3] * o_w

    out = lax.map(block, jnp.arange(S // Q_BLOCK, dtype=jnp.int32))
    out = out.transpose(1, 0, 4, 2, 3, 5).reshape(B, S, NSA_HEADS * DH)
    return out.astype(z.dtype)


def causal_dwconv(u, w, b):
    C = u.shape[-1]
    y = lax.conv_general_dilated(u, w[:, None, :].astype(u.dtype), window_strides=(1,),
                                 padding=[(CONV_W - 1, 0)],
                                 dimension_numbers=('NWC', 'WIO', 'NWC'),
                                 feature_group_count=C)
    return y + b


def setup_inputs(seed: int = 0) -> dict:
    key = jax.random.key(seed)
    ks = jax.random.split(key, 32)
    f32 = jnp.float32
    L = DEPTH
    nrm = lambda k, shape, scale: jax.random.normal(k, shape, f32) * scale
    gain = lambda k, n: 1.0 + 0.05 * jax.random.normal(k, (L, n), f32)
    return {
        "x": jax.random.normal(ks[0], (BATCH, SEQ, D_MODEL), f32),
        "norm_mix_pre": gain(ks[1], D_MODEL),
        "norm_mix_post": gain(ks[2], D_MODEL),
        "norm_ffn_pre": gain(ks[3], D_MODEL),
        "norm_ffn_post": gain(ks[4], D_MODEL),
        "w_in": nrm(ks[5], (L, D_MODEL, IN_COLS), D_MODEL ** -0.5),
        "rwkv_mu": jax.random.uniform(ks[6], (L, RWKV_COLS), f32),
        "rwkv_w0": jax.random.uniform(ks[7], (L, RWKV_DIM), f32, minval=-6.0, maxval=0.0),
        "rwkv_w2": nrm(ks[8], (L, DECAY_LORA, RWKV_DIM), DECAY_LORA ** -0.5),
        "rwkv_a0": nrm(ks[9], (L, RWKV_DIM), 0.1),
        "rwkv_a2": nrm(ks[10], (L, ICLR_LORA, RWKV_DIM), ICLR_LORA ** -0.5),
        "rwkv_g2": nrm(ks[11], (L, GATE_LORA, RWKV_DIM), GATE_LORA ** -0.5),
        "rwkv_k_k": 0.85 + nrm(ks[12], (L, RWKV_DIM), 0.05),
        "rwkv_k_a": 1.0 + nrm(ks[13], (L, RWKV_DIM), 0.05),
        "rwkv_r_k": nrm(ks[14], (L, RWKV_HEADS, RWKV_HEAD), 0.1),
        "rwkv_ln_w": gain(ks[15], RWKV_DIM),
        "rwkv_ln_b": nrm(ks[16], (L, RWKV_DIM), 0.02),
        "cmp_pos_k": nrm(ks[17], (L, CMP_BLOCK, NSA_HEAD), 0.1),
        "cmp_w1_k": nrm(ks[18], (L, CMP_BLOCK * NSA_HEAD, CMP_HIDDEN), (CMP_BLOCK * NSA_HEAD) ** -0.5),
        "cmp_w2_k": nrm(ks[19], (L, CMP_HIDDEN, NSA_HEAD), CMP_HIDDEN ** -0.5),
        "cmp_pos_v": nrm(ks[20], (L, CMP_BLOCK, NSA_HEAD), 0.1),
        "cmp_w1_v": nrm(ks[21], (L, CMP_BLOCK * NSA_HEAD, CMP_HIDDEN), (CMP_BLOCK * NSA_HEAD) ** -0.5),
        "cmp_w2_v": nrm(ks[22], (L, CMP_HIDDEN, NSA_HEAD), CMP_HIDDEN ** -0.5),
        "w_branch_rwkv": nrm(ks[23], (L, RWKV_DIM, D_MODEL), RWKV_DIM ** -0.5),
        "w_branch_nsa": nrm(ks[24], (L, NSA_HEADS * NSA_HEAD, D_MODEL), (NSA_HEADS * NSA_HEAD) ** -0.5),
        "w_out": nrm(ks[25], (L, D_MODEL, D_MODEL), D_MODEL ** -0.5),
        "ffn_up": nrm(ks[26], (L, D_MODEL, 2 * D_FF), D_MODEL ** -0.5),
        "ffn_conv_w": nrm(ks[27], (L, CONV_W, 2 * D_FF), CONV_W ** -0.5),
        "ffn_conv_b": nrm(ks[28], (L, 2 * D_FF), 0.02),
        "ffn_down": nrm(ks[29], (L, D_FF, D_MODEL), D_FF ** -0.5),
    }


def reference(x, norm_mix_pre, norm_mix_post, norm_ffn_pre, norm_ffn_post, w_in,
              rwkv_mu, rwkv_w0, rwkv_w2, rwkv_a0, rwkv_a2, rwkv_g2, rwkv_k_k, rwkv_k_a,
              rwkv_r_k, rwkv_ln_w, rwkv_ln_b,
              cmp_pos_k, cmp_w1_k, cmp_w2_k, cmp_pos_v, cmp_w1_v, cmp_w2_v,
              w_branch_rwkv, w_branch_nsa, w_out,
              ffn_up, ffn_conv_w, ffn_conv_b, ffn_down):
    for l in range(DEPTH):
        h = rms_norm(x, norm_mix_pre[l])
        z = h @ w_in[l]
        z_rwkv, z_nsa, z_gate = jnp.split(z, [RWKV_COLS, RWKV_COLS + NSA_COLS], axis=-1)
        o_a = rwkv7_time_mix(z_rwkv, rwkv_mu[l], rwkv_w0[l], rwkv_w2[l], rwkv_a0[l],
                             rwkv_a2[l], rwkv_g2[l], rwkv_k_k[l], rwkv_k_a[l],
                             rwkv_r_k[l], rwkv_ln_w[l], rwkv_ln_b[l])
        o_b = nsa_attention(z_nsa, cmp_pos_k[l], cmp_w1_k[l], cmp_w2_k[l],
                            cmp_pos_v[l], cmp_w1_v[l], cmp_w2_v[l])
        g_a, g_b = jnp.split(jax.nn.sigmoid(z_gate), 2, axis=-1)
        m = g_a * (o_a @ w_branch_rwkv[l]) + g_b * (o_b @ w_branch_nsa[l])
        x = x + rms_norm(m @ w_out[l], norm_mix_post[l])
        h = rms_norm(x, norm_ffn_pre[l])
        u = causal_dwconv(h @ ffn_up[l], ffn_conv_w[l], ffn_conv_b[l])
        u_gate, u_val = jnp.split(u, 2, axis=-1)
        x = x + rms_norm((jax.nn.silu(u_gate) * u_val) @ ffn_down[l], norm_ffn_post[l])
    return x
```

```python
import math
from contextlib import ExitStack
import numpy as np
import concourse.bass as bass
import concourse.mybir as mybir
from concourse.bass_utils import run_bass_kernel_spmd

F32 = mybir.dt.float32
BF16 = mybir.dt.bfloat16
AF = mybir.ActivationFunctionType
ALU = mybir.AluOpType
AX = mybir.AxisListType

D = 1024
NEGM = -30000.0
ROLL_LIMIT = 40000


class V:
    def __init__(self, t, ap):
        self.t = t
        self.ap = ap

    def __getitem__(self, idx):
        return V(self.t, self.ap[idx])

    def r(self, pat, **kw):
        return V(self.t, self.ap.rearrange(pat, **kw))

    def bc(self, shape):
        return V(self.t, self.ap.to_broadcast(list(shape)))

    def us(self, d):
        return V(self.t, self.ap.unsqueeze(d))


class T:
    def __init__(self, name, t, space):
        self.name = name
        self.t = t
        self.space = space
        self.writers = {}
        self.readers = {}

    def __getitem__(self, idx):
        return V(self, self.t[idx])

    @property
    def a(self):
        return V(self, self.t[:])


def _ap(x):
    return x.ap if isinstance(x, V) else x


class Sched:
    def __init__(self, nc, stack):
        self.nc = nc
        self.stack = stack
        self.sem_stack = stack
        self.dsems = []
        self.eng = {"pe": nc.tensor, "act": nc.scalar, "dve": nc.vector,
                    "pool": nc.gpsimd, "sp": nc.sync}
        self.sem = {}
        self.cnt = {}
        for k in self.eng:
            self.sem[k] = stack.enter_context(nc.semaphore("s_" + k))
            self.cnt[k] = 0
        self.waited = {k: {} for k in self.eng}
        self.key = {k: k for k in self.eng}
        self.epoch = {k: 0 for k in self.eng}
        self.prev = {k: None for k in self.eng}
        self.n_inst = 0
        self.uid = 0
        self.sb_bytes = 0

    def sbuf(self, shape, dtype, name=None):
        self.uid += 1
        name = (name or "sb") + str(self.uid)
        t = self.stack.enter_context(self.nc.sbuf_tensor(name, list(shape), dtype))
        self.sb_bytes += int(np.prod(shape[1:])) * (4 if dtype == F32 else 2)
        return T(name, t, "sbuf")

    def psum(self, shape, dtype, name=None):
        self.uid += 1
        name = (name or "ps") + str(self.uid)
        t = self.stack.enter_context(self.nc.psum_tensor(name, list(shape), dtype))
        return T(name, t, "psum")

    def dram(self, shape, dtype, name=None):
        self.uid += 1
        name = (name or "dr") + str(self.uid)
        t = self.nc.dram_tensor(name, list(shape), dtype, kind="Internal")
        return T(name, t.ap(), "dram")

    def ext(self, ap, name):
        return T(name, ap, "dram")

    def _wait(self, e, deps):
        w = self.waited[e]
        for key, (sem, val) in deps.items():
            if e == "pe" and key.split("#")[0] == "pe":
                continue
            if w.get(key, 0) < val:
                self.eng[e].wait_ge(sem, val)
                w[key] = val

    @staticmethod
    def _merge(d, key, sem, val):
        if key not in d or d[key][1] < val:
            d[key] = (sem, val)

    def _deps(self, reads, writes):
        deps = {}
        for t in reads:
            for k, (s, v) in t.writers.items():
                self._merge(deps, k, s, v)
        for t in writes:
            for k, (s, v) in t.writers.items():
                self._merge(deps, k, s, v)
            for k, (s, v) in t.readers.items():
                self._merge(deps, k, s, v)
        return deps

    def op(self, e, fn, reads, writes):
        reads = list({id(x.t): x.t for x in reads if isinstance(x, V)}.values())
        writes = list({id(x.t): x.t for x in writes if isinstance(x, V)}.values())
        if self.cnt[e] >= ROLL_LIMIT:
            self._roll(e)
        self._wait(e, self._deps(reads, writes))
        inst = fn(self.eng[e])
        self.cnt[e] += 1
        inst.then_inc(self.sem[e], 1)
        v = self.cnt[e]
        kk_ = self.key[e]
        for t in reads:
            self._merge(t.readers, kk_, self.sem[e], v)
        for t in writes:
            self._merge(t.writers, kk_, self.sem[e], v)
        self.n_inst += 1
        return inst

    def dma(self, q, out, in_, **kw):
        reads = [in_.t]
        writes = [out.t]
        self._wait(q, self._deps(reads, writes))
        sb = out.t if out.t.space == "sbuf" else in_.t
        if not hasattr(sb, "dsem"):
            self.uid += 1
            sb.dsem = {"name": sb.name, "val": 0,
                       "sem": self.sem_stack.enter_context(self.nc.semaphore(f"d{self.uid}"))}
            self.dsems.append(sb.dsem)
        sem = sb.dsem
        key = "dma_" + sem["name"]
        sem["val"] += 16
        inst = self.eng[q].dma_start(out=out.ap, in_=in_.ap, **kw)
        inst.then_inc(sem["sem"], 16)
        for t in reads:
            self._merge(t.readers, key, sem["sem"], sem["val"])
        for t in writes:
            self._merge(t.writers, key, sem["sem"], sem["val"])
        self.n_inst += 1
        return inst

    def _roll(self, e):
        self.prev[e] = (self.key[e], self.sem[e], self.cnt[e])
        self.epoch[e] += 1
        self.key[e] = f"{e}#{self.epoch[e]}"
        self.uid += 1
        self.sem[e] = self.sem_stack.enter_context(self.nc.semaphore(f"s_{e}_{self.uid}"))
        self.cnt[e] = 0

    def barrier(self):
        for e in self.eng:
            for o in self.eng:
                if o == e:
                    continue
                cands = [(self.key[o], self.sem[o], self.cnt[o])]
                if self.prev[o] is not None:
                    cands.append(self.prev[o])
                for (ko, so, co) in cands:
                    if co > 0 and self.waited[e].get(ko, 0) < co:
                        self.eng[e].wait_ge(so, co)
                        self.waited[e][ko] = co
            for d in self.dsems:
                key = "dma_" + d["name"]
                if d["val"] > 0 and self.waited[e].get(key, 0) < d["val"]:
                    self.eng[e].wait_ge(d["sem"], d["val"])
                    self.waited[e][key] = d["val"]

    def mm(self, out, lhsT, rhs, start=True, stop=True):
        return self.op("pe", lambda e: e.matmul(out.ap, lhsT=lhsT.ap, rhs=rhs.ap, start=start, stop=stop),
                       [lhsT, rhs], [out])

    def tr(self, out, in_, ident):
        return self.op("pe", lambda e: e.transpose(out=out.ap, in_=in_.ap, identity=ident.ap), [in_, ident], [out])

    def act(self, out, in_, func, scale=1.0, bias=0.0, accum=None, e="act"):
        kw = {}
        if accum is not None:
            kw["accum_out"] = accum.ap
        return self.op(e, lambda g: g.activation(out=out.ap, in_=in_.ap, func=func, scale=_ap(scale), bias=_ap(bias), **kw),
                       [in_, scale, bias], [out, accum])

    def tt(self, out, a, b, op, e="dve"):
        return self.op(e, lambda g: g.tensor_tensor(out=out.ap, in0=a.ap, in1=b.ap, op=op), [a, b], [out])

    def ts(self, out, a, s1, s2=None, op0=ALU.mult, op1=None, e="dve"):
        if op1 is None:
            return self.op(e, lambda g: g.tensor_scalar(out=out.ap, in0=a.ap, scalar1=_ap(s1), scalar2=None, op0=op0),
                           [a, s1], [out])
        return self.op(e, lambda g: g.tensor_scalar(out=out.ap, in0=a.ap, scalar1=_ap(s1), scalar2=_ap(s2), op0=op0, op1=op1),
                       [a, s1, s2], [out])

    def stt(self, out, a, s, b, op0, op1):
        return self.op("dve", lambda g: g.scalar_tensor_tensor(out=out.ap, in0=a.ap, scalar=_ap(s), in1=b.ap, op0=op0, op1=op1),
                       [a, s, b], [out])

    def cp(self, out, a, e="dve"):
        if e == "act":
            return self.act(out, a, AF.Copy)
        return self.op(e, lambda g: g.tensor_copy(out=out.ap, in_=a.ap), [a], [out])

    def ms(self, out, val, e="dve"):
        return self.op(e, lambda g: g.memset(out.ap, val), [], [out])

    def red(self, out, a, op=ALU.add):
        return self.op("dve", lambda g: g.tensor_reduce(out=out.ap, in_=a.ap, axis=AX.X, op=op), [a], [out])

    def recip(self, out, a):
        return self.op("dve", lambda g: g.reciprocal(out=out.ap, in_=a.ap), [a], [out])

    def asel(self, out, in_, pattern, cmp, fill, base, cm):
        return self.op("pool", lambda g: g.affine_select(out=out.ap, in_=in_.ap, pattern=pattern, compare_op=cmp,
                                                         fill=fill, base=base, channel_multiplier=cm), [in_], [out])


IN_NAMES = ["x", "norm_mix_pre", "norm_mix_post", "norm_ffn_pre", "norm_ffn_post", "w_in",
            "rwkv_mu", "rwkv_w0", "rwkv_w2", "rwkv_a0", "rwkv_a2", "rwkv_g2", "rwkv_k_k", "rwkv_k_a",
            "rwkv_r_k", "rwkv_ln_w", "rwkv_ln_b",
            "cmp_pos_k", "cmp_w1_k", "cmp_w2_k", "cmp_pos_v", "cmp_w1_v", "cmp_w2_v",
            "w_branch_rwkv", "w_branch_nsa", "w_out", "ffn_up", "ffn_conv_w", "ffn_conv_b", "ffn_down"]
IN_SHAPES = {
    "norm_mix_pre": [1, 1024], "norm_mix_post": [1, 1024], "norm_ffn_pre": [1, 1024], "norm_ffn_post": [1, 1024],
    "w_in": [1024, 8016], "rwkv_mu": [1, 3360], "rwkv_w0": [1, 1024], "rwkv_w2": [64, 1024], "rwkv_a0": [1, 1024],
    "rwkv_a2": [64, 1024], "rwkv_g2": [160, 1024], "rwkv_k_k": [1, 1024], "rwkv_k_a": [1, 1024], "rwkv_r_k": [1, 1024],
    "rwkv_ln_w": [1, 1024], "rwkv_ln_b": [1, 1024], "cmp_pos_k": [1, 2048], "cmp_w1_k": [2048, 128], "cmp_w2_k": [128, 64],
    "cmp_pos_v": [1, 2048], "cmp_w1_v": [2048, 128], "cmp_w2_v": [128, 64], "w_branch_rwkv": [1024, 1024],
    "w_branch_nsa": [1024, 1024], "w_out": [1024, 1024], "ffn_up": [1024, 5632], "ffn_conv_w": [3, 5632],
    "ffn_conv_b": [1, 5632], "ffn_down": [2816, 1024],
}
C_R, C_K, C_V, C_WD, C_AD, C_GD = 0, 1024, 2048, 3072, 3136, 3200
NS0 = 3360
C_Q, C_KC, C_VC, C_KS, C_VS, C_KW, C_VW, C_GL = NS0, NS0 + 1024, NS0 + 1280, NS0 + 1536, NS0 + 1792, NS0 + 2048, NS0 + 2304, NS0 + 2560
C_GA, C_GB = 5968, 6992
SLOPES = [2.0 ** (-0.5 * h) for h in range(1, 17)]
CDEC = math.exp(-0.5)


def build_l4(S_LEN, parts=(), ctx=None):
    NQ = S_LEN // 512
    fused = ctx is not None
    if not fused:
        nc = bass.Bass("TRN2", target_bir_lowering=False)
        ins = {}
        for n in IN_NAMES:
            shp = [S_LEN, D] if n == "x" else IN_SHAPES[n]
            ins[n] = nc.dram_tensor(n, shp, F32, kind="ExternalInput").ap()
        out_ap = nc.dram_tensor("out", [S_LEN, D], F32, kind="ExternalOutput").ap()
        oa_ap = nc.dram_tensor("oaT", [D, S_LEN], F32, kind="ExternalInput").ap()
        ob_ap = nc.dram_tensor("obT", [D, S_LEN], F32, kind="ExternalInput").ap()

    with ExitStack() as st:
        if not fused:
            S = Sched(nc, st)
            X = {n: S.ext(ins[n], n) for n in IN_NAMES}
            OUT = S.ext(out_ap, "out")
            OA = S.ext(oa_ap, "oaT")
            OB = S.ext(ob_ap, "obT")
        else:
            nc, S, X, OUT, OA, OB = ctx
            S.stack = st

        identF = S.sbuf([128, 128], F32, "identF")
        identB = S.sbuf([128, 128], BF16, "identB")
        S.ms(identF.a, 1.0, e="pool")
        S.asel(identF.a, identF.a, [[-1, 128]], ALU.is_equal, 0.0, 0, 1)
        S.cp(identB.a, identF.a)

        def bload(name, n, q="sp"):
            t = S.sbuf([128, n], F32, name)
            S.dma(q, t.a, V(X[name], X[name].t[0:1, :].to_broadcast([128, n])))
            return t

        def pload(name, kc, q="sp"):
            t = S.sbuf([128, kc], F32, name + "_p")
            src = X[name].t[0, :].rearrange("(kc p) -> p kc", p=128)
            S.dma(q, t.a, V(X[name], src), allow_slow_non_contiguous=True)
            return t

        g_pre = pload("norm_mix_pre", 8)
        g_fpre = pload("norm_ffn_pre", 8)
        g_post = bload("norm_mix_post", 1024)
        g_fpost = bload("norm_ffn_post", 1024)
        eps_t = S.sbuf([128, 1], F32, "eps")
        S.ms(eps_t.a, 1e-6)

        WinA = S.dram([128, 8, 8016], BF16, "WinA")
        WinB = S.dram([128, 8, 3360], BF16, "WinB")
        Wbra = S.dram([128, 8, 1024], BF16, "Wbra")
        Wbrb = S.dram([128, 8, 1024], BF16, "Wbrb")
        Wout = S.dram([128, 8, 1024], BF16, "Wout")
        Wup = S.dram([128, 8, 5632], BF16, "Wup")
        Wdn = S.dram([128, 22, 1024], BF16, "Wdn")

        NWK = 14
        WK = [S.sbuf([128, 1024], F32, f"wk{i}") for i in range(NWK)]
        PS = [S.psum([128, 512], F32, f"psb{i}") for i in range(7)]
        PSB = S.psum([128, 1024], BF16, "psB")
        stg_b = [S.sbuf([128, 8, 128], BF16, f"stgb{i}") for i in range(2)]

        mu_bc = WK[12]
        omu_bc = WK[13]
        step_i = [0]

        def prepass(src_name, ncols, kcs, dst, gain, colscale=None, c_src0=0):
            src = X[src_name]
            for c0 in range(0, ncols, 128):
                cw = min(128, ncols - c0)
                for k0 in range(0, kcs, 8):
                    kn = min(8, kcs - k0)
                    i = step_i[0]
                    step_i[0] += 1
                    sf = WK[i % 4]
                    sb_ = stg_b[i % 2]
                    sview = sf.a.r("p (k c) -> p k c", c=128)
                    srcap = src.t[k0 * 128:(k0 + kn) * 128, c_src0 + c0:c_src0 + c0 + cw].rearrange("(k p) c -> p k c", p=128)
                    S.dma("sp" if i % 2 == 0 else "act", sview[:, 0:kn, 0:cw], V(src, srcap))
                    for k in range(kn):
                        e = "dve" if k % 2 == 0 else "pool"
                        if gain is None:
                            S.cp(sb_[:, k, 0:cw], sview[:, k, 0:cw], e=e)
                        elif colscale is None:
                            S.ts(sb_[:, k, 0:cw], sview[:, k, 0:cw], gain[:, k0 + k:k0 + k + 1], op0=ALU.mult, e=e)
                        else:
                            S.stt(sb_[:, k, 0:cw], sview[:, k, 0:cw], gain[:, k0 + k:k0 + k + 1],
                                  colscale[:, c0:c0 + cw], ALU.mult, ALU.mult)
                    S.dma("pool", dst[:, k0:k0 + kn, c0:c0 + cw], sb_[:, 0:kn, 0:cw])

        if "rwkv" in parts:
            for cb in range(0, 3360, 1024):
                cbw = min(1024, 3360 - cb)
                S.dma("sp", mu_bc[:, 0:cbw], V(X["rwkv_mu"], X["rwkv_mu"].t[0:1, cb:cb + cbw].to_broadcast([128, cbw])))
                S.ts(omu_bc[:, 0:cbw], mu_bc[:, 0:cbw], -1.0, 1.0, ALU.mult, ALU.add)
                prepass("w_in", cbw, 8, V(WinA, WinA.t[:, :, cb:cb + cbw]), g_pre, omu_bc, c_src0=cb)
                prepass("w_in", cbw, 8, V(WinB, WinB.t[:, :, cb:cb + cbw]), g_pre, mu_bc, c_src0=cb)
        prepass("w_in", 8016 - C_GA, 8, V(WinA, WinA.t[:, :, C_GA:8016]), g_pre, None, c_src0=C_GA)
        prepass("w_branch_rwkv", 1024, 8, Wbra, None)
        prepass("w_branch_nsa", 1024, 8, Wbrb, None)
        prepass("w_out", 1024, 8, Wout, None)
        prepass("ffn_up", 5632, 8, Wup, g_fpre)
        prepass("ffn_down", 1024, 22, Wdn, None)

        xnT = S.sbuf([128, 8, 513], BF16, "xnT")
        S.ms(xnT[:, :, 0:1], 0.0)
        xn2T = xnT[:, :, 1:513]
        oaT = S.sbuf([128, 8, 512], BF16, "oaT")
        obT = S.sbuf([128, 8, 512], BF16, "obT")
        mT = S.sbuf([128, 8, 512], BF16, "mT")
        x1 = S.sbuf([128, 4, 1024], F32, "x1")
        ring = [S.sbuf([128, 8, 512], BF16, f"ring{i}") for i in range(4)]
        ring_i = [0]
        convw = S.sbuf([128, 44, 3], F32, "convw")
        convb = S.sbuf([128, 44], F32, "convb")
        for j in range(3):
            S.dma("sp", convw[:, :, j], V(X["ffn_conv_w"], X["ffn_conv_w"].t[j, :].rearrange("(c p) -> p c", p=128)),
                  allow_slow_non_contiguous=True)
        S.dma("sp", convb.a, V(X["ffn_conv_b"], X["ffn_conv_b"].t[0, :].rearrange("(c p) -> p c", p=128)),
              allow_slow_non_contiguous=True)
        ucarry = S.sbuf([128, 44, 2], F32, "ucarry")
        S.ms(ucarry.a, 0.0)
        actT = S.sbuf([128, 22, 256], BF16, "actT")
        small = S.sbuf([128, 64], F32, "small")
        xb = S.sbuf([128, 1024], BF16, "xb")

        def slab(src, c0, cw, kcs=8, k0=0):
            r = ring[ring_i[0] % len(ring)]
            ring_i[0] += 1
            q = ["sp", "act"][ring_i[0] % 2]
            S.dma(q, r[:, 0:kcs, 0:cw], src[:, k0:k0 + kcs, c0:c0 + cw])
            return r

        def rstd_of(xt, col):
            S.act(WK[11].a, xt, AF.Square, accum=small[:, col:col + 1])
            S.act(small[:, col + 1:col + 2], small[:, col:col + 1], AF.Sqrt, scale=1.0 / 1024, bias=1e-6)
            S.recip(small[:, col:col + 1], small[:, col + 1:col + 2])

        def to_T(dst, src_b, ti):
            psb = PSB
            for kc in range(8):
                S.tr(psb[:, kc * 128:(kc + 1) * 128], src_b[:, kc * 128:(kc + 1) * 128], identB.a)
            S.cp(dst, psb.a.r("p (k t) -> p k t", t=128), e="act")


        for Q in range(NQ):
            t0 = Q * 512
            for ti in range(4):
                xt = x1[:, ti, :]
                S.dma("sp", xt, V(X["x"], X["x"].t[t0 + ti * 128:t0 + (ti + 1) * 128, :]))
                rstd_of(xt, 0)
                S.ts(xb.a, xt, small[:, 0:1], op0=ALU.mult)
                to_T(xnT[:, :, 1 + ti * 128:1 + (ti + 1) * 128], xb, ti)

            if not fused:
                S.dma("pool", oaT.a, V(OA, OA.t[:, t0:t0 + 512].rearrange("(k p) t -> p k t", p=128)))
                S.dma("pool", obT.a, V(OB, OB.t[:, t0:t0 + 512].rearrange("(k p) t -> p k t", p=128)))
            else:
                for SRC, dstT in ((OA, oaT), (OB, obT)):
                    for ti in range(4):
                        S.dma("pool", xb.a, V(SRC, SRC.t[t0 + ti * 128:t0 + (ti + 1) * 128, :]))
                        to_T(dstT[:, :, ti * 128:(ti + 1) * 128], xb, ti)

            for half in range(2):
                wa = slab(Wbra, half * 512, 512)
                wga = slab(WinA, C_GA + half * 512, 512)
                wb = slab(Wbrb, half * 512, 512)
                wgb = slab(WinA, C_GB + half * 512, 512)
                for cc in range(4):
                    mc = half * 4 + cc
                    cs = slice(cc * 128, (cc + 1) * 128)
                    for kc in range(8):
                        S.mm(PS[0].a, wa[:, kc, cs], oaT[:, kc, :], kc == 0, kc == 7)
                    for kc in range(8):
                        S.mm(PS[1].a, wga[:, kc, cs], xnT[:, kc, 1:513], kc == 0, kc == 7)
                    for kc in range(8):
                        S.mm(PS[2].a, wb[:, kc, cs], obT[:, kc, :], kc == 0, kc == 7)
                    for kc in range(8):
                        S.mm(PS[3].a, wgb[:, kc, cs], xnT[:, kc, 1:513], kc == 0, kc == 7)
                    ga = WK[4][:, 0:512]
                    gb = WK[4][:, 512:1024]
                    S.act(ga, PS[1].a, AF.Sigmoid)
                    S.act(gb, PS[3].a, AF.Sigmoid)
                    S.tt(ga, ga, PS[0].a, ALU.mult)
                    S.tt(gb, gb, PS[2].a, ALU.mult)
                    S.tt(mT[:, mc, :], ga, gb, ALU.add)
            S.cp(xnT[:, :, 0:1], xnT[:, :, 512:513], e="pool")
            wo = [slab(Wout, h * 512, 512) for h in range(2)]
            for ti in range(4):
                ts_ = slice(ti * 128, (ti + 1) * 128)
                for h in range(2):
                    for kc in range(8):
                        S.mm(PS[h].a, mT[:, kc, ts_], wo[h][:, kc, :], kc == 0, kc == 7)
                y = WK[5]
                S.cp(y[:, 0:512], PS[0].a, e="act")
                S.cp(y[:, 512:1024], PS[1].a)
                rstd_of(y.a, 2)
                S.stt(y.a, y.a, small[:, 2:3], g_post.a, ALU.mult, ALU.mult)
                S.tt(x1[:, ti, :], x1[:, ti, :], y.a, ALU.add)
                rstd_of(x1[:, ti, :], 4)
                S.ts(xb.a, x1[:, ti, :], small[:, 4:5], op0=ALU.mult)
                to_T(xn2T[:, :, ti * 128:(ti + 1) * 128], xb, ti)
            for hf in range(2):
                fs = slice(hf * 256, (hf + 1) * 256)
                for cg in range(0, 22, 4):
                    ncg = min(4, 22 - cg)
                    wg = slab(Wup, cg * 128, ncg * 128)
                    wv = slab(Wup, 2816 + cg * 128, ncg * 128)
                    for c in range(ncg):
                        ch = cg + c
                        res = []
                        for which, wsl in ((0, wg), (1, wv)):
                            chan = ch + 22 * which
                            ps = PS[2 + which]
                            for kc in range(8):
                                S.mm(ps[:, 0:256], wsl[:, kc, c * 128:(c + 1) * 128], xn2T[:, kc, fs], kc == 0, kc == 7)
                            ub = WK[6 + which]
                            S.cp(ub[:, 0:2], ucarry[:, chan, :], e="pool")
                            S.cp(ub[:, 2:258], ps[:, 0:256], e="act")
                            S.cp(ucarry[:, chan, :], ub[:, 256:258], e="pool")
                            cv = WK[8 + which]
                            S.ts(cv[:, 0:256], ub[:, 2:258], convw[:, chan, 2:3], convb[:, chan:chan + 1], ALU.mult, ALU.add)
                            S.stt(cv[:, 0:256], ub[:, 1:257], convw[:, chan, 1:2], cv[:, 0:256], ALU.mult, ALU.add)
                            S.stt(cv[:, 0:256], ub[:, 0:256], convw[:, chan, 0:1], cv[:, 0:256], ALU.mult, ALU.add)
                            res.append(cv)
                        S.act(res[0][:, 256:512], res[0][:, 0:256], AF.Silu)
                        S.tt(actT[:, ch, :], res[0][:, 256:512], res[1][:, 0:256], ALU.mult)
                for oh in range(2):
                    for kg in range(0, 22, 8):
                        kn = min(8, 22 - kg)
                        wd_ = slab(Wdn, oh * 512, 512, kcs=kn, k0=kg)
                        for tl in range(2):
                            for k in range(kn):
                                S.mm(PS[4 + tl].a, actT[:, kg + k, tl * 128:(tl + 1) * 128], wd_[:, k, :],
                                     kg + k == 0, kg + k == 21)
                    for tl in range(2):
                        ti = hf * 2 + tl
                        y = WK[5 + tl]
                        S.cp(y[:, oh * 512:(oh + 1) * 512], PS[4 + tl].a, e="act" if tl == 0 else "dve")
                for tl in range(2):
                    ti = hf * 2 + tl
                    y = WK[5 + tl]
                    rstd_of(y.a, 6 + 2 * tl)
                    S.stt(y.a, y.a, small[:, 6 + 2 * tl:7 + 2 * tl], g_fpost.a, ALU.mult, ALU.mult)
                    S.tt(y.a, y.a, x1[:, ti, :], ALU.add)
                    S.dma("sp", V(OUT, OUT.t[t0 + ti * 128:t0 + (ti + 1) * 128, :]), y.a)

        S.barrier()
        print("L4 instructions:", S.n_inst)
    return nc


def build_l2(S_LEN, STAGE=99, ctx=None):
    NQ = S_LEN // 512
    fused = ctx is not None
    names = ["x", "norm_mix_pre", "w_in", "rwkv_mu", "rwkv_w0", "rwkv_w2", "rwkv_a0", "rwkv_a2", "rwkv_g2",
             "rwkv_k_k", "rwkv_k_a", "rwkv_r_k", "rwkv_ln_w", "rwkv_ln_b"]
    if not fused:
        nc = bass.Bass("TRN2", target_bir_lowering=False)
        ins = {}
        for n in names:
            shp = [S_LEN, D] if n == "x" else IN_SHAPES[n]
            ins[n] = nc.dram_tensor(n, shp, F32, kind="ExternalInput").ap()
        out_ap = nc.dram_tensor("oa", [S_LEN, D], F32, kind="ExternalOutput").ap()
    with ExitStack() as st:
        if not fused:
            S = Sched(nc, st)
            X = {n: S.ext(ins[n], n) for n in names}
            OUT = S.ext(out_ap, "oa")
        else:
            nc, S, X, OUT = ctx
            S.stack = st
        identF = S.sbuf([128, 128], F32, "identF")
        identB = S.sbuf([128, 128], BF16, "identB")
        S.ms(identF.a, 1.0, e="pool")
        S.asel(identF.a, identF.a, [[-1, 128]], ALU.is_equal, 0.0, 0, 1)
        S.cp(identB.a, identF.a)
        MUi = S.sbuf([128, 128], F32, "MUi")
        MUs = S.sbuf([128, 128], F32, "MUs")
        MLs = S.sbuf([128, 128], F32, "MLs")
        for m_, base, cm, pat in ((MUi, 0, -1, 1), (MUs, -1, -1, 1), (MLs, -1, 1, -1)):
            S.ms(m_.a, 1.0, e="pool")
            S.asel(m_.a, m_.a, [[pat, 128]], ALU.is_ge, 0.0, base, cm)
        S.ms(MUi[0:64, 64:128], 0.0, e="pool")
        S.ms(MUs[0:64, 64:128], 0.0, e="pool")
        S.ms(MLs[64:128, 0:64], 0.0, e="pool")
        Blk = S.sbuf([128, 128], F32, "Blk")
        S.ms(Blk.a, 0.0, e="pool")
        S.ms(Blk[0:64, 0:64], 1.0, e="pool")
        S.ms(Blk[64:128, 64:128], 1.0, e="pool")
        Elast = S.sbuf([128, 2, 64], F32, "Elast")
        S.ms(Elast.a, 0.0, e="pool")
        S.ms(Elast[32:64, 0, :], 1.0, e="pool")
        S.asel(Elast[32:64, 0, :], Elast[32:64, 0, :], [[0, 64]], ALU.is_equal, 0.0, -31, 1)
        S.ms(Elast[96:128, 1, :], 1.0, e="pool")
        S.asel(Elast[96:128, 1, :], Elast[96:128, 1, :], [[0, 64]], ALU.is_equal, 0.0, -31, 1)

        def bload(name, n=1024):
            t = S.sbuf([128, n], F32, name)
            S.dma("sp", t.a, V(X[name], X[name].t[0:1, :].to_broadcast([128, n])))
            return t

        g_pre = S.sbuf([128, 8], F32, "gpre")
        S.dma("sp", g_pre.a, V(X["norm_mix_pre"], X["norm_mix_pre"].t[0, :].rearrange("(kc p) -> p kc", p=128)),
              allow_slow_non_contiguous=True)
        w0b, a0b, kkb, kab, rkb, lnw, lnb = [bload(n) for n in
                                             ("rwkv_w0", "rwkv_a0", "rwkv_k_k", "rwkv_k_a", "rwkv_r_k", "rwkv_ln_w", "rwkv_ln_b")]
        w2 = S.sbuf([64, 1024], F32, "w2")
        a2 = S.sbuf([64, 1024], F32, "a2")
        g2 = S.sbuf([128, 2, 1024], F32, "g2")
        S.dma("sp", w2.a, V(X["rwkv_w2"], X["rwkv_w2"].t[:, :]))
        S.dma("sp", a2.a, V(X["rwkv_a2"], X["rwkv_a2"].t[:, :]))
        S.dma("sp", g2[:, 0, :], V(X["rwkv_g2"], X["rwkv_g2"].t[0:128, :]))
        S.dma("sp", g2[0:32, 1, :], V(X["rwkv_g2"], X["rwkv_g2"].t[128:160, :]))
        eps_t = S.sbuf([128, 1], F32, "eps")
        S.ms(eps_t.a, 1e-6)
        gneps = S.sbuf([128, 1], F32, "gneps")
        S.ms(gneps.a, 64e-5)

        WinA = S.dram([128, 8, 3360], BF16, "WinA")
        WinB = S.dram([128, 8, 3360], BF16, "WinB")
        WK = [S.sbuf([128, 1024], F32, f"wk{i}") for i in range(14)]
        PS = [S.psum([128, 512], F32, f"psb{i}") for i in range(7)]
        PSB = S.psum([128, 1024], BF16, "psB")
        ring = [S.sbuf([128, 8, 288], BF16, f"ring{i}") for i in range(3)]
        stg_b = [ring[0][:, :, 0:128], ring[1][:, :, 0:128]]
        mu_bc, omu_bc = WK[12], WK[13]
        step_i = [0]

        def prepass(ncols, dst, colscale, c_src0):
            src = X["w_in"]
            for c0 in range(0, ncols, 128):
                cw = min(128, ncols - c0)
                i = step_i[0]
                step_i[0] += 1
                sf = WK[i % 4]
                sb_ = stg_b[i % 2]
                sview = sf.a.r("p (k c) -> p k c", c=128)
                srcap = src.t[:, c_src0 + c0:c_src0 + c0 + cw].rearrange("(k p) c -> p k c", p=128)
                S.dma("sp" if i % 2 == 0 else "act", sview[:, :, 0:cw], V(src, srcap))
                for k in range(8):
                    S.stt(sb_[:, k, 0:cw], sview[:, k, 0:cw], g_pre[:, k:k + 1], colscale[:, c0:c0 + cw], ALU.mult, ALU.mult)
                S.dma("act" if i % 2 == 0 else "sp", dst[:, :, c0:c0 + cw], sb_[:, :, 0:cw])

        for cb in range(0, 3360, 1024):
            cbw = min(1024, 3360 - cb)
            S.dma("sp", mu_bc[:, 0:cbw], V(X["rwkv_mu"], X["rwkv_mu"].t[0:1, cb:cb + cbw].to_broadcast([128, cbw])))
            S.ts(omu_bc[:, 0:cbw], mu_bc[:, 0:cbw], -1.0, 1.0, ALU.mult, ALU.add)
            prepass(cbw, V(WinA, WinA.t[:, :, cb:cb + cbw]), omu_bc, cb)
            prepass(cbw, V(WinB, WinB.t[:, :, cb:cb + cbw]), mu_bc, cb)

        xnT = S.sbuf([128, 8, 513], BF16, "xnT")
        S.ms(xnT[:, :, 0:1], 0.0)
        ring_i = [0]
        small = S.sbuf([128, 128], F32, "small")
        xb = S.sbuf([128, 1024], BF16, "xb")
        thT = S.sbuf([64, 512], F32, "thT")
        adT = S.sbuf([64, 512], F32, "adT")
        sgT = S.sbuf([128, 2, 512], F32, "sgT")
        Sst = S.sbuf([64, 16, 64], F32, "Sst")
        S.ms(Sst.a, 0.0)
        XF = [S.sbuf([64, 8, 128], F32, f"xf{i}") for i in range(4)]
        Xm = S.sbuf([128, 8, 128], F32, "Xm")
        XTm = S.sbuf([128, 8, 128], F32, "XTm")
        Tm = S.sbuf([128, 8, 128], F32, "Tm")
        Mbr = S.sbuf([128, 8, 128], F32, "Mbr")
        Mkr = S.sbuf([128, 8, 128], F32, "Mkr")
        MakT = S.sbuf([128, 8, 128], F32, "MakT")
        ATt = S.sbuf([128, 8, 64], F32, "ATt")
        Gm = S.sbuf([64, 2, 8, 64], F32, "Gm")
        Hm = S.sbuf([128, 8, 64], F32, "Hm")
        RyC = S.sbuf([64, 2, 8, 128], F32, "RyC")
        pcd = S.sbuf([64, 8, 64], F32, "pcd")
        S.ms(RyC.a, 0.0)
        ATc = S.sbuf([128, 2, 8, 64], F32, "ATc")
        Hc = S.sbuf([128, 2, 8, 64], F32, "Hc")
        cmask = S.sbuf([128, 2], F32, "cmask")
        S.ms(cmask.a, 0.0)
        S.ms(cmask[0:64, 0:1], 1.0)
        S.ms(cmask[64:128, 1:2], 1.0)

        def slab(src, c0, cw):
            r = ring[ring_i[0] % len(ring)]
            ring_i[0] += 1
            q = ["sp", "act"][ring_i[0] % 2]
            S.dma(q, r[:, :, 0:cw], src[:, :, c0:c0 + cw])
            return r

        def h16(v):
            return v.r("p (h k) -> p h k", k=64)

        for Q in range(NQ):
            t0 = Q * 512
            for ti in range(4):
                xt = WK[0].a
                S.dma("sp", xt, V(X["x"], X["x"].t[t0 + ti * 128:t0 + (ti + 1) * 128, :]))
                S.act(WK[1].a, xt, AF.Square, accum=small[:, 0:1])
                S.act(small[:, 1:2], small[:, 0:1], AF.Sqrt, scale=1.0 / 1024, bias=1e-6)
                S.recip(small[:, 0:1], small[:, 1:2])
                S.ts(xb.a, xt, small[:, 0:1], op0=ALU.mult)
                for kc in range(8):
                    S.tr(PSB[:, kc * 128:(kc + 1) * 128], xb[:, kc * 128:(kc + 1) * 128], identB.a)
                S.cp(xnT[:, :, 1 + ti * 128:1 + (ti + 1) * 128], PSB.a.r("p (k t) -> p k t", t=128), e="act")
            sa = slab(WinA, C_WD, 288)
            sb2 = slab(WinB, C_WD, 288)
            for (m0, M, dst, fn) in ((0, 64, thT.a, AF.Tanh), (64, 64, adT.a, AF.Copy),
                                     (128, 128, sgT[:, 0, :], AF.Sigmoid), (256, 32, sgT[0:32, 1, :], AF.Sigmoid)):
                for kc in range(8):
                    S.mm(PS[0][0:M, :], sa[:, kc, m0:m0 + M], xnT[:, kc, 1:513], kc == 0, False)
                for kc in range(8):
                    S.mm(PS[0][0:M, :], sb2[:, kc, m0:m0 + M], xnT[:, kc, 0:512], False, kc == 7)
                S.act(dst, PS[0][0:M, :], fn)

            for ti in range(4):
                tsl = slice(ti * 128, (ti + 1) * 128)
                zr, zk, zv, sg, av, kk, kmod, bb, Lps, e2, At, Rt, Bt, Kt = [w.a for w in WK]
                for xi, dst in enumerate((zr, zk, zv)):
                    for qc in range(4):
                        c0 = xi * 1024 + qc * 256
                        sa_ = slab(WinA, c0, 256)
                        sb_ = slab(WinB, c0, 256)
                        ps = PS[qc % 2]
                        for kc in range(8):
                            S.mm(ps[:, 0:256], xnT[:, kc, 1 + ti * 128:1 + (ti + 1) * 128], sa_[:, kc, 0:256], kc == 0, False)
                        for kc in range(8):
                            S.mm(ps[:, 0:256], xnT[:, kc, ti * 128:(ti + 1) * 128], sb_[:, kc, 0:256], False, kc == 7)
                        S.cp(dst[:, qc * 256:(qc + 1) * 256], ps[:, 0:256], e="act" if qc % 2 == 0 else "dve")
                if STAGE < 2:
                    S.dma("sp", V(OUT, OUT.t[t0 + ti * 128:t0 + (ti + 1) * 128, :]), zr)
                    continue
                for h in range(2):
                    S.mm(PS[2 + h].a, thT[:, tsl], w2[:, h * 512:(h + 1) * 512])
                    S.tt(sg[:, h * 512:(h + 1) * 512], PS[2 + h].a, w0b[:, h * 512:(h + 1) * 512], ALU.add)
                S.act(sg, sg, AF.Sigmoid)
                for h in range(2):
                    S.mm(PS[2 + h].a, adT[:, tsl], a2[:, h * 512:(h + 1) * 512])
                    S.tt(av[:, h * 512:(h + 1) * 512], PS[2 + h].a, a0b[:, h * 512:(h + 1) * 512], ALU.add)
                S.act(av, av, AF.Sigmoid)
                for h in range(2):
                    S.mm(PS[2 + h].a, MUi.a, sg[:, h * 512:(h + 1) * 512])
                    S.cp(Lps[:, h * 512:(h + 1) * 512], PS[2 + h].a, e="act")
                    S.mm(PS[4 + h].a, Blk.a, sg[:, h * 512:(h + 1) * 512])
                S.tt(e2, Lps, sg, ALU.subtract)
                S.act(e2, e2, AF.Exp, scale=-CDEC)
                S.tt(kk, zk, kkb.a, ALU.mult)
                S.tt(bb, kk, kk, ALU.mult)
                S.red(small[:, 16:32], h16(bb))
                S.act(small[:, 16:32], small[:, 16:32], AF.Sqrt)
                S.ts(small[:, 16:32], small[:, 16:32], 1e-12, op0=ALU.max)
                S.recip(small[:, 32:48], small[:, 16:32])
                S.tt(h16(kk), h16(kk), small[:, 32:48].us(2).bc([128, 16, 64]), ALU.mult)
                S.stt(kmod, av, -1.0, kab.a, ALU.add, ALU.mult)
                S.stt(kmod, kmod, 1.0, zk, ALU.add, ALU.mult)
                S.tt(bb, kk, av, ALU.mult)
                S.stt(At, kk, -1.0, e2, ALU.mult, ALU.mult)
                eL = zk
                S.act(eL, Lps, AF.Exp, scale=-CDEC)
                S.tt(Rt, zr, eL, ALU.mult)
                S.act(e2, Lps, AF.Exp, scale=CDEC)
                S.tt(Bt, bb, e2, ALU.mult)
                S.tt(Kt, kmod, e2, ALU.mult)
                for h in range(2):
                    S.tt(e2[:, h * 512:(h + 1) * 512], PS[4 + h].a, Lps[:, h * 512:(h + 1) * 512], ALU.subtract)
                S.act(e2, e2, AF.Exp, scale=-CDEC)
                Bh, Kh = av, sg
                S.tt(Bh, bb, e2, ALU.mult)
                S.tt(Kh, kmod, e2, ALU.mult)
                if STAGE < 3:
                    S.dma("sp", V(OUT, OUT.t[t0 + ti * 128:t0 + (ti + 1) * 128, :]), Kt)
                    continue
                AtF, RtF, BtF, KtF = XF
                if STAGE < 4:
                    S.dma("sp", V(OUT, OUT.t[t0 + ti * 128:t0 + (ti + 1) * 128, :]), Kt)
                    continue
                yps = [PS[5], PS[6]]
                for hh in range(2):
                    for xi, src in enumerate((At, Rt, Bt, Kt)):
                        for q4 in range(2):
                            pb_ = PS[2 + q4]
                            for hq in range(4):
                                h = hh * 8 + q4 * 4 + hq
                                S.tr(pb_[0:64, hq * 128:(hq + 1) * 128], src[:, h * 64:(h + 1) * 64], identF.a)
                            S.cp(XF[xi][:, q4 * 4:(q4 + 1) * 4, :], pb_[0:64, :].r("p (a t) -> p a t", t=128),
                                 e="act" if q4 % 2 == 0 else "dve")

                    def fm(Xf, hl):
                        return Xf[:, hl, :]
                    def gram(dst, Lf, Rf, mask):
                        for g4 in range(2):
                            for hl4 in range(4):
                                hl = g4 * 4 + hl4
                                S.mm(PS[g4][:, hl4 * 128:(hl4 + 1) * 128], fm(Lf, hl), fm(Rf, hl))
                            S.tt(dst[:, g4 * 4:(g4 + 1) * 4, :], PS[g4].a.r("p (a t) -> p a t", t=128),
                                 mask.a.us(1).bc([128, 4, 128]), ALU.mult)
                    gram(Xm, BtF, AtF, MUs)
                    gram(XTm, AtF, BtF, MLs)
                    gram(MakT, AtF, KtF, MLs)
                    gram(Mbr, BtF, RtF, MUi)
                    gram(Mkr, KtF, RtF, MUi)
                    if STAGE < 5:
                        continue
                    S.tt(Tm.a, Xm.a, identF.a.us(1).bc([128, 8, 128]), ALU.add)
                    for it in range(5):
                        for g4 in range(2):
                            for hl4 in range(4):
                                hl = g4 * 4 + hl4
                                S.mm(PS[g4][:, hl4 * 128:(hl4 + 1) * 128], XTm[:, hl, :], Xm[:, hl, :])
                                S.mm(PS[2 + g4][:, hl4 * 128:(hl4 + 1) * 128], Xm[:, hl, :], XTm[:, hl, :])
                        for g4 in range(2):
                            S.cp(Xm[:, g4 * 4:(g4 + 1) * 4, :], PS[g4].a.r("p (a t) -> p a t", t=128), e="act")
                            S.cp(XTm[:, g4 * 4:(g4 + 1) * 4, :], PS[2 + g4].a.r("p (a t) -> p a t", t=128))
                        for g4 in range(2):
                            for hl4 in range(4):
                                hl = g4 * 4 + hl4
                                S.mm(PS[g4][:, hl4 * 128:(hl4 + 1) * 128], XTm[:, hl, :], Tm[:, hl, :])
                            S.tt(Tm[:, g4 * 4:(g4 + 1) * 4, :], Tm[:, g4 * 4:(g4 + 1) * 4, :],
                                 PS[g4].a.r("p (a t) -> p a t", t=128), ALU.add)
                    if STAGE < 6:
                        continue
                    for hl in range(8):
                        h = hh * 8 + hl
                        S.mm(PS[0][:, hl * 64:(hl + 1) * 64], Tm[:, hl, :], At[:, h * 64:(h + 1) * 64])
                    S.cp(ATt.a, PS[0].a.r("p (a k) -> p a k", k=64), e="act")
                    for g4 in range(2):
                        for hl4 in range(4):
                            hl = g4 * 4 + hl4
                            S.mm(PS[1 + g4][:, hl4 * 128:(hl4 + 1) * 128], Tm[:, hl, :], MakT[:, hl, :])
                    for g4 in range(2):
                        S.cp(Xm[:, g4 * 4:(g4 + 1) * 4, :], PS[1 + g4].a.r("p (a t) -> p a t", t=128), e="act" if g4 else "dve")
                    N1T = Xm
                    for g4 in range(2):
                        for hl4 in range(4):
                            hl = g4 * 4 + hl4
                            S.mm(PS[3 + g4][:, hl4 * 128:(hl4 + 1) * 128], N1T[:, hl, :], Mbr[:, hl, :])
                        S.tt(XTm[:, g4 * 4:(g4 + 1) * 4, :], Mkr[:, g4 * 4:(g4 + 1) * 4, :],
                             PS[3 + g4].a.r("p (a t) -> p a t", t=128), ALU.add)
                    Ny = XTm
                    for hl in range(8):
                        h = hh * 8 + hl
                        S.mm(PS[0][:, hl * 64:(hl + 1) * 64], N1T[:, hl, :], Bh[:, h * 64:(h + 1) * 64])
                    S.tt(Hm.a, PS[0].a.r("p (a k) -> p a k", k=64), Kh[:, hh * 512:(hh + 1) * 512].r("p (a k) -> p a k", k=64), ALU.add)
                    for c in range(2):
                        S.ts(ATc[:, c, :, :], ATt.a, cmask[:, c:c + 1], op0=ALU.mult, e="pool")
                        S.ts(Hc[:, c, :, :], Hm.a, cmask[:, c:c + 1], op0=ALU.mult, e="pool")
                    for g4 in range(2):
                        for hl4 in range(4):
                            hl = g4 * 4 + hl4
                            h = hh * 8 + hl
                            o = PS[3 + g4][0:64, hl4 * 128:(hl4 + 1) * 128]
                            S.mm(o, ATt[:, hl, :], Mbr[:, hl, :], True, False)
                            S.mm(o, Rt[:, h * 64:(h + 1) * 64], identF.a, False, True)
                        pv = PS[3 + g4][0:64, :].r("p (a t) -> p a t", t=128)
                        S.cp(RyC[:, 0, g4 * 4:(g4 + 1) * 4, 0:64], pv[:, :, 0:64], e="act")
                        S.cp(RyC[:, 1, g4 * 4:(g4 + 1) * 4, 64:128], pv[:, :, 64:128])
                    for c in range(2):
                        for hl in range(8):
                            h = hh * 8 + hl
                            S.mm(PS[1][0:64, hl * 64:(hl + 1) * 64], ATc[:, c, hl, :], Bh[:, h * 64:(h + 1) * 64])
                        S.mm(PS[2][0:64, :], Elast[:, c, :], eL[:, hh * 512:(hh + 1) * 512])
                        S.tt(pcd.a, PS[2][0:64, :].r("p (a k) -> p a k", k=64), identF[0:64, 0:64].us(1).bc([64, 8, 64]), ALU.mult)
                        S.tt(Gm[:, c, :, :], PS[1][0:64, :].r("p (a k) -> p a k", k=64), pcd.a, ALU.add)
                    if STAGE < 7:
                        continue
                    yp = yps[hh]
                    for hl in range(8):
                        h = hh * 8 + hl
                        S.mm(yp[:, hl * 64:(hl + 1) * 64], Ny[:, hl, :], zv[:, h * 64:(h + 1) * 64], hl == 0, False)
                    for c in range(2):
                        for hl in range(8):
                            h = hh * 8 + hl
                            S.mm(yp[:, hl * 64:(hl + 1) * 64], RyC[:, c, hl, :], Sst[:, h, :], False, c == 1 and hl == 7)
                        for hl in range(8):
                            h = hh * 8 + hl
                            o = PS[0][0:64, hl * 64:(hl + 1) * 64]
                            S.mm(o, Gm[:, c, hl, :], Sst[:, h, :], True, False)
                            S.mm(o, Hc[:, c, hl, :], zv[:, h * 64:(h + 1) * 64], False, True)
                        S.cp(Sst[:, hh * 8:(hh + 1) * 8, :], PS[0][0:64, :].r("p (a k) -> p a k", k=64), e="act")
                if STAGE < 8:
                    S.dma("sp", V(OUT, OUT.t[t0 + ti * 128:t0 + (ti + 1) * 128, :]), Mkr.a.r("p a t -> p (a t)"))
                    continue
                yv, t1, t2 = kk, bb, Lps
                S.cp(yv[:, 0:512], yps[0].a, e="act")
                S.cp(yv[:, 512:1024], yps[1].a)
                S.red(small[:, 48:64], h16(yv))
                S.ts(small[:, 48:64], small[:, 48:64], 1.0 / 64, op0=ALU.mult)
                S.tt(h16(yv), h16(yv), small[:, 48:64].us(2).bc([128, 16, 64]), ALU.subtract)
                S.tt(t1, yv, yv, ALU.mult)
                S.red(small[:, 64:80], h16(t1))
                S.act(small[:, 64:80], small[:, 64:80], AF.Sqrt, scale=1.0 / 64, bias=64e-5)
                S.recip(small[:, 80:96], small[:, 64:80])
                S.tt(h16(yv), h16(yv), small[:, 80:96].us(2).bc([128, 16, 64]), ALU.mult)
                S.tt(yv, yv, lnw.a, ALU.mult)
                S.tt(yv, yv, lnb.a, ALU.add)
                S.tt(t1, zr, kmod, ALU.mult)
                S.tt(t1, t1, rkb.a, ALU.mult)
                S.red(small[:, 96:112], h16(t1))
                S.tt(h16(t1), h16(zv), small[:, 96:112].us(2).bc([128, 16, 64]), ALU.mult)
                S.tt(yv, yv, t1, ALU.add)
                for h in range(2):
                    S.mm(PS[2 + h].a, sgT[:, 0, tsl], g2[:, 0, h * 512:(h + 1) * 512], True, False)
                    S.mm(PS[2 + h].a, sgT[0:32, 1, tsl], g2[0:32, 1, h * 512:(h + 1) * 512], False, True)
                    S.tt(t2[:, h * 512:(h + 1) * 512], yv[:, h * 512:(h + 1) * 512], PS[2 + h].a, ALU.mult)
                S.dma("sp", V(OUT, OUT.t[t0 + ti * 128:t0 + (ti + 1) * 128, :]), t2)
            S.cp(xnT[:, :, 0:1], xnT[:, :, 512:513], e="pool")
        S.barrier()
        print("L2 instructions:", S.n_inst)
    return nc


def build_l3(S_LEN, STG=99, ctx=None):
    NQ = S_LEN // 512
    fused = ctx is not None
    names = ["x", "norm_mix_pre", "w_in", "cmp_pos_k", "cmp_w1_k", "cmp_w2_k", "cmp_pos_v", "cmp_w1_v", "cmp_w2_v"]
    if not fused:
        nc = bass.Bass("TRN2", target_bir_lowering=False)
        ins = {}
        for n in names:
            shp = [S_LEN, D] if n == "x" else IN_SHAPES[n]
            ins[n] = nc.dram_tensor(n, shp, F32, kind="ExternalInput").ap()
        out_ap = nc.dram_tensor("ob", [S_LEN, D], F32, kind="ExternalOutput").ap()
    with ExitStack() as st:
        if not fused:
            S = Sched(nc, st)
            X = {n: S.ext(ins[n], n) for n in names}
            OUT = S.ext(out_ap, "ob")
        else:
            nc, S, X, OUT = ctx
            S.stack = st
        identF = S.sbuf([128, 128], F32, "identF")
        identB = S.sbuf([128, 128], BF16, "identB")
        S.ms(identF.a, 1.0, e="pool")
        S.asel(identF.a, identF.a, [[-1, 128]], ALU.is_equal, 0.0, 0, 1)
        S.cp(identB.a, identF.a)
        g_pre = S.sbuf([128, 8], F32, "gpre")
        S.dma("sp", g_pre.a, V(X["norm_mix_pre"], X["norm_mix_pre"].t[0, :].rearrange("(kc p) -> p kc", p=128)),
              allow_slow_non_contiguous=True)
        eps_t = S.sbuf([128, 1], F32, "eps")
        S.ms(eps_t.a, 1e-6)
        WK = [S.sbuf([128, 1024], F32, f"wk{i}") for i in range(4)]
        PS = [S.psum([128, 512], F32, f"psb{i}") for i in range(7)]
        PSB = S.psum([128, 1024], BF16, "psB")
        stg_b = [S.sbuf([128, 8, 128], BF16, f"stgb{i}") for i in range(2)]
        NC_ = 2608
        WinA = S.dram([128, 8, NC_], BF16, "WinA")
        for i, c0 in enumerate(range(0, NC_, 128)):
            cw = min(128, NC_ - c0)
            sf = WK[i % 4]
            sb_ = stg_b[i % 2]
            sview = sf.a.r("p (k c) -> p k c", c=128)
            srcap = X["w_in"].t[:, NS0 + c0:NS0 + c0 + cw].rearrange("(k p) c -> p k c", p=128)
            S.dma("sp" if i % 2 == 0 else "act", sview[:, :, 0:cw], V(X["w_in"], srcap))
            for k in range(8):
                S.ts(sb_[:, k, 0:cw], sview[:, k, 0:cw], g_pre[:, k:k + 1], op0=ALU.mult, e="dve" if k % 2 else "pool")
            S.dma("pool", WinA[:, :, c0:c0 + cw], sb_[:, :, 0:cw])
        KS = S.dram([67, 4, S_LEN], BF16, "KS")
        VS = S.dram([128, S_LEN // 128, 4, 65], BF16, "VS")

        Emat = S.sbuf([128, 64, 128], BF16, "Emat")
        S.ms(Emat.a, 1.0, e="pool")
        S.asel(Emat.a.r("p i (a b) -> p i a b", b=64), Emat.a.r("p i (a b) -> p i a b", b=64),
               [[-2, 64], [-1, 2], [0, 64]], ALU.is_equal, 0.0, 0, 1)
        NEGcur = S.sbuf([128, 4, 512], BF16, "NEGcur")
        NEGprev = S.sbuf([128, 4, 512], BF16, "NEGprev")
        NEGcmp = S.sbuf([128, 4, 512], BF16, "NEGcmp")
        S.ms(NEGcur.a, 0.0, e="pool")
        S.ms(NEGprev.a, 0.0, e="pool")
        S.ms(NEGcmp.a, 0.0, e="pool")
        for a in range(4):
            S.asel(NEGcur[:, a, :], NEGcur[:, a, :], [[1, 512]], ALU.is_ge, NEGM, -128 * a, -1)
            S.asel(NEGprev[:, a, :], NEGprev[:, a, :], [[-1, 512]], ALU.is_ge, NEGM, 128 * a - 1, 1)
            S.asel(NEGcmp[:, a, :], NEGcmp[:, a, :], [[1, 512]], ALU.is_ge, NEGM, 512 * a - 15, -16)
        Amap = S.sbuf([128, 4, 128], BF16, "Amap")
        A2 = S.sbuf([128, 4, 128], BF16, "A2")
        S.ms(Amap.a, 1.0, e="pool")
        S.ms(A2.a, 1.0, e="pool")
        for N in range(4):
            S.asel(Amap[:, N, :], Amap[:, N, :], [[-4, 128]], ALU.is_ge, 0.0, 128 * N - 1 + 1, 1)
            S.asel(Amap[:, N, :], Amap[:, N, :], [[4, 128]], ALU.is_ge, 0.0, 3 - (128 * N - 1), -1)
            S.asel(A2[:, N, :], A2[:, N, :], [[-4, 128]], ALU.is_ge, 0.0, 128 * N - 1, 1)
            S.asel(A2[:, N, :], A2[:, N, :], [[4, 128]], ALU.is_ge, 0.0, 2 - (128 * N - 1), -1)
        S.tt(Amap.a, Amap.a, A2.a, ALU.add, e="pool")
        pd_i = S.sbuf([128, 67], mybir.dt.int32, "pdi")
        pd = S.sbuf([128, 67], F32, "pd")
        S.op("pool", lambda g: g.iota(pd_i.t[:], pattern=[[-128, 67]], base=384, channel_multiplier=1), [], [pd_i.a])
        S.cp(pd.a, pd_i.a)
        biasSel = S.sbuf([128, 16, 67], F32, "biasSel")
        pc_i = S.sbuf([128, 16], mybir.dt.int32, "pci")
        pcf = S.sbuf([128, 16], F32, "pcf")
        S.op("pool", lambda g: g.iota(pc_i.t[:], pattern=[[-512, 16]], base=0, channel_multiplier=16), [], [pc_i.a])
        S.cp(pcf.a, pc_i.a)
        biasCmp = S.sbuf([128, 16, 16], F32, "biasCmp")
        biasCmp0 = S.sbuf([128, 16, 16], F32, "biasCmp0")
        for h in range(16):
            S.ts(biasSel[:, h, :], pd.a, SLOPES[h], op0=ALU.mult)
            S.ts(biasCmp[:, h, :], pcf.a, -0.5, SLOPES[h], ALU.add, ALU.mult)
        S.cp(biasCmp0.a, biasCmp.a)
        S.ms(biasCmp0[0:1, :, :], NEGM)
        tiebrk = S.sbuf([128, 128], F32, "tiebrk")
        tb_i = S.sbuf([128, 128], mybir.dt.int32, "tbi")
        S.op("pool", lambda g: g.iota(tb_i.t[:], pattern=[[1, 128]], base=0, channel_multiplier=0), [], [tb_i.a])
        S.cp(tiebrk.a, tb_i.a)
        S.ts(tiebrk.a, tiebrk.a, -1e-30, op0=ALU.mult)
        Qaug = S.sbuf([67, 16, 512], BF16, "Qaug")
        fi = S.sbuf([67, 512], mybir.dt.int32, "fi")
        ff = S.sbuf([67, 512], F32, "ff")
        S.op("pool", lambda g: g.iota(fi.t[:], pattern=[[1, 512]], base=0, channel_multiplier=0), [], [fi.a])
        S.cp(ff.a, fi.a)
        r0 = S.sbuf([67, 512], F32, "r0")
        r1 = S.sbuf([67, 512], F32, "r1")
        qs = S.sbuf([67, 3, 512], BF16, "qs")
        for h in range(16):
            S.ts(r0[64:65, :], ff[64:65, :], -SLOPES[h], op0=ALU.mult)
            S.cp(qs[64:65, 0, :], r0[64:65, :])
            S.tt(r1[64:65, :], r0[64:65, :], qs[64:65, 0, :], ALU.subtract)
            S.cp(qs[64:65, 1, :], r1[64:65, :])
            S.tt(r0[64:65, :], r1[64:65, :], qs[64:65, 1, :], ALU.subtract)
            S.cp(qs[64:65, 2, :], r0[64:65, :])
            for j in range(3):
                S.dma("sp", Qaug[64 + j:65 + j, h, :], qs[64:65, j, :])

        def aug_ones(tile_v):
            S.ms(tile_v, 1.0)

        KSst = S.sbuf([67, 4, 512], BF16, "KSst")
        aug_ones(KSst[64:67, :, :])
        kwT = S.sbuf([67, 4, 2, 512], BF16, "kwT")
        aug_ones(kwT[64:67, :, :, :])
        kcmpT = S.sbuf([67, 4, 512], BF16, "kcmpT")
        S.ms(kcmpT.a, 0.0)
        aug_ones(kcmpT[64:67, :, :])
        VSst = S.sbuf([128, 4, 4, 65], BF16, "VSst")
        S.ms(VSst.a, 1.0)
        vwA = S.sbuf([128, 2, 4, 4, 65], BF16, "vwA")
        S.ms(vwA.a, 1.0)
        vcmpA = S.sbuf([128, 4, 4, 65], BF16, "vcmpA")
        S.ms(vcmpA.a, 0.0)
        S.ms(vcmpA[:, :, :, 64:65], 1.0)
        kcT = S.sbuf([64, 4, 528], BF16, "kcT")
        vcT = S.sbuf([64, 4, 528], BF16, "vcT")
        S.ms(kcT.a, 0.0)
        S.ms(vcT.a, 0.0)
        W1c = {}
        W2c = {}
        pbias = {}
        for kv in ("k", "v"):
            w1 = S.sbuf([64, 32, 128], BF16, "W1" + kv)
            S.dma("pool", w1.a, V(X["cmp_w1_" + kv], X["cmp_w1_" + kv].t[:, :].rearrange("(l d) h -> d l h", d=64)))
            W1c[kv] = w1
            w2 = S.sbuf([128, 64], BF16, "W2" + kv)
            S.dma("pool", w2.a, V(X["cmp_w2_" + kv], X["cmp_w2_" + kv].t[:, :]))
            W2c[kv] = w2
            w1f = WK[0].a.r("p (c h) -> p c h", h=128)[:, 0:16 // 2, :]
            pb = S.sbuf([128, 1], F32, "pb" + kv)
            posT = S.sbuf([128, 16], F32, "posT" + kv)
            S.dma("sp", posT.a, V(X["cmp_pos_" + kv], X["cmp_pos_" + kv].t[0, :].rearrange("(c p) -> p c", p=128)),
                  allow_slow_non_contiguous=True)
            for half in range(2):
                S.dma("sp", w1f, V(X["cmp_w1_" + kv], X["cmp_w1_" + kv].t[half * 1024:(half + 1) * 1024, :].rearrange("(c p) h -> p c h", p=128)))
                for c in range(8):
                    S.mm(PS[0][:, 0:1], w1f[:, c, :], posT[:, half * 8 + c:half * 8 + c + 1], half == 0 and c == 0, half == 1 and c == 7)
            S.cp(pb.a, PS[0][:, 0:1])
            pbias[kv] = pb

        xnT = S.sbuf([128, 8, 512], BF16, "xnT")
        ring = [S.sbuf([128, 8, 512], BF16, f"ring{i}") for i in range(3)]
        ring_i = [0]
        small = S.sbuf([128, 64], F32, "small")
        xb = S.sbuf([128, 1024], BF16, "xb")
        gates = S.sbuf([128, 4, 48], F32, "gates")
        obuf = S.sbuf([128, 4, 1024], F32, "obuf")
        NBT = S.sbuf([128, 4, 512], BF16, "NBT")
        Kc = [S.sbuf([67, 512], BF16, f"Kc{i}") for i in range(2)]
        Vc = [S.sbuf([128, 4, 65], BF16, f"Vc{i}") for i in range(2)]
        Pt = [S.sbuf([128, 512], BF16, f"Pt{i}") for i in range(2)]
        Sb = [S.sbuf([128, 512], F32, f"Sb{i}") for i in range(2)]
        p_i = [0]
        hid = S.sbuf([128, 4, 32], F32, "hid")
        hidpad = S.sbuf([128, 128], BF16, "hidpad")
        S.ms(hidpad.a, 0.0)
        sc = S.sbuf([128, 128], F32, "sc")
        sc2 = S.sbuf([128, 128], F32, "sc2")
        mx = S.sbuf([128, 16], F32, "mx")
        rs = S.sbuf([128, 64], F32, "rs")

        def slab(c0, cw):
            r = ring[ring_i[0] % len(ring)]
            ring_i[0] += 1
            q = ["sp", "act"][ring_i[0] % 2]
            S.dma(q, r[:, :, 0:cw], WinA[:, :, c0 - NS0:c0 - NS0 + cw])
            return r

        def attn_tile(h, lhsT, masks, bias_ap, vrhs, acc, first, last, imp=None, imp_rhs=None):
            ps = PS[p_i[0] % 2]
            pt = Pt[p_i[0] % 2]
            p_i[0] += 1
            S.mm(ps.a, lhsT, Qaug[:, h, :], True, len(masks) == 0)
            for mi, (ml, mr) in enumerate(masks):
                S.mm(ps.a, ml, mr, False, mi == len(masks) - 1)
            sb_ = Sb[p_i[0] % 2]
            S.ts(sb_.a, ps.a, bias_ap, op0=ALU.add)
            S.act(pt.a, sb_.a, AF.Exp)
            for s in range(4):
                S.mm(acc[:, s * 65:(s + 1) * 65], pt[:, s * 128:(s + 1) * 128], vrhs, first and s == 0, last and s == 3)
            if imp is not None:
                for s in range(4):
                    S.mm(imp[:, s * 128:(s + 1) * 128], pt[:, s * 128:(s + 1) * 128], imp_rhs, first and s == 0, last and s == 3)

        def finish(h, br, acc, first_branch):
            accv = acc[:, 0:260].r("p (s c) -> p s c", c=65)
            S.ts(rs[:, 0:4], accv[:, :, 64], 1e-30, op0=ALU.max)
            S.recip(rs[:, 4:8], rs[:, 0:4])
            S.tt(rs[:, 8:12], rs[:, 4:8], gates[:, :, 3 * h + br], ALU.mult)
            for s in range(4):
                o = obuf[:, s, h * 64:(h + 1) * 64]
                if first_branch:
                    S.ts(o, accv[:, s, 0:64], rs[:, 8 + s:9 + s], op0=ALU.mult)
                else:
                    S.stt(o, accv[:, s, 0:64], rs[:, 8 + s:9 + s], o, ALU.mult, ALU.add)

        for Q in range(NQ):
            t0 = Q * 512
            par = Q % 2
            for ti in range(4):
                xt = WK[0].a
                S.dma("sp", xt, V(X["x"], X["x"].t[t0 + ti * 128:t0 + (ti + 1) * 128, :]))
                S.act(WK[1].a, xt, AF.Square, accum=small[:, 0:1])
                S.act(small[:, 1:2], small[:, 0:1], AF.Sqrt, scale=1.0 / 1024, bias=1e-6)
                S.recip(small[:, 0:1], small[:, 1:2])
                S.ts(xb.a, xt, small[:, 0:1], op0=ALU.mult)
                for kc in range(8):
                    S.tr(PSB[:, kc * 128:(kc + 1) * 128], xb[:, kc * 128:(kc + 1) * 128], identB.a)
                S.cp(xnT[:, :, ti * 128:(ti + 1) * 128], PSB.a.r("p (k t) -> p k t", t=128), e="act")
            S.cp(kcT[:, :, 0:16], kcT[:, :, 512:528], e="pool")
            S.cp(vcT[:, :, 0:16], vcT[:, :, 512:528], e="pool")

            def fm_proj(c0, M, dst, scale=1.0):
                sl_ = slab(c0, M)
                for kc in range(8):
                    S.mm(PS[0][0:M, :], sl_[:, kc, 0:M], xnT[:, kc, :], kc == 0, kc == 7)
                S.act(dst, PS[0][0:M, :], AF.Copy, scale=scale)

            for h in range(16):
                fm_proj(C_Q + h * 64, 64, Qaug[0:64, h, :], 0.125)
            for g in range(4):
                fm_proj(C_KS + g * 64, 64, KSst[0:64, g, :])
                fm_proj(C_KW + g * 64, 64, kwT[0:64, g, par, :])
                fm_proj(C_KC + g * 64, 64, kcT[:, g, 16:528])
                fm_proj(C_VC + g * 64, 64, vcT[:, g, 16:528])
            S.dma("sp", KS[:, :, t0:t0 + 512], KSst.a)
            s_vs = slab(C_VS, 256)
            s_vw = slab(C_VW, 304)
            for ti in range(4):
                tsl = slice(ti * 128, (ti + 1) * 128)
                for kc in range(8):
                    S.mm(PS[0][:, 0:256], xnT[:, kc, tsl], s_vs[:, kc, 0:256], kc == 0, kc == 7)
                S.cp(VSst[:, ti, :, 0:64], PS[0][:, 0:256].r("p (g d) -> p g d", d=64), e="act")
                for kc in range(8):
                    S.mm(PS[1][:, 0:304], xnT[:, kc, tsl], s_vw[:, kc, 0:304], kc == 0, kc == 7)
                S.cp(vwA[:, par, ti, :, 0:64], PS[1][:, 0:256].r("p (g d) -> p g d", d=64))
                S.act(gates[:, ti, :], PS[1][:, 256:304], AF.Sigmoid)
            S.dma("act", VS[:, Q * 4:(Q + 1) * 4, :, :], VSst.a)

            if STG < 1:
                continue
            i0 = 1 if Q == 0 else 0
            ni = 32 - i0
            for kv, src in (("k", kcT), ("v", vcT)):
                for g in range(4):
                    for l in range(32):
                        S.mm(PS[0][:, 0:ni], W1c[kv][:, l, :], src[:, g, l + 16 * i0:l + 16 * 31 + 1:16], l == 0, l == 31)
                    xs = hid[:, 0, 0:ni]
                    tq = hid[:, 1, 0:ni]
                    S.ts(xs, PS[0][:, 0:ni], pbias[kv][:, 0:1], op0=ALU.add)
                    S.tt(tq, xs, xs, ALU.mult)
                    S.ts(tq, tq, 0.044715, 1.0, ALU.mult, ALU.add)
                    S.tt(tq, tq, xs, ALU.mult)
                    S.act(tq, tq, AF.Sigmoid, scale=1.5957691216057308)
                    pr = 32 * (Q % 4)
                    S.tt(hidpad[:, pr + i0:pr + 32], xs, tq, ALU.mult)
                    if kv == "k":
                        S.mm(PS[1][0:64, 0:32], W2c[kv].a, hidpad[:, pr:pr + 32])
                        S.cp(kcmpT[0:64, g, 32 * Q:32 * Q + 32], PS[1][0:64, 0:32], e="act")
                    else:
                        S.mm(PS[1][:, 0:64], hidpad.a, W2c[kv].a)
                        S.cp(vcmpA[pr:pr + 32, Q // 4, g, 0:64], PS[1][pr:pr + 32, 0:64], e="act")

            if STG < 2:
                continue
            NN = Q // 4 + 1
            for g in range(4):
                for hl in range(4):
                    h = g * 4 + hl
                    acc = PS[2]
                    imp = PS[3 + (hl % 2)]
                    for N in range(NN):
                        d = Q - 4 * N
                        masks = [(identB.a, NEGcmp[:, d, :])] if d <= 3 else []
                        btab = biasCmp0 if N == 0 else biasCmp
                        attn_tile(h, kcmpT[:, g, N * 128:(N + 1) * 128], masks, btab[:, h, d:d + 1],
                                  vcmpA[:, N, g, :], acc, N == 0, N == NN - 1, imp=imp, imp_rhs=Amap[:, N, :])
                    finish(h, 0, acc, True)
                    for s in range(4):
                        scg = WK[2 + (s // 2)][:, (s % 2) * 512 + g * 128:(s % 2) * 512 + (g + 1) * 128]
                        S.stt(scg, imp[:, s * 128:(s + 1) * 128], rs[:, 4 + s:5 + s], tiebrk.a if hl == 0 else scg, ALU.mult, ALU.add)
                c0b = Q * 8
                for s in range(4):
                    scg = WK[2 + (s // 2)][:, (s % 2) * 512 + g * 128:(s % 2) * 512 + (g + 1) * 128]
                    cb = c0b + 2 * s
                    S.cp(sc.a, scg)
                    if cb + 2 < 128:
                        S.ms(sc[:, cb + 2:128], -1e30)
                    S.ms(sc[0:64, cb + 1:cb + 2], -1e30)
                    S.ms(sc[:, 0:1], 1e9)
                    S.ms(sc[:, cb:cb + 1], 1e9)
                    if cb >= 1:
                        S.ms(sc[0:64, cb - 1:cb], 1e9)
                    S.ms(sc[64:128, cb + 1:cb + 2], 1e9)
                    S.op("dve", lambda e_: e_.max(out=mx.t[:, 0:8], in_=sc.t[:]), [sc.a], [mx.a])
                    S.op("dve", lambda e_: e_.match_replace(out=sc2.t[:], in_to_replace=mx.t[:, 0:8], in_values=sc.t[:], imm_value=-1e30),
                         [sc.a, mx.a], [sc2.a])
                    S.op("dve", lambda e_: e_.max(out=mx.t[:, 8:16], in_=sc2.t[:]), [sc2.a], [mx.a])
                    S.ts(sc2.a, sc.a, mx[:, 15:16], op0=ALU.is_ge)
                    S.tr(PS[5][:, 0:128], sc2.a, identF.a)
                    S.act(NBT[:, g, s * 128:(s + 1) * 128], PS[5][:, 0:128], AF.Identity, scale=-NEGM, bias=NEGM)

            if STG < 3:
                continue
            for g in range(4):
                accs = [PS[2], PS[3], PS[4], PS[6]]
                for C in range(Q + 1):
                    kc_ = Kc[C % 2]
                    vc_ = Vc[C % 2]
                    S.dma("sp", kc_.a, KS[:, g, C * 512:(C + 1) * 512])
                    S.dma("act", vc_.a, VS[:, C * 4:(C + 1) * 4, g, :])
                    for hl in range(4):
                        h = g * 4 + hl
                        for a in range(4):
                            i = 4 * C + a
                            masks = [(Emat[:, i, :], NBT[:, g, :])]
                            if C == Q:
                                masks.append((identB.a, NEGcur[:, a, :]))
                            dd = 4 * (Q - C) - a
                            attn_tile(h, kc_[:, a * 128:(a + 1) * 128], masks, biasSel[:, h, dd + 3:dd + 4],
                                      vc_[:, a, :], accs[hl], C == 0 and a == 0, C == Q and a == 3)
                for hl in range(4):
                    finish(g * 4 + hl, 1, accs[hl], False)
            if STG < 4:
                if STG == 3:
                    for ti in range(4):
                        S.dma('sp', V(OUT, OUT.t[t0 + ti * 128:t0 + (ti + 1) * 128, :]), obuf[:, ti, :])
                continue
            for h in range(16):
                g = h // 4
                acc = PS[2 + (h % 2)]
                tiles = []
                if Q > 0:
                    for a in range(4):
                        tiles.append((kwT[:, g, 1 - par, a * 128:(a + 1) * 128], NEGprev[:, a, :], 4 - a, vwA[:, 1 - par, a, g, :]))
                for a in range(4):
                    tiles.append((kwT[:, g, par, a * 128:(a + 1) * 128], NEGcur[:, a, :], -a, vwA[:, par, a, g, :]))
                for k_, (kl, mk, dd, vr) in enumerate(tiles):
                    attn_tile(h, kl, [(identB.a, mk)], biasSel[:, h, dd + 3:dd + 4], vr, acc, k_ == 0, k_ == len(tiles) - 1)
                finish(h, 2, acc, False)
            for ti in range(4):
                S.dma("sp", V(OUT, OUT.t[t0 + ti * 128:t0 + (ti + 1) * 128, :]), obuf[:, ti, :])
        S.barrier()
        print("L3 instructions:", S.n_inst)
    return nc


_CACHE = {}


def build_fused(S_LEN):
    nc = bass.Bass("TRN2", target_bir_lowering=False)
    ins = {}
    for n in IN_NAMES:
        shp = [S_LEN, D] if n == "x" else IN_SHAPES[n]
        ins[n] = nc.dram_tensor(n, shp, F32, kind="ExternalInput").ap()
    out_ap = nc.dram_tensor("out", [S_LEN, D], F32, kind="ExternalOutput").ap()
    with ExitStack() as outer:
        S = Sched(nc, outer)
        X = {n: S.ext(ins[n], n) for n in IN_NAMES}
        OUT = S.ext(out_ap, "out")
        OA = S.dram([S_LEN, D], F32, "oa_s")
        OB = S.dram([S_LEN, D], F32, "ob_s")
        build_l2(S_LEN, ctx=(nc, S, X, OA))
        build_l3(S_LEN, ctx=(nc, S, X, OB))
        build_l4(S_LEN, ctx=(nc, S, X, OUT, OA, OB))
    return nc


def kernel(**inputs):
    x = np.asarray(inputs["x"], dtype=np.float32)
    B, S_LEN = x.shape[0], x.shape[1]
    if S_LEN not in _CACHE:
        _CACHE[S_LEN] = build_fused(S_LEN)
    nc = _CACHE[S_LEN]
    shared = {n: np.ascontiguousarray(np.asarray(inputs[n], dtype=np.float32).reshape(IN_SHAPES[n]))
              for n in IN_NAMES if n != "x"}
    maps = [dict(shared, x=np.ascontiguousarray(x[b])) for b in range(B)]
    res = run_bass_kernel_spmd(nc, maps, core_ids=list(range(B)))
    return np.stack([np.asarray(r["out"], dtype=np.float32) for r in res.results], axis=0)
```

```python
import math
from contextlib import ExitStack
import numpy as np
import concourse.bass as bass
import concourse.mybir as mybir
from concourse.bass_utils import run_bass_kernel_spmd

F32 = mybir.dt.float32
BF16 = mybir.dt.bfloat16
AF = mybir.ActivationFunctionType
ALU = mybir.AluOpType
AX = mybir.AxisListType

D = 1024
NEGM = -30000.0
ROLL_LIMIT = 40000


class V:
    def __init__(self, t, ap):
        self.t = t
        self.ap = ap

    def __getitem__(self, idx):
        return V(self.t, self.ap[idx])

    def r(self, pat, **kw):
        return V(self.t, self.ap.rearrange(pat, **kw))

    def bc(self, shape):
        return V(self.t, self.ap.to_broadcast(list(shape)))

    def us(self, d):
        return V(self.t, self.ap.unsqueeze(d))


class T:
    def __init__(self, name, t, space):
        self.name = name
        self.t = t
        self.space = space
        self.writers = {}
        self.readers = {}

    def __getitem__(self, idx):
        return V(self, self.t[idx])

    @property
    def a(self):
        return V(self, self.t[:])


def _ap(x):
    return x.ap if isinstance(x, V) else x


class Sched:
    def __init__(self, nc, stack):
        self.nc = nc
        self.stack = stack
        self.sem_stack = stack
        self.dsems = []
        self.eng = {"pe": nc.tensor, "act": nc.scalar, "dve": nc.vector,
                    "pool": nc.gpsimd, "sp": nc.sync}
        self.sem = {}
        self.cnt = {}
        for k in self.eng:
            self.sem[k] = stack.enter_context(nc.semaphore("s_" + k))
            self.cnt[k] = 0
        self.waited = {k: {} for k in self.eng}
        self.key = {k: k for k in self.eng}
        self.epoch = {k: 0 for k in self.eng}
        self.prev = {k: None for k in self.eng}
        self.n_inst = 0
        self.uid = 0
        self.sb_bytes = 0

    def sbuf(self, shape, dtype, name=None):
        self.uid += 1
        name = (name or "sb") + str(self.uid)
        t = self.stack.enter_context(self.nc.sbuf_tensor(name, list(shape), dtype))
        self.sb_bytes += int(np.prod(shape[1:])) * (4 if dtype == F32 else 2)
        return T(name, t, "sbuf")

    def psum(self, shape, dtype, name=None):
        self.uid += 1
        name = (name or "ps") + str(self.uid)
        t = self.stack.enter_context(self.nc.psum_tensor(name, list(shape), dtype))
        return T(name, t, "psum")

    def dram(self, shape, dtype, name=None):
        self.uid += 1
        name = (name or "dr") + str(self.uid)
        t = self.nc.dram_tensor(name, list(shape), dtype, kind="Internal")
        return T(name, t.ap(), "dram")

    def ext(self, ap, name):
        return T(name, ap, "dram")

    def _wait(self, e, deps):
        w = self.waited[e]
        for key, (sem, val) in deps.items():
            if e == "pe" and key.split("#")[0] == "pe":
                continue
            if w.get(key, 0) < val:
                self.eng[e].wait_ge(sem, val)
                w[key] = val

    @staticmethod
    def _merge(d, key, sem, val):
        if key not in d or d[key][1] < val:
            d[key] = (sem, val)

    def _deps(self, reads, writes):
        deps = {}
        for t in reads:
            for k, (s, v) in t.writers.items():
                self._merge(deps, k, s, v)
        for t in writes:
            for k, (s, v) in t.writers.items():
                self._merge(deps, k, s, v)
            for k, (s, v) in t.readers.items():
                self._merge(deps, k, s, v)
        return deps

    def op(self, e, fn, reads, writes):
        reads = list({id(x.t): x.t for x in reads if isinstance(x, V)}.values())
        writes = list({id(x.t): x.t for x in writes if isinstance(x, V)}.values())
        if self.cnt[e] >= ROLL_LIMIT:
            self._roll(e)
        self._wait(e, self._deps(reads, writes))
        inst = fn(self.eng[e])
        self.cnt[e] += 1
        inst.then_inc(self.sem[e], 1)
        v = self.cnt[e]
        kk_ = self.key[e]
        for t in reads:
            self._merge(t.readers, kk_, self.sem[e], v)
        for t in writes:
            self._merge(t.writers, kk_, self.sem[e], v)
        self.n_inst += 1
        return inst

    def dma(self, q, out, in_, **kw):
        reads = [in_.t]
        writes = [out.t]
        self._wait(q, self._deps(reads, writes))
        sb = out.t if out.t.space == "sbuf" else in_.t
        if not hasattr(sb, "dsem"):
            self.uid += 1
            sb.dsem = {"name": sb.name, "val": 0,
                       "sem": self.sem_stack.enter_context(self.nc.semaphore(f"d{self.uid}"))}
            self.dsems.append(sb.dsem)
        sem = sb.dsem
        key = "dma_" + sem["name"]
        sem["val"] += 16
        inst = self.eng[q].dma_start(out=out.ap, in_=in_.ap, **kw)
        inst.then_inc(sem["sem"], 16)
        for t in reads:
            self._merge(t.readers, key, sem["sem"], sem["val"])
        for t in writes:
            self._merge(t.writers, key, sem["sem"], sem["val"])
        self.n_inst += 1
        return inst

    def _roll(self, e):
        self.prev[e] = (self.key[e], self.sem[e], self.cnt[e])
        self.epoch[e] += 1
        self.key[e] = f"{e}#{self.epoch[e]}"
        self.uid += 1
        self.sem[e] = self.sem_stack.enter_context(self.nc.semaphore(f"s_{e}_{self.uid}"))
        self.cnt[e] = 0

    def barrier(self):
        for e in self.eng:
            for o in self.eng:
                if o == e:
                    continue
                cands = [(self.key[o], self.sem[o], self.cnt[o])]
                if self.prev[o] is not None:
                    cands.append(self.prev[o])
                for (ko, so, co) in cands:
                    if co > 0 and self.waited[e].get(ko, 0) < co:
                        self.eng[e].wait_ge(so, co)
                        self.waited[e][ko] = co
            for d in self.dsems:
                key = "dma_" + d["name"]
                if d["val"] > 0 and self.waited[e].get(key, 0) < d["val"]:
                    self.eng[e].wait_ge(d["sem"], d["val"])
                    self.waited[e][key] = d["val"]

    def mm(self, out, lhsT, rhs, start=True, stop=True):
        return self.op("pe", lambda e: e.matmul(out.ap, lhsT=lhsT.ap, rhs=rhs.ap, start=start, stop=stop),
                       [lhsT, rhs], [out])

    def tr(self, out, in_, ident):
        return self.op("pe", lambda e: e.transpose(out=out.ap, in_=in_.ap, identity=ident.ap), [in_, ident], [out])

    def act(self, out, in_, func, scale=1.0, bias=0.0, accum=None, e="act"):
        kw = {}
        if accum is not None:
            kw["accum_out"] = accum.ap
        return self.op(e, lambda g: g.activation(out=out.ap, in_=in_.ap, func=func, scale=_ap(scale), bias=_ap(bias), **kw),
                       [in_, scale, bias], [out, accum])

    def tt(self, out, a, b, op, e="dve"):
        return self.op(e, lambda g: g.tensor_tensor(out=out.ap, in0=a.ap, in1=b.ap, op=op), [a, b], [out])

    def ts(self, out, a, s1, s2=None, op0=ALU.mult, op1=None, e="dve"):
        if op1 is None:
            return self.op(e, lambda g: g.tensor_scalar(out=out.ap, in0=a.ap, scalar1=_ap(s1), scalar2=None, op0=op0),
                           [a, s1], [out])
        return self.op(e, lambda g: g.tensor_scalar(out=out.ap, in0=a.ap, scalar1=_ap(s1), scalar2=_ap(s2), op0=op0, op1=op1),
                       [a, s1, s2], [out])

    def stt(self, out, a, s, b, op0, op1):
        return self.op("dve", lambda g: g.scalar_tensor_tensor(out=out.ap, in0=a.ap, scalar=_ap(s), in1=b.ap, op0=op0, op1=op1),
                       [a, s, b], [out])

    def cp(self, out, a, e="dve"):
        if e == "act":
            return self.act(out, a, AF.Copy)
        return self.op(e, lambda g: g.tensor_copy(out=out.ap, in_=a.ap), [a], [out])

    def ms(self, out, val, e="dve"):
        return self.op(e, lambda g: g.memset(out.ap, val), [], [out])

    def red(self, out, a, op=ALU.add):
        return self.op("dve", lambda g: g.tensor_reduce(out=out.ap, in_=a.ap, axis=AX.X, op=op), [a], [out])

    def recip(self, out, a):
        return self.op("dve", lambda g: g.reciprocal(out=out.ap, in_=a.ap), [a], [out])

    def asel(self, out, in_, pattern, cmp, fill, base, cm):
        return self.op("pool", lambda g: g.affine_select(out=out.ap, in_=in_.ap, pattern=pattern, compare_op=cmp,
                                                         fill=fill, base=base, channel_multiplier=cm), [in_], [out])


IN_NAMES = ["x", "norm_mix_pre", "norm_mix_post", "norm_ffn_pre", "norm_ffn_post", "w_in",
            "rwkv_mu", "rwkv_w0", "rwkv_w2", "rwkv_a0", "rwkv_a2", "rwkv_g2", "rwkv_k_k", "rwkv_k_a",
            "rwkv_r_k", "rwkv_ln_w", "rwkv_ln_b",
            "cmp_pos_k", "cmp_w1_k", "cmp_w2_k", "cmp_pos_v", "cmp_w1_v", "cmp_w2_v",
            "w_branch_rwkv", "w_branch_nsa", "w_out", "ffn_up", "ffn_conv_w", "ffn_conv_b", "ffn_down"]
IN_SHAPES = {
    "norm_mix_pre": [1, 1024], "norm_mix_post": [1, 1024], "norm_ffn_pre": [1, 1024], "norm_ffn_post": [1, 1024],
    "w_in": [1024, 8016], "rwkv_mu": [1, 3360], "rwkv_w0": [1, 1024], "rwkv_w2": [64, 1024], "rwkv_a0": [1, 1024],
    "rwkv_a2": [64, 1024], "rwkv_g2": [160, 1024], "rwkv_k_k": [1, 1024], "rwkv_k_a": [1, 1024], "rwkv_r_k": [1, 1024],
    "rwkv_ln_w": [1, 1024], "rwkv_ln_b": [1, 1024], "cmp_pos_k": [1, 2048], "cmp_w1_k": [2048, 128], "cmp_w2_k": [128, 64],
    "cmp_pos_v": [1, 2048], "cmp_w1_v": [2048, 128], "cmp_w2_v": [128, 64], "w_branch_rwkv": [1024, 1024],
    "w_branch_nsa": [1024, 1024], "w_out": [1024, 1024], "ffn_up": [1024, 5632], "ffn_conv_w": [3, 5632],
    "ffn_conv_b": [1, 5632], "ffn_down": [2816, 1024],
}
C_R, C_K, C_V, C_WD, C_AD, C_GD = 0, 1024, 2048, 3072, 3136, 3200
NS0 = 3360
C_Q, C_KC, C_VC, C_KS, C_VS, C_KW, C_VW, C_GL = NS0, NS0 + 1024, NS0 + 1280, NS0 + 1536, NS0 + 1792, NS0 + 2048, NS0 + 2304, NS0 + 2560
C_GA, C_GB = 5968, 6992
SLOPES = [2.0 ** (-0.5 * h) for h in range(1, 17)]
CDEC = math.exp(-0.5)


def build_l4(S_LEN, parts=(), ctx=None):
    NQ = S_LEN // 512
    fused = ctx is not None
    if not fused:
        nc = bass.Bass("TRN2", target_bir_lowering=False)
        ins = {}
        for n in IN_NAMES:
            shp = [S_LEN, D] if n == "x" else IN_SHAPES[n]
            ins[n] = nc.dram_tensor(n, shp, F32, kind="ExternalInput").ap()
        out_ap = nc.dram_tensor("out", [S_LEN, D], F32, kind="ExternalOutput").ap()
        oa_ap = nc.dram_tensor("oaT", [D, S_LEN], F32, kind="ExternalInput").ap()
        ob_ap = nc.dram_tensor("obT", [D, S_LEN], F32, kind="ExternalInput").ap()

    with ExitStack() as st:
        if not fused:
            S = Sched(nc, st)
            X = {n: S.ext(ins[n], n) for n in IN_NAMES}
            OUT = S.ext(out_ap, "out")
            OA = S.ext(oa_ap, "oaT")
            OB = S.ext(ob_ap, "obT")
        else:
            nc, S, X, OUT, OA, OB = ctx
            S.stack = st

        identF = S.sbuf([128, 128], F32, "identF")
        identB = S.sbuf([128, 128], BF16, "identB")
        S.ms(identF.a, 1.0, e="pool")
        S.asel(identF.a, identF.a, [[-1, 128]], ALU.is_equal, 0.0, 0, 1)
        S.cp(identB.a, identF.a)

        def bload(name, n, q="sp"):
            t = S.sbuf([128, n], F32, name)
            S.dma(q, t.a, V(X[name], X[name].t[0:1, :].to_broadcast([128, n])))
            return t

        def pload(name, kc, q="sp"):
            t = S.sbuf([128, kc], F32, name + "_p")
            src = X[name].t[0, :].rearrange("(kc p) -> p kc", p=128)
            S.dma(q, t.a, V(X[name], src), allow_slow_non_contiguous=True)
            return t

        g_pre = pload("norm_mix_pre", 8)
        g_fpre = pload("norm_ffn_pre", 8)
        g_post = bload("norm_mix_post", 1024)
        g_fpost = bload("norm_ffn_post", 1024)
        eps_t = S.sbuf([128, 1], F32, "eps")
        S.ms(eps_t.a, 1e-6)

        WinA = S.dram([128, 8, 8016], BF16, "WinA")
        WinB = S.dram([128, 8, 3360], BF16, "WinB")
        Wbra = S.dram([128, 8, 1024], BF16, "Wbra")
        Wbrb = S.dram([128, 8, 1024], BF16, "Wbrb")
        Wout = S.dram([128, 8, 1024], BF16, "Wout")
        Wup = S.dram([128, 8, 5632], BF16, "Wup")
        Wdn = S.dram([128, 22, 1024], BF16, "Wdn")

        NWK = 14
        WK = [S.sbuf([128, 1024], F32, f"wk{i}") for i in range(NWK)]
        PS = [S.psum([128, 512], F32, f"psb{i}") for i in range(7)]
        PSB = S.psum([128, 1024], BF16, "psB")
        stg_b = [S.sbuf([128, 8, 128], BF16, f"stgb{i}") for i in range(2)]

        mu_bc = WK[12]
        omu_bc = WK[13]
        step_i = [0]

        def prepass(src_name, ncols, kcs, dst, gain, colscale=None, c_src0=0):
            src = X[src_name]
            for c0 in range(0, ncols, 128):
                cw = min(128, ncols - c0)
                for k0 in range(0, kcs, 8):
                    kn = min(8, kcs - k0)
                    i = step_i[0]
                    step_i[0] += 1
                    sf = WK[i % 4]
                    sb_ = stg_b[i % 2]
                    sview = sf.a.r("p (k c) -> p k c", c=128)
                    srcap = src.t[k0 * 128:(k0 + kn) * 128, c_src0 + c0:c_src0 + c0 + cw].rearrange("(k p) c -> p k c", p=128)
                    S.dma("sp" if i % 2 == 0 else "act", sview[:, 0:kn, 0:cw], V(src, srcap))
                    for k in range(kn):
                        e = "dve" if k % 2 == 0 else "pool"
                        if gain is None:
                            S.cp(sb_[:, k, 0:cw], sview[:, k, 0:cw], e=e)
                        elif colscale is None:
                            S.ts(sb_[:, k, 0:cw], sview[:, k, 0:cw], gain[:, k0 + k:k0 + k + 1], op0=ALU.mult, e=e)
                        else:
                            S.stt(sb_[:, k, 0:cw], sview[:, k, 0:cw], gain[:, k0 + k:k0 + k + 1],
                                  colscale[:, c0:c0 + cw], ALU.mult, ALU.mult)
                    S.dma("pool", dst[:, k0:k0 + kn, c0:c0 + cw], sb_[:, 0:kn, 0:cw])

        if "rwkv" in parts:
            for cb in range(0, 3360, 1024):
                cbw = min(1024, 3360 - cb)
                S.dma("sp", mu_bc[:, 0:cbw], V(X["rwkv_mu"], X["rwkv_mu"].t[0:1, cb:cb + cbw].to_broadcast([128, cbw])))
                S.ts(omu_bc[:, 0:cbw], mu_bc[:, 0:cbw], -1.0, 1.0, ALU.mult, ALU.add)
                prepass("w_in", cbw, 8, V(WinA, WinA.t[:, :, cb:cb + cbw]), g_pre, omu_bc, c_src0=cb)
                prepass("w_in", cbw, 8, V(WinB, WinB.t[:, :, cb:cb + cbw]), g_pre, mu_bc, c_src0=cb)
        prepass("w_in", 8016 - C_GA, 8, V(WinA, WinA.t[:, :, C_GA:8016]), g_pre, None, c_src0=C_GA)
        prepass("w_branch_rwkv", 1024, 8, Wbra, None)
        prepass("w_branch_nsa", 1024, 8, Wbrb, None)
        prepass("w_out", 1024, 8, Wout, None)
        prepass("ffn_up", 5632, 8, Wup, g_fpre)
        prepass("ffn_down", 1024, 22, Wdn, None)

        xnT = S.sbuf([128, 8, 513], BF16, "xnT")
        S.ms(xnT[:, :, 0:1], 0.0)
        xn2T = xnT[:, :, 1:513]
        oaT = S.sbuf([128, 8, 512], BF16, "oaT")
        obT = S.sbuf([128, 8, 512], BF16, "obT")
        mT = S.sbuf([128, 8, 512], BF16, "mT")
        x1 = S.sbuf([128, 4, 1024], F32, "x1")
        ring = [S.sbuf([128, 8, 512], BF16, f"ring{i}") for i in range(4)]
        ring_i = [0]
        convw = S.sbuf([128, 44, 3], F32, "convw")
        convb = S.sbuf([128, 44], F32, "convb")
        for j in range(3):
            S.dma("sp", convw[:, :, j], V(X["ffn_conv_w"], X["ffn_conv_w"].t[j, :].rearrange("(c p) -> p c", p=128)),
                  allow_slow_non_contiguous=True)
        S.dma("sp", convb.a, V(X["ffn_conv_b"], X["ffn_conv_b"].t[0, :].rearrange("(c p) -> p c", p=128)),
              allow_slow_non_contiguous=True)
        ucarry = S.sbuf([128, 44, 2], F32, "ucarry")
        S.ms(ucarry.a, 0.0)
        actT = S.sbuf([128, 22, 256], BF16, "actT")
        small = S.sbuf([128, 64], F32, "small")
        xb = S.sbuf([128, 1024], BF16, "xb")

        def slab(src, c0, cw, kcs=8, k0=0):
            r = ring[ring_i[0] % len(ring)]
            ring_i[0] += 1
            q = ["sp", "act"][ring_i[0] % 2]
            S.dma(q, r[:, 0:kcs, 0:cw], src[:, k0:k0 + kcs, c0:c0 + cw])
            return r

        def rstd_of(xt, col):
            S.act(WK[11].a, xt, AF.Square, accum=small[:, col:col + 1])
            S.act(small[:, col + 1:col + 2], small[:, col:col + 1], AF.Sqrt, scale=1.0 / 1024, bias=1e-6)
            S.recip(small[:, col:col + 1], small[:, col + 1:col + 2])

        def to_T(dst, src_b, ti):
            psb = PSB
            for kc in range(8):
                S.tr(psb[:, kc * 128:(kc + 1) * 128], src_b[:, kc * 128:(kc + 1) * 128], identB.a)
            S.cp(dst, psb.a.r("p (k t) -> p k t", t=128), e="act")


        for Q in range(NQ):
            t0 = Q * 512
            for ti in range(4):
                xt = x1[:, ti, :]
                S.dma("sp", xt, V(X["x"], X["x"].t[t0 + ti * 128:t0 + (ti + 1) * 128, :]))
                rstd_of(xt, 0)
                S.ts(xb.a, xt, small[:, 0:1], op0=ALU.mult)
                to_T(xnT[:, :, 1 + ti * 128:1 + (ti + 1) * 128], xb, ti)

            if not fused:
                S.dma("pool", oaT.a, V(OA, OA.t[:, t0:t0 + 512].rearrange("(k p) t -> p k t", p=128)))
                S.dma("pool", obT.a, V(OB, OB.t[:, t0:t0 + 512].rearrange("(k p) t -> p k t", p=128)))
            else:
                for SRC, dstT in ((OA, oaT), (OB, obT)):
                    for ti in range(4):
                        S.dma("pool", xb.a, V(SRC, SRC.t[t0 + ti * 128:t0 + (ti + 1) * 128, :]))
                        to_T(dstT[:, :, ti * 128:(ti + 1) * 128], xb, ti)

            for half in range(2):
                wa = slab(Wbra, half * 512, 512)
                wga = slab(WinA, C_GA + half * 512, 512)
                wb = slab(Wbrb, half * 512, 512)
                wgb = slab(WinA, C_GB + half * 512, 512)
                for cc in range(4):
                    mc = half * 4 + cc
                    cs = slice(cc * 128, (cc + 1) * 128)
                    for kc in range(8):
                        S.mm(PS[0].a, wa[:, kc, cs], oaT[:, kc, :], kc == 0, kc == 7)
                    for kc in range(8):
                        S.mm(PS[1].a, wga[:, kc, cs], xnT[:, kc, 1:513], kc == 0, kc == 7)
                    for kc in range(8):
                        S.mm(PS[2].a, wb[:, kc, cs], obT[:, kc, :], kc == 0, kc == 7)
                    for kc in range(8):
                        S.mm(PS[3].a, wgb[:, kc, cs], xnT[:, kc, 1:513], kc == 0, kc == 7)
                    ga = WK[4][:, 0:512]
                    gb = WK[4][:, 512:1024]
                    S.act(ga, PS[1].a, AF.Sigmoid)
                    S.act(gb, PS[3].a, AF.Sigmoid)
                    S.tt(ga, ga, PS[0].a, ALU.mult)
                    S.tt(gb, gb, PS[2].a, ALU.mult)
                    S.tt(mT[:, mc, :], ga, gb, ALU.add)
            S.cp(xnT[:, :, 0:1], xnT[:, :, 512:513], e="pool")
            wo = [slab(Wout, h * 512, 512) for h in range(2)]
            for ti in range(4):
                ts_ = slice(ti * 128, (ti + 1) * 128)
                for h in range(2):
                    for kc in range(8):
                        S.mm(PS[h].a, mT[:, kc, ts_], wo[h][:, kc, :], kc == 0, kc == 7)
                y = WK[5]
                S.cp(y[:, 0:512], PS[0].a, e="act")
                S.cp(y[:, 512:1024], PS[1].a)
                rstd_of(y.a, 2)
                S.stt(y.a, y.a, small[:, 2:3], g_post.a, ALU.mult, ALU.mult)
                S.tt(x1[:, ti, :], x1[:, ti, :], y.a, ALU.add)
                rstd_of(x1[:, ti, :], 4)
                S.ts(xb.a, x1[:, ti, :], small[:, 4:5], op0=ALU.mult)
                to_T(xn2T[:, :, ti * 128:(ti + 1) * 128], xb, ti)
            for hf in range(2):
                fs = slice(hf * 256, (hf + 1) * 256)
                for cg in range(0, 22, 4):
                    ncg = min(4, 22 - cg)
                    wg = slab(Wup, cg * 128, ncg * 128)
                    wv = slab(Wup, 2816 + cg * 128, ncg * 128)
                    for c in range(ncg):
                        ch = cg + c
                        res = []
                        for which, wsl in ((0, wg), (1, wv)):
                            chan = ch + 22 * which
                            ps = PS[2 + which]
                            for kc in range(8):
                                S.mm(ps[:, 0:256], wsl[:, kc, c * 128:(c + 1) * 128], xn2T[:, kc, fs], kc == 0, kc == 7)
                            ub = WK[6 + which]
                            S.cp(ub[:, 0:2], ucarry[:, chan, :], e="pool")
                            S.cp(ub[:, 2:258], ps[:, 0:256], e="act")
                            S.cp(ucarry[:, chan, :], ub[:, 256:258], e="pool")
                            cv = WK[8 + which]
                            S.ts(cv[:, 0:256], ub[:, 2:258], convw[:, chan, 2:3], convb[:, chan:chan + 1], ALU.mult, ALU.add)
                            S.stt(cv[:, 0:256], ub[:, 1:257], convw[:, chan, 1:2], cv[:, 0:256], ALU.mult, ALU.add)
                            S.stt(cv[:, 0:256], ub[:, 0:256], convw[:, chan, 0:1], cv[:, 0:256], ALU.mult, ALU.add)
                            res.append(cv)
                        S.act(res[0][:, 256:512], res[0][:, 0:256], AF.Silu)
                        S.tt(actT[:, ch, :], res[0][:, 256:512], res[1][:, 0:256], ALU.mult)
                for oh in range(2):
                    for kg in range(0, 22, 8):
                        kn = min(8, 22 - kg)
                        wd_ = slab(Wdn, oh * 512, 512, kcs=kn, k0=kg)
                        for tl in range(2):
                            for k in range(kn):
                                S.mm(PS[4 + tl].a, actT[:, kg + k, tl * 128:(tl + 1) * 128], wd_[:, k, :],
                                     kg + k == 0, kg + k == 21)
                    for tl in range(2):
                        ti = hf * 2 + tl
                        y = WK[5 + tl]
                        S.cp(y[:, oh * 512:(oh + 1) * 512], PS[4 + tl].a, e="act" if tl == 0 else "dve")
                for tl in range(2):
                    ti = hf * 2 + tl
                    y = WK[5 + tl]
                    rstd_of(y.a, 6 + 2 * tl)
                    S.stt(y.a, y.a, small[:, 6 + 2 * tl:7 + 2 * tl], g_fpost.a, ALU.mult, ALU.mult)
                    S.tt(y.a, y.a, x1[:, ti, :], ALU.add)
                    S.dma("sp", V(OUT, OUT.t[t0 + ti * 128:t0 + (ti + 1) * 128, :]), y.a)

        S.barrier()
        print("L4 instructions:", S.n_inst)
    return nc


def build_l2(S_LEN, STAGE=99, ctx=None):
    NQ = S_LEN // 512
    fused = ctx is not None
    names = ["x", "norm_mix_pre", "w_in", "rwkv_mu", "rwkv_w0", "rwkv_w2", "rwkv_a0", "rwkv_a2", "rwkv_g2",
             "rwkv_k_k", "rwkv_k_a", "rwkv_r_k", "rwkv_ln_w", "rwkv_ln_b"]
    if not fused:
        nc = bass.Bass("TRN2", target_bir_lowering=False)
        ins = {}
        for n in names:
            shp = [S_LEN, D] if n == "x" else IN_SHAPES[n]
            ins[n] = nc.dram_tensor(n, shp, F32, kind="ExternalInput").ap()
        out_ap = nc.dram_tensor("oa", [S_LEN, D], F32, kind="ExternalOutput").ap()
    with ExitStack() as st:
        if not fused:
            S = Sched(nc, st)
            X = {n: S.ext(ins[n], n) for n in names}
            OUT = S.ext(out_ap, "oa")
        else:
            nc, S, X, OUT = ctx
            S.stack = st
        identF = S.sbuf([128, 128], F32, "identF")
        identB = S.sbuf([128, 128], BF16, "identB")
        S.ms(identF.a, 1.0, e="pool")
        S.asel(identF.a, identF.a, [[-1, 128]], ALU.is_equal, 0.0, 0, 1)
        S.cp(identB.a, identF.a)
        MUi = S.sbuf([128, 128], F32, "MUi")
        MUs = S.sbuf([128, 128], F32, "MUs")
        MLs = S.sbuf([128, 128], F32, "MLs")
        for m_, base, cm, pat in ((MUi, 0, -1, 1), (MUs, -1, -1, 1), (MLs, -1, 1, -1)):
            S.ms(m_.a, 1.0, e="pool")
            S.asel(m_.a, m_.a, [[pat, 128]], ALU.is_ge, 0.0, base, cm)
        S.ms(MUi[0:64, 64:128], 0.0, e="pool")
        S.ms(MUs[0:64, 64:128], 0.0, e="pool")
        S.ms(MLs[64:128, 0:64], 0.0, e="pool")
        Blk = S.sbuf([128, 128], F32, "Blk")
        S.ms(Blk.a, 0.0, e="pool")
        S.ms(Blk[0:64, 0:64], 1.0, e="pool")
        S.ms(Blk[64:128, 64:128], 1.0, e="pool")
        Elast = S.sbuf([128, 2, 64], F32, "Elast")
        S.ms(Elast.a, 0.0, e="pool")
        S.ms(Elast[32:64, 0, :], 1.0, e="pool")
        S.asel(Elast[32:64, 0, :], Elast[32:64, 0, :], [[0, 64]], ALU.is_equal, 0.0, -31, 1)
        S.ms(Elast[96:128, 1, :], 1.0, e="pool")
        S.asel(Elast[96:128, 1, :], Elast[96:128, 1, :], [[0, 64]], ALU.is_equal, 0.0, -31, 1)

        def bload(name, n=1024):
            t = S.sbuf([128, n], F32, name)
            S.dma("sp", t.a, V(X[name], X[name].t[0:1, :].to_broadcast([128, n])))
            return t

        g_pre = S.sbuf([128, 8], F32, "gpre")
        S.dma("sp", g_pre.a, V(X["norm_mix_pre"], X["norm_mix_pre"].t[0, :].rearrange("(kc p) -> p kc", p=128)),
              allow_slow_non_contiguous=True)
        w0b, a0b, kkb, kab, rkb, lnw, lnb = [bload(n) for n in
                                             ("rwkv_w0", "rwkv_a0", "rwkv_k_k", "rwkv_k_a", "rwkv_r_k", "rwkv_ln_w", "rwkv_ln_b")]
        w2 = S.sbuf([64, 1024], F32, "w2")
        a2 = S.sbuf([64, 1024], F32, "a2")
        g2 = S.sbuf([128, 2, 1024], F32, "g2")
        S.dma("sp", w2.a, V(X["rwkv_w2"], X["rwkv_w2"].t[:, :]))
        S.dma("sp", a2.a, V(X["rwkv_a2"], X["rwkv_a2"].t[:, :]))
        S.dma("sp", g2[:, 0, :], V(X["rwkv_g2"], X["rwkv_g2"].t[0:128, :]))
        S.dma("sp", g2[0:32, 1, :], V(X["rwkv_g2"], X["rwkv_g2"].t[128:160, :]))
        eps_t = S.sbuf([128, 1], F32, "eps")
        S.ms(eps_t.a, 1e-6)
        gneps = S.sbuf([128, 1], F32, "gneps")
        S.ms(gneps.a, 64e-5)

        WinA = S.dram([128, 8, 3360], BF16, "WinA")
        WinB = S.dram([128, 8, 3360], BF16, "WinB")
        WK = [S.sbuf([128, 1024], F32, f"wk{i}") for i in range(14)]
        PS = [S.psum([128, 512], F32, f"psb{i}") for i in range(7)]
        PSB = S.psum([128, 1024], BF16, "psB")
        ring = [S.sbuf([128, 8, 288], BF16, f"ring{i}") for i in range(3)]
        stg_b = [ring[0][:, :, 0:128], ring[1][:, :, 0:128]]
        mu_bc, omu_bc = WK[12], WK[13]
        step_i = [0]

        def prepass(ncols, dst, colscale, c_src0):
            src = X["w_in"]
            for c0 in range(0, ncols, 128):
                cw = min(128, ncols - c0)
                i = step_i[0]
                step_i[0] += 1
                sf = WK[i % 4]
                sb_ = stg_b[i % 2]
                sview = sf.a.r("p (k c) -> p k c", c=128)
                srcap = src.t[:, c_src0 + c0:c_src0 + c0 + cw].rearrange("(k p) c -> p k c", p=128)
                S.dma("sp" if i % 2 == 0 else "act", sview[:, :, 0:cw], V(src, srcap))
                for k in range(8):
                    S.stt(sb_[:, k, 0:cw], sview[:, k, 0:cw], g_pre[:, k:k + 1], colscale[:, c0:c0 + cw], ALU.mult, ALU.mult)
                S.dma("act" if i % 2 == 0 else "sp", dst[:, :, c0:c0 + cw], sb_[:, :, 0:cw])

        for cb in range(0, 3360, 1024):
            cbw = min(1024, 3360 - cb)
            S.dma("sp", mu_bc[:, 0:cbw], V(X["rwkv_mu"], X["rwkv_mu"].t[0:1, cb:cb + cbw].to_broadcast([128, cbw])))
            S.ts(omu_bc[:, 0:cbw], mu_bc[:, 0:cbw], -1.0, 1.0, ALU.mult, ALU.add)
            prepass(cbw, V(WinA, WinA.t[:, :, cb:cb + cbw]), omu_bc, cb)
            prepass(cbw, V(WinB, WinB.t[:, :, cb:cb + cbw]), mu_bc, cb)

        xnT = S.sbuf([128, 8, 513], BF16, "xnT")
        S.ms(xnT[:, :, 0:1], 0.0)
        ring_i = [0]
        small = S.sbuf([128, 128], F32, "small")
        xb = S.sbuf([128, 1024], BF16, "xb")
        thT = S.sbuf([64, 512], F32, "thT")
        adT = S.sbuf([64, 512], F32, "adT")
        sgT = S.sbuf([128, 2, 512], F32, "sgT")
        Sst = S.sbuf([64, 16, 64], F32, "Sst")
        S.ms(Sst.a, 0.0)
        XF = [S.sbuf([64, 8, 128], F32, f"xf{i}") for i in range(4)]
        Xm = S.sbuf([128, 8, 128], F32, "Xm")
        XTm = S.sbuf([128, 8, 128], F32, "XTm")
        Tm = S.sbuf([128, 8, 128], F32, "Tm")
        Mbr = S.sbuf([128, 8, 128], F32, "Mbr")
        Mkr = S.sbuf([128, 8, 128], F32, "Mkr")
        MakT = S.sbuf([128, 8, 128], F32, "MakT")
        ATt = S.sbuf([128, 8, 64], F32, "ATt")
        Gm = S.sbuf([64, 2, 8, 64], F32, "Gm")
        Hm = S.sbuf([128, 8, 64], F32, "Hm")
        RyC = S.sbuf([64, 2, 8, 128], F32, "RyC")
        pcd = S.sbuf([64, 8, 64], F32, "pcd")
        S.ms(RyC.a, 0.0)
        ATc = S.sbuf([128, 2, 8, 64], F32, "ATc")
        Hc = S.sbuf([128, 2, 8, 64], F32, "Hc")
        cmask = S.sbuf([128, 2], F32, "cmask")
        S.ms(cmask.a, 0.0)
        S.ms(cmask[0:64, 0:1], 1.0)
        S.ms(cmask[64:128, 1:2], 1.0)

        def slab(src, c0, cw):
            r = ring[ring_i[0] % len(ring)]
            ring_i[0] += 1
            q = ["sp", "act"][ring_i[0] % 2]
            S.dma(q, r[:, :, 0:cw], src[:, :, c0:c0 + cw])
            return r

        def h16(v):
            return v.r("p (h k) -> p h k", k=64)

        for Q in range(NQ):
            t0 = Q * 512
            for ti in range(4):
                xt = WK[0].a
                S.dma("sp", xt, V(X["x"], X["x"].t[t0 + ti * 128:t0 + (ti + 1) * 128, :]))
                S.act(WK[1].a, xt, AF.Square, accum=small[:, 0:1])
                S.act(small[:, 1:2], small[:, 0:1], AF.Sqrt, scale=1.0 / 1024, bias=1e-6)
                S.recip(small[:, 0:1], small[:, 1:2])
                S.ts(xb.a, xt, small[:, 0:1], op0=ALU.mult)
                for kc in range(8):
                    S.tr(PSB[:, kc * 128:(kc + 1) * 128], xb[:, kc * 128:(kc + 1) * 128], identB.a)
                S.cp(xnT[:, :, 1 + ti * 128:1 + (ti + 1) * 128], PSB.a.r("p (k t) -> p k t", t=128), e="act")
            sa = slab(WinA, C_WD, 288)
            sb2 = slab(WinB, C_WD, 288)
            for (m0, M, dst, fn) in ((0, 64, thT.a, AF.Tanh), (64, 64, adT.a, AF.Copy),
                                     (128, 128, sgT[:, 0, :], AF.Sigmoid), (256, 32, sgT[0:32, 1, :], AF.Sigmoid)):
                for kc in range(8):
                    S.mm(PS[0][0:M, :], sa[:, kc, m0:m0 + M], xnT[:, kc, 1:513], kc == 0, False)
                for kc in range(8):
                    S.mm(PS[0][0:M, :], sb2[:, kc, m0:m0 + M], xnT[:, kc, 0:512], False, kc == 7)
                S.act(dst, PS[0][0:M, :], fn)

            for ti in range(4):
                tsl = slice(ti * 128, (ti + 1) * 128)
                zr, zk, zv, sg, av, kk, kmod, bb, Lps, e2, At, Rt, Bt, Kt = [w.a for w in WK]
                for xi, dst in enumerate((zr, zk, zv)):
                    for qc in range(4):
                        c0 = xi * 1024 + qc * 256
                        sa_ = slab(WinA, c0, 256)
                        sb_ = slab(WinB, c0, 256)
                        ps = PS[qc % 2]
                        for kc in range(8):
                            S.mm(ps[:, 0:256], xnT[:, kc, 1 + ti * 128:1 + (ti + 1) * 128], sa_[:, kc, 0:256], kc == 0, False)
                        for kc in range(8):
                            S.mm(ps[:, 0:256], xnT[:, kc, ti * 128:(ti + 1) * 128], sb_[:, kc, 0:256], False, kc == 7)
                        S.cp(dst[:, qc * 256:(qc + 1) * 256], ps[:, 0:256], e="act" if qc % 2 == 0 else "dve")
                if STAGE < 2:
                    S.dma("sp", V(OUT, OUT.t[t0 + ti * 128:t0 + (ti + 1) * 128, :]), zr)
                    continue
                for h in range(2):
                    S.mm(PS[2 + h].a, thT[:, tsl], w2[:, h * 512:(h + 1) * 512])
                    S.tt(sg[:, h * 512:(h + 1) * 512], PS[2 + h].a, w0b[:, h * 512:(h + 1) * 512], ALU.add)
                S.act(sg, sg, AF.Sigmoid)
                for h in range(2):
                    S.mm(PS[2 + h].a, adT[:, tsl], a2[:, h * 512:(h + 1) * 512])
                    S.tt(av[:, h * 512:(h + 1) * 512], PS[2 + h].a, a0b[:, h * 512:(h + 1) * 512], ALU.add)
                S.act(av, av, AF.Sigmoid)
                for h in range(2):
                    S.mm(PS[2 + h].a, MUi.a, sg[:, h * 512:(h + 1) * 512])
                    S.cp(Lps[:, h * 512:(h + 1) * 512], PS[2 + h].a, e="act")
                    S.mm(PS[4 + h].a, Blk.a, sg[:, h * 512:(h + 1) * 512])
                S.tt(e2, Lps, sg, ALU.subtract)
                S.act(e2, e2, AF.Exp, scale=-CDEC)
                S.tt(kk, zk, kkb.a, ALU.mult)
                S.tt(bb, kk, kk, ALU.mult)
                S.red(small[:, 16:32], h16(bb))
                S.act(small[:, 16:32], small[:, 16:32], AF.Sqrt)
                S.ts(small[:, 16:32], small[:, 16:32], 1e-12, op0=ALU.max)
                S.recip(small[:, 32:48], small[:, 16:32])
                S.tt(h16(kk), h16(kk), small[:, 32:48].us(2).bc([128, 16, 64]), ALU.mult)
                S.stt(kmod, av, -1.0, kab.a, ALU.add, ALU.mult)
                S.stt(kmod, kmod, 1.0, zk, ALU.add, ALU.mult)
                S.tt(bb, kk, av, ALU.mult)
                S.stt(At, kk, -1.0, e2, ALU.mult, ALU.mult)
                eL = zk
                S.act(eL, Lps, AF.Exp, scale=-CDEC)
                S.tt(Rt, zr, eL, ALU.mult)
                S.act(e2, Lps, AF.Exp, scale=CDEC)
                S.tt(Bt, bb, e2, ALU.mult)
                S.tt(Kt, kmod, e2, ALU.mult)
                for h in range(2):
                    S.tt(e2[:, h * 512:(h + 1) * 512], PS[4 + h].a, Lps[:, h * 512:(h + 1) * 512], ALU.subtract)
                S.act(e2, e2, AF.Exp, scale=-CDEC)
                Bh, Kh = av, sg
                S.tt(Bh, bb, e2, ALU.mult)
                S.tt(Kh, kmod, e2, ALU.mult)
                if STAGE < 3:
                    S.dma("sp", V(OUT, OUT.t[t0 + ti * 128:t0 + (ti + 1) * 128, :]), Kt)
                    continue
                AtF, RtF, BtF, KtF = XF
                if STAGE < 4:
                    S.dma("sp", V(OUT, OUT.t[t0 + ti * 128:t0 + (ti + 1) * 128, :]), Kt)
                    continue
                yps = [PS[5], PS[6]]
                for hh in range(2):
                    for xi, src in enumerate((At, Rt, Bt, Kt)):
                        for q4 in range(2):
                            pb_ = PS[2 + q4]
                            for hq in range(4):
                                h = hh * 8 + q4 * 4 + hq
                                S.tr(pb_[0:64, hq * 128:(hq + 1) * 128], src[:, h * 64:(h + 1) * 64], identF.a)
                            S.cp(XF[xi][:, q4 * 4:(q4 + 1) * 4, :], pb_[0:64, :].r("p (a t) -> p a t", t=128),
                                 e="act" if q4 % 2 == 0 else "dve")

                    def fm(Xf, hl):
                        return Xf[:, hl, :]
                    def gram(dst, Lf, Rf, mask):
                        for g4 in range(2):
                            for hl4 in range(4):
                                hl = g4 * 4 + hl4
                                S.mm(PS[g4][:, hl4 * 128:(hl4 + 1) * 128], fm(Lf, hl), fm(Rf, hl))
                            S.tt(dst[:, g4 * 4:(g4 + 1) * 4, :], PS[g4].a.r("p (a t) -> p a t", t=128),
                                 mask.a.us(1).bc([128, 4, 128]), ALU.mult)
                    gram(Xm, BtF, AtF, MUs)
                    gram(XTm, AtF, BtF, MLs)
                    gram(MakT, AtF, KtF, MLs)
                    gram(Mbr, BtF, RtF, MUi)
                    gram(Mkr, KtF, RtF, MUi)
                    if STAGE < 5:
                        continue
                    S.tt(Tm.a, Xm.a, identF.a.us(1).bc([128, 8, 128]), ALU.add)
                    for it in range(5):
                        for g4 in range(2):
                            for hl4 in range(4):
                                hl = g4 * 4 + hl4
                                S.mm(PS[g4][:, hl4 * 128:(hl4 + 1) * 128], XTm[:, hl, :], Xm[:, hl, :])
                                S.mm(PS[2 + g4][:, hl4 * 128:(hl4 + 1) * 128], Xm[:, hl, :], XTm[:, hl, :])
                        for g4 in range(2):
                            S.cp(Xm[:, g4 * 4:(g4 + 1) * 4, :], PS[g4].a.r("p (a t) -> p a t", t=128), e="act")
                            S.cp(XTm[:, g4 * 4:(g4 + 1) * 4, :], PS[2 + g4].a.r("p (a t) -> p a t", t=128))
                        for g4 in range(2):
                            for hl4 in range(4):
                                hl = g4 * 4 + hl4
                                S.mm(PS[g4][:, hl4 * 128:(hl4 + 1) * 128], XTm[:, hl, :], Tm[:, hl, :])
                            S.tt(Tm[:, g4 * 4:(g4 + 1) * 4, :], Tm[:, g4 * 4:(g4 + 1) * 4, :],
                                 PS[g4].a.r("p (a t) -> p a t", t=128), ALU.add)
                    if STAGE < 6:
                        continue
                    for hl in range(8):
                        h = hh * 8 + hl
                        S.mm(PS[0][:, hl * 64:(hl + 1) * 64], Tm[:, hl, :], At[:, h * 64:(h + 1) * 64])
                    S.cp(ATt.a, PS[0].a.r("p (a k) -> p a k", k=64), e="act")
                    for g4 in range(2):
                        for hl4 in range(4):
                            hl = g4 * 4 + hl4
                            S.mm(PS[1 + g4][:, hl4 * 128:(hl4 + 1) * 128], Tm[:, hl, :], MakT[:, hl, :])
                    for g4 in range(2):
                        S.cp(Xm[:, g4 * 4:(g4 + 1) * 4, :], PS[1 + g4].a.r("p (a t) -> p a t", t=128), e="act" if g4 else "dve")
                    N1T = Xm
                    for g4 in range(2):
                        for hl4 in range(4):
                            hl = g4 * 4 + hl4
                            S.mm(PS[3 + g4][:, hl4 * 128:(hl4 + 1) * 128], N1T[:, hl, :], Mbr[:, hl, :])
                        S.tt(XTm[:, g4 * 4:(g4 + 1) * 4, :], Mkr[:, g4 * 4:(g4 + 1) * 4, :],
                             PS[3 + g4].a.r("p (a t) -> p a t", t=128), ALU.add)
                    Ny = XTm
                    for hl in range(8):
                        h = hh * 8 + hl
                        S.mm(PS[0][:, hl * 64:(hl + 1) * 64], N1T[:, hl, :], Bh[:, h * 64:(h + 1) * 64])
                    S.tt(Hm.a, PS[0].a.r("p (a k) -> p a k", k=64), Kh[:, hh * 512:(hh + 1) * 512].r("p (a k) -> p a k", k=64), ALU.add)
                    for c in range(2):
                        S.ts(ATc[:, c, :, :], ATt.a, cmask[:, c:c + 1], op0=ALU.mult, e="pool")
                        S.ts(Hc[:, c, :, :], Hm.a, cmask[:, c:c + 1], op0=ALU.mult, e="pool")
                    for g4 in range(2):
                        for hl4 in range(4):
                            hl = g4 * 4 + hl4
                            h = hh * 8 + hl
                            o = PS[3 + g4][0:64, hl4 * 128:(hl4 + 1) * 128]
                            S.mm(o, ATt[:, hl, :], Mbr[:, hl, :], True, False)
                            S.mm(o, Rt[:, h * 64:(h + 1) * 64], identF.a, False, True)
                        pv = PS[3 + g4][0:64, :].r("p (a t) -> p a t", t=128)
                        S.cp(RyC[:, 0, g4 * 4:(g4 + 1) * 4, 0:64], pv[:, :, 0:64], e="act")
                        S.cp(RyC[:, 1, g4 * 4:(g4 + 1) * 4, 64:128], pv[:, :, 64:128])
                    for c in range(2):
                        for hl in range(8):
                            h = hh * 8 + hl
                            S.mm(PS[1][0:64, hl * 64:(hl + 1) * 64], ATc[:, c, hl, :], Bh[:, h * 64:(h + 1) * 64])
                        S.mm(PS[2][0:64, :], Elast[:, c, :], eL[:, hh * 512:(hh + 1) * 512])
                        S.tt(pcd.a, PS[2][0:64, :].r("p (a k) -> p a k", k=64), identF[0:64, 0:64].us(1).bc([64, 8, 64]), ALU.mult)
                        S.tt(Gm[:, c, :, :], PS[1][0:64, :].r("p (a k) -> p a k", k=64), pcd.a, ALU.add)
                    if STAGE < 7:
                        continue
                    yp = yps[hh]
                    for hl in range(8):
                        h = hh * 8 + hl
                        S.mm(yp[:, hl * 64:(hl + 1) * 64], Ny[:, hl, :], zv[:, h * 64:(h + 1) * 64], hl == 0, False)
                    for c in range(2):
                        for hl in range(8):
                            h = hh * 8 + hl
                            S.mm(yp[:, hl * 64:(hl + 1) * 64], RyC[:, c, hl, :], Sst[:, h, :], False, c == 1 and hl == 7)
                        for hl in range(8):
                            h = hh * 8 + hl
                            o = PS[0][0:64, hl * 64:(hl + 1) * 64]
                            S.mm(o, Gm[:, c, hl, :], Sst[:, h, :], True, False)
                            S.mm(o, Hc[:, c, hl, :], zv[:, h * 64:(h + 1) * 64], False, True)
                        S.cp(Sst[:, hh * 8:(hh + 1) * 8, :], PS[0][0:64, :].r("p (a k) -> p a k", k=64), e="act")
                if STAGE < 8:
                    S.dma("sp", V(OUT, OUT.t[t0 + ti * 128:t0 + (ti + 1) * 128, :]), Mkr.a.r("p a t -> p (a t)"))
                    continue
                yv, t1, t2 = kk, bb, Lps
                S.cp(yv[:, 0:512], yps[0].a, e="act")
                S.cp(yv[:, 512:1024], yps[1].a)
                S.red(small[:, 48:64], h16(yv))
                S.ts(small[:, 48:64], small[:, 48:64], 1.0 / 64, op0=ALU.mult)
                S.tt(h16(yv), h16(yv), small[:, 48:64].us(2).bc([128, 16, 64]), ALU.subtract)
                S.tt(t1, yv, yv, ALU.mult)
                S.red(small[:, 64:80], h16(t1))
                S.act(small[:, 64:80], small[:, 64:80], AF.Sqrt, scale=1.0 / 64, bias=64e-5)
                S.recip(small[:, 80:96], small[:, 64:80])
                S.tt(h16(yv), h16(yv), small[:, 80:96].us(2).bc([128, 16, 64]), ALU.mult)
                S.tt(yv, yv, lnw.a, ALU.mult)
                S.tt(yv, yv, lnb.a, ALU.add)
                S.tt(t1, zr, kmod, ALU.mult)
                S.tt(t1, t1, rkb.a, ALU.mult)
                S.red(small[:, 96:112], h16(t1))
                S.tt(h16(t1), h16(zv), small[:, 96:112].us(2).bc([128, 16, 64]), ALU.mult)
                S.tt(yv, yv, t1, ALU.add)
                for h in range(2):
                    S.mm(PS[2 + h].a, sgT[:, 0, tsl], g2[:, 0, h * 512:(h + 1) * 512], True, False)
                    S.mm(PS[2 + h].a, sgT[0:32, 1, tsl], g2[0:32, 1, h * 512:(h + 1) * 512], False, True)
                    S.tt(t2[:, h * 512:(h + 1) * 512], yv[:, h * 512:(h + 1) * 512], PS[2 + h].a, ALU.mult)
                S.dma("sp", V(OUT, OUT.t[t0 + ti * 128:t0 + (ti + 1) * 128, :]), t2)
            S.cp(xnT[:, :, 0:1], xnT[:, :, 512:513], e="pool")
        S.barrier()
        print("L2 instructions:", S.n_inst)
    return nc


def build_l3(S_LEN, STG=99, ctx=None):
    NQ = S_LEN // 512
    fused = ctx is not None
    names = ["x", "norm_mix_pre", "w_in", "cmp_pos_k", "cmp_w1_k", "cmp_w2_k", "cmp_pos_v", "cmp_w1_v", "cmp_w2_v"]
    if not fused:
        nc = bass.Bass("TRN2", target_bir_lowering=False)
        ins = {}
        for n in names:
            shp = [S_LEN, D] if n == "x" else IN_SHAPES[n]
            ins[n] = nc.dram_tensor(n, shp, F32, kind="ExternalInput").ap()
        out_ap = nc.dram_tensor("ob", [S_LEN, D], F32, kind="ExternalOutput").ap()
    with ExitStack() as st:
        if not fused:
            S = Sched(nc, st)
            X = {n: S.ext(ins[n], n) for n in names}
            OUT = S.ext(out_ap, "ob")
        else:
            nc, S, X, OUT = ctx
            S.stack = st
        identF = S.sbuf([128, 128], F32, "identF")
        identB = S.sbuf([128, 128], BF16, "identB")
        S.ms(identF.a, 1.0, e="pool")
        S.asel(identF.a, identF.a, [[-1, 128]], ALU.is_equal, 0.0, 0, 1)
        S.cp(identB.a, identF.a)
        g_pre = S.sbuf([128, 8], F32, "gpre")
        S.dma("sp", g_pre.a, V(X["norm_mix_pre"], X["norm_mix_pre"].t[0, :].rearrange("(kc p) -> p kc", p=128)),
              allow_slow_non_contiguous=True)
        eps_t = S.sbuf([128, 1], F32, "eps")
        S.ms(eps_t.a, 1e-6)
        WK = [S.sbuf([128, 1024], F32, f"wk{i}") for i in range(4)]
        PS = [S.psum([128, 512], F32, f"psb{i}") for i in range(7)]
        PSB = S.psum([128, 1024], BF16, "psB")
        stg_b = [S.sbuf([128, 8, 128], BF16, f"stgb{i}") for i in range(2)]
        NC_ = 2608
        WinA = S.dram([128, 8, NC_], BF16, "WinA")
        for i, c0 in enumerate(range(0, NC_, 128)):
            cw = min(128, NC_ - c0)
            sf = WK[i % 4]
            sb_ = stg_b[i % 2]
            sview = sf.a.r("p (k c) -> p k c", c=128)
            srcap = X["w_in"].t[:, NS0 + c0:NS0 + c0 + cw].rearrange("(k p) c -> p k c", p=128)
            S.dma("sp" if i % 2 == 0 else "act", sview[:, :, 0:cw], V(X["w_in"], srcap))
            for k in range(8):
                S.ts(sb_[:, k, 0:cw], sview[:, k, 0:cw], g_pre[:, k:k + 1], op0=ALU.mult, e="dve" if k % 2 else "pool")
            S.dma("pool", WinA[:, :, c0:c0 + cw], sb_[:, :, 0:cw])
        KS = S.dram([67, 4, S_LEN], BF16, "KS")
        VS = S.dram([128, S_LEN // 128, 4, 65], BF16, "VS")

        Emat = S.sbuf([128, 64, 128], BF16, "Emat")
        S.ms(Emat.a, 1.0, e="pool")
        S.asel(Emat.a.r("p i (a b) -> p i a b", b=64), Emat.a.r("p i (a b) -> p i a b", b=64),
               [[-2, 64], [-1, 2], [0, 64]], ALU.is_equal, 0.0, 0, 1)
        NEGcur = S.sbuf([128, 4, 512], BF16, "NEGcur")
        NEGprev = S.sbuf([128, 4, 512], BF16, "NEGprev")
        NEGcmp = S.sbuf([128, 4, 512], BF16, "NEGcmp")
        S.ms(NEGcur.a, 0.0, e="pool")
        S.ms(NEGprev.a, 0.0, e="pool")
        S.ms(NEGcmp.a, 0.0, e="pool")
        for a in range(4):
            S.asel(NEGcur[:, a, :], NEGcur[:, a, :], [[1, 512]], ALU.is_ge, NEGM, -128 * a, -1)
            S.asel(NEGprev[:, a, :], NEGprev[:, a, :], [[-1, 512]], ALU.is_ge, NEGM, 128 * a - 1, 1)
            S.asel(NEGcmp[:, a, :], NEGcmp[:, a, :], [[1, 512]], ALU.is_ge, NEGM, 512 * a - 15, -16)
        Amap = S.sbuf([128, 4, 128], BF16, "Amap")
        A2 = S.sbuf([128, 4, 128], BF16, "A2")
        S.ms(Amap.a, 1.0, e="pool")
        S.ms(A2.a, 1.0, e="pool")
        for N in range(4):
            S.asel(Amap[:, N, :], Amap[:, N, :], [[-4, 128]], ALU.is_ge, 0.0, 128 * N - 1 + 1, 1)
            S.asel(Amap[:, N, :], Amap[:, N, :], [[4, 128]], ALU.is_ge, 0.0, 3 - (128 * N - 1), -1)
            S.asel(A2[:, N, :], A2[:, N, :], [[-4, 128]], ALU.is_ge, 0.0, 128 * N - 1, 1)
            S.asel(A2[:, N, :], A2[:, N, :], [[4, 128]], ALU.is_ge, 0.0, 2 - (128 * N - 1), -1)
        S.tt(Amap.a, Amap.a, A2.a, ALU.add, e="pool")
        pd_i = S.sbuf([128, 67], mybir.dt.int32, "pdi")
        pd = S.sbuf([128, 67], F32, "pd")
        S.op("pool", lambda g: g.iota(pd_i.t[:], pattern=[[-128, 67]], base=384, channel_multiplier=1), [], [pd_i.a])
        S.cp(pd.a, pd_i.a)
        biasSel = S.sbuf([128, 16, 67], F32, "biasSel")
        pc_i = S.sbuf([128, 16], mybir.dt.int32, "pci")
        pcf = S.sbuf([128, 16], F32, "pcf")
        S.op("pool", lambda g: g.iota(pc_i.t[:], pattern=[[-512, 16]], base=0, channel_multiplier=16), [], [pc_i.a])
        S.cp(pcf.a, pc_i.a)
        biasCmp = S.sbuf([128, 16, 16], F32, "biasCmp")
        biasCmp0 = S.sbuf([128, 16, 16], F32, "biasCmp0")
        for h in range(16):
            S.ts(biasSel[:, h, :], pd.a, SLOPES[h], op0=ALU.mult)
            S.ts(biasCmp[:, h, :], pcf.a, -0.5, SLOPES[h], ALU.add, ALU.mult)
        S.cp(biasCmp0.a, biasCmp.a)
        S.ms(biasCmp0[0:1, :, :], NEGM)
        tiebrk = S.sbuf([128, 128], F32, "tiebrk")
        tb_i = S.sbuf([128, 128], mybir.dt.int32, "tbi")
        S.op("pool", lambda g: g.iota(tb_i.t[:], pattern=[[1, 128]], base=0, channel_multiplier=0), [], [tb_i.a])
        S.cp(tiebrk.a, tb_i.a)
        S.ts(tiebrk.a, tiebrk.a, -1e-30, op0=ALU.mult)
        Qaug = S.sbuf([67, 16, 512], BF16, "Qaug")
        fi = S.sbuf([67, 512], mybir.dt.int32, "fi")
        ff = S.sbuf([67, 512], F32, "ff")
        S.op("pool", lambda g: g.iota(fi.t[:], pattern=[[1, 512]], base=0, channel_multiplier=0), [], [fi.a])
        S.cp(ff.a, fi.a)
        r0 = S.sbuf([67, 512], F32, "r0")
        r1 = S.sbuf([67, 512], F32, "r1")
        qs = S.sbuf([67, 3, 512], BF16, "qs")
        for h in range(16):
            S.ts(r0[64:65, :], ff[64:65, :], -SLOPES[h], op0=ALU.mult)
            S.cp(qs[64:65, 0, :], r0[64:65, :])
            S.tt(r1[64:65, :], r0[64:65, :], qs[64:65, 0, :], ALU.subtract)
            S.cp(qs[64:65, 1, :], r1[64:65, :])
            S.tt(r0[64:65, :], r1[64:65, :], qs[64:65, 1, :], ALU.subtract)
            S.cp(qs[64:65, 2, :], r0[64:65, :])
            for j in range(3):
                S.dma("sp", Qaug[64 + j:65 + j, h, :], qs[64:65, j, :])

        def aug_ones(tile_v):
            S.ms(tile_v, 1.0)

        KSst = S.sbuf([67, 4, 512], BF16, "KSst")
        aug_ones(KSst[64:67, :, :])
        kwT = S.sbuf([67, 4, 2, 512], BF16, "kwT")
        aug_ones(kwT[64:67, :, :, :])
        kcmpT = S.sbuf([67, 4, 512], BF16, "kcmpT")
        S.ms(kcmpT.a, 0.0)
        aug_ones(kcmpT[64:67, :, :])
        VSst = S.sbuf([128, 4, 4, 65], BF16, "VSst")
        S.ms(VSst.a, 1.0)
        vwA = S.sbuf([128, 2, 4, 4, 65], BF16, "vwA")
        S.ms(vwA.a, 1.0)
        vcmpA = S.sbuf([128, 4, 4, 65], BF16, "vcmpA")
        S.ms(vcmpA.a, 0.0)
        S.ms(vcmpA[:, :, :, 64:65], 1.0)
        kcT = S.sbuf([64, 4, 528], BF16, "kcT")
        vcT = S.sbuf([64, 4, 528], BF16, "vcT")
        S.ms(kcT.a, 0.0)
        S.ms(vcT.a, 0.0)
        W1c = {}
        W2c = {}
        pbias = {}
        for kv in ("k", "v"):
            w1 = S.sbuf([64, 32, 128], BF16, "W1" + kv)
            S.dma("pool", w1.a, V(X["cmp_w1_" + kv], X["cmp_w1_" + kv].t[:, :].rearrange("(l d) h -> d l h", d=64)))
            W1c[kv] = w1
            w2 = S.sbuf([128, 64], BF16, "W2" + kv)
            S.dma("pool", w2.a, V(X["cmp_w2_" + kv], X["cmp_w2_" + kv].t[:, :]))
            W2c[kv] = w2
            w1f = WK[0].a.r("p (c h) -> p c h", h=128)[:, 0:16 // 2, :]
            pb = S.sbuf([128, 1], F32, "pb" + kv)
            posT = S.sbuf([128, 16], F32, "posT" + kv)
            S.dma("sp", posT.a, V(X["cmp_pos_" + kv], X["cmp_pos_" + kv].t[0, :].rearrange("(c p) -> p c", p=128)),
                  allow_slow_non_contiguous=True)
            for half in range(2):
                S.dma("sp", w1f, V(X["cmp_w1_" + kv], X["cmp_w1_" + kv].t[half * 1024:(half + 1) * 1024, :].rearrange("(c p) h -> p c h", p=128)))
                for c in range(8):
                    S.mm(PS[0][:, 0:1], w1f[:, c, :], posT[:, half * 8 + c:half * 8 + c + 1], half == 0 and c == 0, half == 1 and c == 7)
            S.cp(pb.a, PS[0][:, 0:1])
            pbias[kv] = pb

        xnT = S.sbuf([128, 8, 512], BF16, "xnT")
        ring = [S.sbuf([128, 8, 512], BF16, f"ring{i}") for i in range(3)]
        ring_i = [0]
        small = S.sbuf([128, 64], F32, "small")
        xb = S.sbuf([128, 1024], BF16, "xb")
        gates = S.sbuf([128, 4, 48], F32, "gates")
        obuf = S.sbuf([128, 4, 1024], F32, "obuf")
        NBT = S.sbuf([128, 4, 512], BF16, "NBT")
        Kc = [S.sbuf([67, 512], BF16, f"Kc{i}") for i in range(2)]
        Vc = [S.sbuf([128, 4, 65], BF16, f"Vc{i}") for i in range(2)]
        Pt = [S.sbuf([128, 512], BF16, f"Pt{i}") for i in range(2)]
        Sb = [S.sbuf([128, 512], F32, f"Sb{i}") for i in range(2)]
        p_i = [0]
        hid = S.sbuf([128, 4, 32], F32, "hid")
        hidpad = S.sbuf([128, 128], BF16, "hidpad")
        S.ms(hidpad.a, 0.0)
        sc = S.sbuf([128, 128], F32, "sc")
        sc2 = S.sbuf([128, 128], F32, "sc2")
        mx = S.sbuf([128, 16], F32, "mx")
        rs = S.sbuf([128, 64], F32, "rs")

        def slab(c0, cw):
            r = ring[ring_i[0] % len(ring)]
            ring_i[0] += 1
            q = ["sp", "act"][ring_i[0] % 2]
            S.dma(q, r[:, :, 0:cw], WinA[:, :, c0 - NS0:c0 - NS0 + cw])
            return r

        def attn_tile(h, lhsT, masks, bias_ap, vrhs, acc, first, last, imp=None, imp_rhs=None, s_lo=0, s_hi=4):
            ps = PS[p_i[0] % 2]
            pt = Pt[p_i[0] % 2]
            sb_ = Sb[p_i[0] % 2]
            p_i[0] += 1
            fs = slice(128 * s_lo, 128 * s_hi)
            S.mm(ps[:, fs], lhsT, Qaug[:, h, fs], True, len(masks) == 0)
            for mi, (ml, mr) in enumerate(masks):
                S.mm(ps[:, fs], ml, mr[:, fs], False, mi == len(masks) - 1)
            S.ts(sb_[:, fs], ps[:, fs], bias_ap, op0=ALU.add)
            S.act(pt[:, fs], sb_[:, fs], AF.Exp)
            for s in range(s_lo, s_hi):
                S.mm(acc[:, s * 65:(s + 1) * 65], pt[:, s * 128:(s + 1) * 128], vrhs,
                     first and s == s_lo, last and s == s_hi - 1)
            if imp is not None:
                for s in range(4):
                    S.mm(imp[:, s * 128:(s + 1) * 128], pt[:, s * 128:(s + 1) * 128], imp_rhs, first and s == 0, last and s == 3)

        def finish(h, br, acc, first_branch):
            accv = acc[:, 0:260].r("p (s c) -> p s c", c=65)
            S.ts(rs[:, 0:4], accv[:, :, 64], 1e-30, op0=ALU.max)
            S.recip(rs[:, 4:8], rs[:, 0:4])
            S.tt(rs[:, 8:12], rs[:, 4:8], gates[:, :, 3 * h + br], ALU.mult)
            for s in range(4):
                o = obuf[:, s, h * 64:(h + 1) * 64]
                if first_branch:
                    S.ts(o, accv[:, s, 0:64], rs[:, 8 + s:9 + s], op0=ALU.mult)
                else:
                    S.stt(o, accv[:, s, 0:64], rs[:, 8 + s:9 + s], o, ALU.mult, ALU.add)

        for Q in range(NQ):
            t0 = Q * 512
            par = Q % 2
            for ti in range(4):
                xt = WK[0].a
                S.dma("sp", xt, V(X["x"], X["x"].t[t0 + ti * 128:t0 + (ti + 1) * 128, :]))
                S.act(WK[1].a, xt, AF.Square, accum=small[:, 0:1])
                S.act(small[:, 1:2], small[:, 0:1], AF.Sqrt, scale=1.0 / 1024, bias=1e-6)
                S.recip(small[:, 0:1], small[:, 1:2])
                S.ts(xb.a, xt, small[:, 0:1], op0=ALU.mult)
                for kc in range(8):
                    S.tr(PSB[:, kc * 128:(kc + 1) * 128], xb[:, kc * 128:(kc + 1) * 128], identB.a)
                S.cp(xnT[:, :, ti * 128:(ti + 1) * 128], PSB.a.r("p (k t) -> p k t", t=128), e="act")
            S.cp(kcT[:, :, 0:16], kcT[:, :, 512:528], e="pool")
            S.cp(vcT[:, :, 0:16], vcT[:, :, 512:528], e="pool")

            def fm_proj(c0, M, dst, scale=1.0):
                sl_ = slab(c0, M)
                for kc in range(8):
                    S.mm(PS[0][0:M, :], sl_[:, kc, 0:M], xnT[:, kc, :], kc == 0, kc == 7)
                S.act(dst, PS[0][0:M, :], AF.Copy, scale=scale)

            for h in range(16):
                fm_proj(C_Q + h * 64, 64, Qaug[0:64, h, :], 0.125)
            for g in range(4):
                fm_proj(C_KS + g * 64, 64, KSst[0:64, g, :])
                fm_proj(C_KW + g * 64, 64, kwT[0:64, g, par, :])
                fm_proj(C_KC + g * 64, 64, kcT[:, g, 16:528])
                fm_proj(C_VC + g * 64, 64, vcT[:, g, 16:528])
            S.dma("sp", KS[:, :, t0:t0 + 512], KSst.a)
            s_vs = slab(C_VS, 256)
            s_vw = slab(C_VW, 304)
            for ti in range(4):
                tsl = slice(ti * 128, (ti + 1) * 128)
                for kc in range(8):
                    S.mm(PS[0][:, 0:256], xnT[:, kc, tsl], s_vs[:, kc, 0:256], kc == 0, kc == 7)
                S.cp(VSst[:, ti, :, 0:64], PS[0][:, 0:256].r("p (g d) -> p g d", d=64), e="act")
                for kc in range(8):
                    S.mm(PS[1][:, 0:304], xnT[:, kc, tsl], s_vw[:, kc, 0:304], kc == 0, kc == 7)
                S.cp(vwA[:, par, ti, :, 0:64], PS[1][:, 0:256].r("p (g d) -> p g d", d=64))
                S.act(gates[:, ti, :], PS[1][:, 256:304], AF.Sigmoid)
            S.dma("act", VS[:, Q * 4:(Q + 1) * 4, :, :], VSst.a)

            if STG < 1:
                continue
            i0 = 1 if Q == 0 else 0
            ni = 32 - i0
            for kv, src in (("k", kcT), ("v", vcT)):
                for g in range(4):
                    for l in range(32):
                        S.mm(PS[0][:, 0:ni], W1c[kv][:, l, :], src[:, g, l + 16 * i0:l + 16 * 31 + 1:16], l == 0, l == 31)
                    xs = hid[:, 0, 0:ni]
                    tq = hid[:, 1, 0:ni]
                    S.ts(xs, PS[0][:, 0:ni], pbias[kv][:, 0:1], op0=ALU.add)
                    S.tt(tq, xs, xs, ALU.mult)
                    S.ts(tq, tq, 0.044715, 1.0, ALU.mult, ALU.add)
                    S.tt(tq, tq, xs, ALU.mult)
                    S.act(tq, tq, AF.Sigmoid, scale=1.5957691216057308)
                    pr = 32 * (Q % 4)
                    S.tt(hidpad[:, pr + i0:pr + 32], xs, tq, ALU.mult)
                    if kv == "k":
                        S.mm(PS[1][0:64, 0:32], W2c[kv].a, hidpad[:, pr:pr + 32])
                        S.cp(kcmpT[0:64, g, 32 * Q:32 * Q + 32], PS[1][0:64, 0:32], e="act")
                    else:
                        S.mm(PS[1][:, 0:64], hidpad.a, W2c[kv].a)
                        S.cp(vcmpA[pr:pr + 32, Q // 4, g, 0:64], PS[1][pr:pr + 32, 0:64], e="act")

            if STG < 2:
                continue
            NN = Q // 4 + 1
            for g in range(4):
                for hl in range(4):
                    h = g * 4 + hl
                    acc = PS[2]
                    imp = PS[3 + (hl % 2)]
                    for N in range(NN):
                        d = Q - 4 * N
                        masks = [(identB.a, NEGcmp[:, d, :])] if d <= 3 else []
                        btab = biasCmp0 if N == 0 else biasCmp
                        attn_tile(h, kcmpT[:, g, N * 128:(N + 1) * 128], masks, btab[:, h, d:d + 1],
                                  vcmpA[:, N, g, :], acc, N == 0, N == NN - 1, imp=imp, imp_rhs=Amap[:, N, :])
                    finish(h, 0, acc, True)
                    for s in range(4):
                        scg = WK[2 + (s // 2)][:, (s % 2) * 512 + g * 128:(s % 2) * 512 + (g + 1) * 128]
                        S.stt(scg, imp[:, s * 128:(s + 1) * 128], rs[:, 4 + s:5 + s], tiebrk.a if hl == 0 else scg, ALU.mult, ALU.add)
                c0b = Q * 8
                for s in range(4):
                    scg = WK[2 + (s // 2)][:, (s % 2) * 512 + g * 128:(s % 2) * 512 + (g + 1) * 128]
                    cb = c0b + 2 * s
                    S.cp(sc.a, scg)
                    if cb + 2 < 128:
                        S.ms(sc[:, cb + 2:128], -1e30)
                    S.ms(sc[0:64, cb + 1:cb + 2], -1e30)
                    S.ms(sc[:, 0:1], 1e9)
                    S.ms(sc[:, cb:cb + 1], 1e9)
                    if cb >= 1:
                        S.ms(sc[0:64, cb - 1:cb], 1e9)
                    S.ms(sc[64:128, cb + 1:cb + 2], 1e9)
                    S.op("dve", lambda e_: e_.max(out=mx.t[:, 0:8], in_=sc.t[:]), [sc.a], [mx.a])
                    S.op("dve", lambda e_: e_.match_replace(out=sc2.t[:], in_to_replace=mx.t[:, 0:8], in_values=sc.t[:], imm_value=-1e30),
                         [sc.a, mx.a], [sc2.a])
                    S.op("dve", lambda e_: e_.max(out=mx.t[:, 8:16], in_=sc2.t[:]), [sc2.a], [mx.a])
                    S.ts(sc2.a, sc.a, mx[:, 15:16], op0=ALU.is_ge)
                    S.tr(PS[5][:, 0:128], sc2.a, identF.a)
                    S.act(NBT[:, g, s * 128:(s + 1) * 128], PS[5][:, 0:128], AF.Identity, scale=-NEGM, bias=NEGM)

            if STG < 3:
                continue
            def cmin(h):
                keep = int((150.0 / SLOPES[h] - 1.0) // 512) + 1
                return max(0, Q - keep)
            for g in range(4):
                accs = [PS[2], PS[3], PS[4], PS[6]]
                cm = [cmin(g * 4 + hl) for hl in range(4)]
                for C in range(min(cm), Q + 1):
                    kc_ = Kc[C % 2]
                    vc_ = Vc[C % 2]
                    S.dma("sp", kc_.a, KS[:, g, C * 512:(C + 1) * 512])
                    S.dma("act", vc_.a, VS[:, C * 4:(C + 1) * 4, g, :])
                    for hl in range(4):
                        h = g * 4 + hl
                        if C < cm[hl]:
                            continue
                        for a in range(4):
                            i = 4 * C + a
                            masks = [(Emat[:, i, :], NBT[:, g, :])]
                            if C == Q:
                                masks.append((identB.a, NEGcur[:, a, :]))
                            dd = 4 * (Q - C) - a
                            attn_tile(h, kc_[:, a * 128:(a + 1) * 128], masks, biasSel[:, h, dd + 3:dd + 4],
                                      vc_[:, a, :], accs[hl], C == cm[hl] and a == 0, C == Q and a == 3,
                                      s_lo=(a if C == Q else 0))
                for hl in range(4):
                    finish(g * 4 + hl, 1, accs[hl], False)
            if STG < 4:
                if STG == 3:
                    for ti in range(4):
                        S.dma('sp', V(OUT, OUT.t[t0 + ti * 128:t0 + (ti + 1) * 128, :]), obuf[:, ti, :])
                continue
            for h in range(16):
                g = h // 4
                acc = PS[2 + (h % 2)]
                tiles = []
                if Q > 0:
                    for a in range(4):
                        tiles.append((kwT[:, g, 1 - par, a * 128:(a + 1) * 128], NEGprev[:, a, :], 4 - a, vwA[:, 1 - par, a, g, :], 0, a + 1))
                for a in range(4):
                    tiles.append((kwT[:, g, par, a * 128:(a + 1) * 128], NEGcur[:, a, :], -a, vwA[:, par, a, g, :], a, 4))
                if Q > 0:
                    pass
                for k_, (kl, mk, dd, vr, slo, shi) in enumerate(tiles):
                    attn_tile(h, kl, [(identB.a, mk)], biasSel[:, h, dd + 3:dd + 4], vr, acc, k_ == 0, k_ == len(tiles) - 1,
                              s_lo=slo, s_hi=shi)
                finish(h, 2, acc, False)
            for ti in range(4):
                S.dma("sp", V(OUT, OUT.t[t0 + ti * 128:t0 + (ti + 1) * 128, :]), obuf[:, ti, :])
        S.barrier()
        print("L3 instructions:", S.n_inst)
    return nc


_CACHE = {}


def build_fused(S_LEN):
    nc = bass.Bass("TRN2", target_bir_lowering=False)
    ins = {}
    for n in IN_NAMES:
        shp = [S_LEN, D] if n == "x" else IN_SHAPES[n]
        ins[n] = nc.dram_tensor(n, shp, F32, kind="ExternalInput").ap()
    out_ap = nc.dram_tensor("out", [S_LEN, D], F32, kind="ExternalOutput").ap()
    with ExitStack() as outer:
        S = Sched(nc, outer)
        X = {n: S.ext(ins[n], n) for n in IN_NAMES}
        OUT = S.ext(out_ap, "out")
        OA = S.dram([S_LEN, D], F32, "oa_s")
        OB = S.dram([S_LEN, D], F32, "ob_s")
        build_l2(S_LEN, ctx=(nc, S, X, OA))
        build_l3(S_LEN, ctx=(nc, S, X, OB))
        build_l4(S_LEN, ctx=(nc, S, X, OUT, OA, OB))
    return nc


def kernel(**inputs):
    x = np.asarray(inputs["x"], dtype=np.float32)
    B, S_LEN = x.shape[0], x.shape[1]
    if S_LEN not in _CACHE:
        _CACHE[S_LEN] = build_fused(S_LEN)
    nc = _CACHE[S_LEN]
    shared = {n: np.ascontiguousarray(np.asarray(inputs[n], dtype=np.float32).reshape(IN_SHAPES[n]))
              for n in IN_NAMES if n != "x"}
    maps = [dict(shared, x=np.ascontiguousarray(x[b])) for b in range(B)]
    res = run_bass_kernel_spmd(nc, maps, core_ids=list(range(B)))
    return np.stack([np.asarray(r["out"], dtype=np.float32) for r in res.results], axis=0)
```

```python
import math
from contextlib import ExitStack
import numpy as np
import concourse.bass as bass
import concourse.mybir as mybir
from concourse.bass_utils import run_bass_kernel_spmd

F32 = mybir.dt.float32
BF16 = mybir.dt.bfloat16
AF = mybir.ActivationFunctionType
ALU = mybir.AluOpType
AX = mybir.AxisListType

D = 1024
NEGM = -30000.0
ROLL_LIMIT = 40000


class V:
    def __init__(self, t, ap):
        self.t = t
        self.ap = ap

    def __getitem__(self, idx):
        return V(self.t, self.ap[idx])

    def r(self, pat, **kw):
        return V(self.t, self.ap.rearrange(pat, **kw))

    def bc(self, shape):
        return V(self.t, self.ap.to_broadcast(list(shape)))

    def us(self, d):
        return V(self.t, self.ap.unsqueeze(d))


class T:
    def __init__(self, name, t, space):
        self.name = name
        self.t = t
        self.space = space
        self.writers = {}
        self.readers = {}

    def __getitem__(self, idx):
        return V(self, self.t[idx])

    @property
    def a(self):
        return V(self, self.t[:])


def _ap(x):
    return x.ap if isinstance(x, V) else x


class Sched:
    def __init__(self, nc, stack):
        self.nc = nc
        self.stack = stack
        self.sem_stack = stack
        self.dsems = []
        self.eng = {"pe": nc.tensor, "act": nc.scalar, "dve": nc.vector,
                    "pool": nc.gpsimd, "sp": nc.sync}
        self.sem = {}
        self.cnt = {}
        for k in self.eng:
            self.sem[k] = stack.enter_context(nc.semaphore("s_" + k))
            self.cnt[k] = 0
        self.waited = {k: {} for k in self.eng}
        self.key = {k: k for k in self.eng}
        self.epoch = {k: 0 for k in self.eng}
        self.prev = {k: None for k in self.eng}
        self.n_inst = 0
        self.uid = 0
        self.sb_bytes = 0

    def sbuf(self, shape, dtype, name=None):
        self.uid += 1
        name = (name or "sb") + str(self.uid)
        t = self.stack.enter_context(self.nc.sbuf_tensor(name, list(shape), dtype))
        self.sb_bytes += int(np.prod(shape[1:])) * (4 if dtype == F32 else 2)
        return T(name, t, "sbuf")

    def psum(self, shape, dtype, name=None):
        self.uid += 1
        name = (name or "ps") + str(self.uid)
        t = self.stack.enter_context(self.nc.psum_tensor(name, list(shape), dtype))
        return T(name, t, "psum")

    def dram(self, shape, dtype, name=None):
        self.uid += 1
        name = (name or "dr") + str(self.uid)
        t = self.nc.dram_tensor(name, list(shape), dtype, kind="Internal")
        return T(name, t.ap(), "dram")

    def ext(self, ap, name):
        return T(name, ap, "dram")

    def _wait(self, e, deps):
        w = self.waited[e]
        for key, (sem, val) in deps.items():
            if e == "pe" and key.split("#")[0] == "pe":
                continue
            if w.get(key, 0) < val:
                self.eng[e].wait_ge(sem, val)
                w[key] = val

    @staticmethod
    def _merge(d, key, sem, val):
        if key not in d or d[key][1] < val:
            d[key] = (sem, val)

    def _deps(self, reads, writes):
        deps = {}
        for t in reads:
            for k, (s, v) in t.writers.items():
                self._merge(deps, k, s, v)
        for t in writes:
            for k, (s, v) in t.writers.items():
                self._merge(deps, k, s, v)
            for k, (s, v) in t.readers.items():
                self._merge(deps, k, s, v)
        return deps

    def op(self, e, fn, reads, writes):
        reads = list({id(x.t): x.t for x in reads if isinstance(x, V)}.values())
        writes = list({id(x.t): x.t for x in writes if isinstance(x, V)}.values())
        if self.cnt[e] >= ROLL_LIMIT:
            self._roll(e)
        self._wait(e, self._deps(reads, writes))
        inst = fn(self.eng[e])
        self.cnt[e] += 1
        inst.then_inc(self.sem[e], 1)
        v = self.cnt[e]
        kk_ = self.key[e]
        for t in reads:
            self._merge(t.readers, kk_, self.sem[e], v)
        for t in writes:
            self._merge(t.writers, kk_, self.sem[e], v)
        self.n_inst += 1
        return inst

    def dma(self, q, out, in_, **kw):
        reads = [in_.t]
        writes = [out.t]
        self._wait(q, self._deps(reads, writes))
        sb = out.t if out.t.space == "sbuf" else in_.t
        if not hasattr(sb, "dsem"):
            self.uid += 1
            sb.dsem = {"name": sb.name, "val": 0,
                       "sem": self.sem_stack.enter_context(self.nc.semaphore(f"d{self.uid}"))}
            self.dsems.append(sb.dsem)
        sem = sb.dsem
        key = "dma_" + sem["name"]
        sem["val"] += 16
        inst = self.eng[q].dma_start(out=out.ap, in_=in_.ap, **kw)
        inst.then_inc(sem["sem"], 16)
        for t in reads:
            self._merge(t.readers, key, sem["sem"], sem["val"])
        for t in writes:
            self._merge(t.writers, key, sem["sem"], sem["val"])
        self.n_inst += 1
        return inst

    def _roll(self, e):
        self.prev[e] = (self.key[e], self.sem[e], self.cnt[e])
        self.epoch[e] += 1
        self.key[e] = f"{e}#{self.epoch[e]}"
        self.uid += 1
        self.sem[e] = self.sem_stack.enter_context(self.nc.semaphore(f"s_{e}_{self.uid}"))
        self.cnt[e] = 0

    def barrier(self):
        for e in self.eng:
            for o in self.eng:
                if o == e:
                    continue
                cands = [(self.key[o], self.sem[o], self.cnt[o])]
                if self.prev[o] is not None:
                    cands.append(self.prev[o])
                for (ko, so, co) in cands:
                    if co > 0 and self.waited[e].get(ko, 0) < co:
                        self.eng[e].wait_ge(so, co)
                        self.waited[e][ko] = co
            for d in self.dsems:
                key = "dma_" + d["name"]
                if d["val"] > 0 and self.waited[e].get(key, 0) < d["val"]:
                    self.eng[e].wait_ge(d["sem"], d["val"])
                    self.waited[e][key] = d["val"]

    def mm(self, out, lhsT, rhs, start=True, stop=True):
        return self.op("pe", lambda e: e.matmul(out.ap, lhsT=lhsT.ap, rhs=rhs.ap, start=start, stop=stop),
                       [lhsT, rhs], [out])

    def tr(self, out, in_, ident):
        return self.op("pe", lambda e: e.transpose(out=out.ap, in_=in_.ap, identity=ident.ap), [in_, ident], [out])

    def act(self, out, in_, func, scale=1.0, bias=0.0, accum=None, e="act"):
        kw = {}
        if accum is not None:
            kw["accum_out"] = accum.ap
        return self.op(e, lambda g: g.activation(out=out.ap, in_=in_.ap, func=func, scale=_ap(scale), bias=_ap(bias), **kw),
                       [in_, scale, bias], [out, accum])

    def tt(self, out, a, b, op, e="dve"):
        return self.op(e, lambda g: g.tensor_tensor(out=out.ap, in0=a.ap, in1=b.ap, op=op), [a, b], [out])

    def ts(self, out, a, s1, s2=None, op0=ALU.mult, op1=None, e="dve"):
        if op1 is None:
            return self.op(e, lambda g: g.tensor_scalar(out=out.ap, in0=a.ap, scalar1=_ap(s1), scalar2=None, op0=op0),
                           [a, s1], [out])
        return self.op(e, lambda g: g.tensor_scalar(out=out.ap, in0=a.ap, scalar1=_ap(s1), scalar2=_ap(s2), op0=op0, op1=op1),
                       [a, s1, s2], [out])

    def stt(self, out, a, s, b, op0, op1):
        return self.op("dve", lambda g: g.scalar_tensor_tensor(out=out.ap, in0=a.ap, scalar=_ap(s), in1=b.ap, op0=op0, op1=op1),
                       [a, s, b], [out])

    def cp(self, out, a, e="dve"):
        if e == "act":
            return self.act(out, a, AF.Copy)
        return self.op(e, lambda g: g.tensor_copy(out=out.ap, in_=a.ap), [a], [out])

    def ms(self, out, val, e="dve"):
        return self.op(e, lambda g: g.memset(out.ap, val), [], [out])

    def red(self, out, a, op=ALU.add):
        return self.op("dve", lambda g: g.tensor_reduce(out=out.ap, in_=a.ap, axis=AX.X, op=op), [a], [out])

    def recip(self, out, a):
        return self.op("dve", lambda g: g.reciprocal(out=out.ap, in_=a.ap), [a], [out])

    def asel(self, out, in_, pattern, cmp, fill, base, cm):
        return self.op("pool", lambda g: g.affine_select(out=out.ap, in_=in_.ap, pattern=pattern, compare_op=cmp,
                                                         fill=fill, base=base, channel_multiplier=cm), [in_], [out])


IN_NAMES = ["x", "norm_mix_pre", "norm_mix_post", "norm_ffn_pre", "norm_ffn_post", "w_in",
            "rwkv_mu", "rwkv_w0", "rwkv_w2", "rwkv_a0", "rwkv_a2", "rwkv_g2", "rwkv_k_k", "rwkv_k_a",
            "rwkv_r_k", "rwkv_ln_w", "rwkv_ln_b",
            "cmp_pos_k", "cmp_w1_k", "cmp_w2_k", "cmp_pos_v", "cmp_w1_v", "cmp_w2_v",
            "w_branch_rwkv", "w_branch_nsa", "w_out", "ffn_up", "ffn_conv_w", "ffn_conv_b", "ffn_down"]
IN_SHAPES = {
    "norm_mix_pre": [1, 1024], "norm_mix_post": [1, 1024], "norm_ffn_pre": [1, 1024], "norm_ffn_post": [1, 1024],
    "w_in": [1024, 8016], "rwkv_mu": [1, 3360], "rwkv_w0": [1, 1024], "rwkv_w2": [64, 1024], "rwkv_a0": [1, 1024],
    "rwkv_a2": [64, 1024], "rwkv_g2": [160, 1024], "rwkv_k_k": [1, 1024], "rwkv_k_a": [1, 1024], "rwkv_r_k": [1, 1024],
    "rwkv_ln_w": [1, 1024], "rwkv_ln_b": [1, 1024], "cmp_pos_k": [1, 2048], "cmp_w1_k": [2048, 128], "cmp_w2_k": [128, 64],
    "cmp_pos_v": [1, 2048], "cmp_w1_v": [2048, 128], "cmp_w2_v": [128, 64], "w_branch_rwkv": [1024, 1024],
    "w_branch_nsa": [1024, 1024], "w_out": [1024, 1024], "ffn_up": [1024, 5632], "ffn_conv_w": [3, 5632],
    "ffn_conv_b": [1, 5632], "ffn_down": [2816, 1024],
}
C_R, C_K, C_V, C_WD, C_AD, C_GD = 0, 1024, 2048, 3072, 3136, 3200
NS0 = 3360
C_Q, C_KC, C_VC, C_KS, C_VS, C_KW, C_VW, C_GL = NS0, NS0 + 1024, NS0 + 1280, NS0 + 1536, NS0 + 1792, NS0 + 2048, NS0 + 2304, NS0 + 2560
C_GA, C_GB = 5968, 6992
SLOPES = [2.0 ** (-0.5 * h) for h in range(1, 17)]
CDEC = math.exp(-0.5)


def build_l4(S_LEN, parts=(), ctx=None):
    NQ = S_LEN // 512
    fused = ctx is not None
    if not fused:
        nc = bass.Bass("TRN2", target_bir_lowering=False)
        ins = {}
        for n in IN_NAMES:
            shp = [S_LEN, D] if n == "x" else IN_SHAPES[n]
            ins[n] = nc.dram_tensor(n, shp, F32, kind="ExternalInput").ap()
        out_ap = nc.dram_tensor("out", [S_LEN, D], F32, kind="ExternalOutput").ap()
        oa_ap = nc.dram_tensor("oaT", [D, S_LEN], F32, kind="ExternalInput").ap()
        ob_ap = nc.dram_tensor("obT", [D, S_LEN], F32, kind="ExternalInput").ap()

    with ExitStack() as st:
        if not fused:
            S = Sched(nc, st)
            X = {n: S.ext(ins[n], n) for n in IN_NAMES}
            OUT = S.ext(out_ap, "out")
            OA = S.ext(oa_ap, "oaT")
            OB = S.ext(ob_ap, "obT")
        else:
            nc, S, X, OUT, OA, OB = ctx
            S.stack = st

        identF = S.sbuf([128, 128], F32, "identF")
        identB = S.sbuf([128, 128], BF16, "identB")
        S.ms(identF.a, 1.0, e="pool")
        S.asel(identF.a, identF.a, [[-1, 128]], ALU.is_equal, 0.0, 0, 1)
        S.cp(identB.a, identF.a)

        def bload(name, n, q="sp"):
            t = S.sbuf([128, n], F32, name)
            S.dma(q, t.a, V(X[name], X[name].t[0:1, :].to_broadcast([128, n])))
            return t

        def pload(name, kc, q="sp"):
            t = S.sbuf([128, kc], F32, name + "_p")
            src = X[name].t[0, :].rearrange("(kc p) -> p kc", p=128)
            S.dma(q, t.a, V(X[name], src), allow_slow_non_contiguous=True)
            return t

        g_pre = pload("norm_mix_pre", 8)
        g_fpre = pload("norm_ffn_pre", 8)
        g_post = bload("norm_mix_post", 1024)
        g_fpost = bload("norm_ffn_post", 1024)
        eps_t = S.sbuf([128, 1], F32, "eps")
        S.ms(eps_t.a, 1e-6)

        WinA = S.dram([128, 8, 8016], BF16, "WinA")
        WinB = S.dram([128, 8, 3360], BF16, "WinB")
        Wbra = S.dram([128, 8, 1024], BF16, "Wbra")
        Wbrb = S.dram([128, 8, 1024], BF16, "Wbrb")
        Wout = S.dram([128, 8, 1024], BF16, "Wout")
        Wup = S.dram([128, 8, 5632], BF16, "Wup")
        Wdn = S.dram([128, 22, 1024], BF16, "Wdn")

        NWK = 14
        WK = [S.sbuf([128, 1024], F32, f"wk{i}") for i in range(NWK)]
        PS = [S.psum([128, 512], F32, f"psb{i}") for i in range(7)]
        PSB = S.psum([128, 1024], BF16, "psB")
        stg_b = [S.sbuf([128, 8, 128], BF16, f"stgb{i}") for i in range(2)]

        mu_bc = WK[12]
        omu_bc = WK[13]
        step_i = [0]

        def prepass(src_name, ncols, kcs, dst, gain, colscale=None, c_src0=0):
            src = X[src_name]
            for c0 in range(0, ncols, 128):
                cw = min(128, ncols - c0)
                for k0 in range(0, kcs, 8):
                    kn = min(8, kcs - k0)
                    i = step_i[0]
                    step_i[0] += 1
                    sf = WK[i % 4]
                    sb_ = stg_b[i % 2]
                    sview = sf.a.r("p (k c) -> p k c", c=128)
                    srcap = src.t[k0 * 128:(k0 + kn) * 128, c_src0 + c0:c_src0 + c0 + cw].rearrange("(k p) c -> p k c", p=128)
                    S.dma("sp" if i % 2 == 0 else "act", sview[:, 0:kn, 0:cw], V(src, srcap))
                    for k in range(kn):
                        e = "dve" if k % 2 == 0 else "pool"
                        if gain is None:
                            S.cp(sb_[:, k, 0:cw], sview[:, k, 0:cw], e=e)
                        elif colscale is None:
                            S.ts(sb_[:, k, 0:cw], sview[:, k, 0:cw], gain[:, k0 + k:k0 + k + 1], op0=ALU.mult, e=e)
                        else:
                            S.stt(sb_[:, k, 0:cw], sview[:, k, 0:cw], gain[:, k0 + k:k0 + k + 1],
                                  colscale[:, c0:c0 + cw], ALU.mult, ALU.mult)
                    S.dma("pool", dst[:, k0:k0 + kn, c0:c0 + cw], sb_[:, 0:kn, 0:cw])

        if "rwkv" in parts:
            for cb in range(0, 3360, 1024):
                cbw = min(1024, 3360 - cb)
                S.dma("sp", mu_bc[:, 0:cbw], V(X["rwkv_mu"], X["rwkv_mu"].t[0:1, cb:cb + cbw].to_broadcast([128, cbw])))
                S.ts(omu_bc[:, 0:cbw], mu_bc[:, 0:cbw], -1.0, 1.0, ALU.mult, ALU.add)
                prepass("w_in", cbw, 8, V(WinA, WinA.t[:, :, cb:cb + cbw]), g_pre, omu_bc, c_src0=cb)
                prepass("w_in", cbw, 8, V(WinB, WinB.t[:, :, cb:cb + cbw]), g_pre, mu_bc, c_src0=cb)
        prepass("w_in", 8016 - C_GA, 8, V(WinA, WinA.t[:, :, C_GA:8016]), g_pre, None, c_src0=C_GA)
        prepass("w_branch_rwkv", 1024, 8, Wbra, None)
        prepass("w_branch_nsa", 1024, 8, Wbrb, None)
        prepass("w_out", 1024, 8, Wout, None)
        prepass("ffn_up", 5632, 8, Wup, g_fpre)
        prepass("ffn_down", 1024, 22, Wdn, None)

        xnT = S.sbuf([128, 8, 513], BF16, "xnT")
        S.ms(xnT[:, :, 0:1], 0.0)
        xn2T = xnT[:, :, 1:513]
        oaT = S.sbuf([128, 8, 512], BF16, "oaT")
        obT = S.sbuf([128, 8, 512], BF16, "obT")
        mT = S.sbuf([128, 8, 512], BF16, "mT")
        x1 = S.sbuf([128, 4, 1024], F32, "x1")
        ring = [S.sbuf([128, 8, 512], BF16, f"ring{i}") for i in range(4)]
        ring_i = [0]
        convw = S.sbuf([128, 44, 3], F32, "convw")
        convb = S.sbuf([128, 44], F32, "convb")
        for j in range(3):
            S.dma("sp", convw[:, :, j], V(X["ffn_conv_w"], X["ffn_conv_w"].t[j, :].rearrange("(c p) -> p c", p=128)),
                  allow_slow_non_contiguous=True)
        S.dma("sp", convb.a, V(X["ffn_conv_b"], X["ffn_conv_b"].t[0, :].rearrange("(c p) -> p c", p=128)),
              allow_slow_non_contiguous=True)
        ucarry = S.sbuf([128, 44, 2], F32, "ucarry")
        S.ms(ucarry.a, 0.0)
        actT = S.sbuf([128, 22, 256], BF16, "actT")
        small = S.sbuf([128, 64], F32, "small")
        xb = S.sbuf([128, 1024], BF16, "xb")

        def slab(src, c0, cw, kcs=8, k0=0):
            r = ring[ring_i[0] % len(ring)]
            ring_i[0] += 1
            q = ["sp", "act"][ring_i[0] % 2]
            S.dma(q, r[:, 0:kcs, 0:cw], src[:, k0:k0 + kcs, c0:c0 + cw])
            return r

        def rstd_of(xt, col):
            S.act(WK[11].a, xt, AF.Square, accum=small[:, col:col + 1])
            S.act(small[:, col + 1:col + 2], small[:, col:col + 1], AF.Sqrt, scale=1.0 / 1024, bias=1e-6)
            S.recip(small[:, col:col + 1], small[:, col + 1:col + 2])

        def to_T(dst, src_b, ti):
            psb = PSB
            for kc in range(8):
                S.tr(psb[:, kc * 128:(kc + 1) * 128], src_b[:, kc * 128:(kc + 1) * 128], identB.a)
            S.cp(dst, psb.a.r("p (k t) -> p k t", t=128), e="act")


        for Q in range(NQ):
            t0 = Q * 512
            for ti in range(4):
                xt = x1[:, ti, :]
                S.dma("sp", xt, V(X["x"], X["x"].t[t0 + ti * 128:t0 + (ti + 1) * 128, :]))
                rstd_of(xt, 0)
                S.ts(xb.a, xt, small[:, 0:1], op0=ALU.mult)
                to_T(xnT[:, :, 1 + ti * 128:1 + (ti + 1) * 128], xb, ti)

            if not fused:
                S.dma("pool", oaT.a, V(OA, OA.t[:, t0:t0 + 512].rearrange("(k p) t -> p k t", p=128)))
                S.dma("pool", obT.a, V(OB, OB.t[:, t0:t0 + 512].rearrange("(k p) t -> p k t", p=128)))
            else:
                for SRC, dstT in ((OA, oaT), (OB, obT)):
                    for ti in range(4):
                        S.dma("pool", xb.a, V(SRC, SRC.t[t0 + ti * 128:t0 + (ti + 1) * 128, :]))
                        to_T(dstT[:, :, ti * 128:(ti + 1) * 128], xb, ti)

            for half in range(2):
                wa = slab(Wbra, half * 512, 512)
                wga = slab(WinA, C_GA + half * 512, 512)
                wb = slab(Wbrb, half * 512, 512)
                wgb = slab(WinA, C_GB + half * 512, 512)
                for cc in range(4):
                    mc = half * 4 + cc
                    cs = slice(cc * 128, (cc + 1) * 128)
                    for kc in range(8):
                        S.mm(PS[0].a, wa[:, kc, cs], oaT[:, kc, :], kc == 0, kc == 7)
                    for kc in range(8):
                        S.mm(PS[1].a, wga[:, kc, cs], xnT[:, kc, 1:513], kc == 0, kc == 7)
                    for kc in range(8):
                        S.mm(PS[2].a, wb[:, kc, cs], obT[:, kc, :], kc == 0, kc == 7)
                    for kc in range(8):
                        S.mm(PS[3].a, wgb[:, kc, cs], xnT[:, kc, 1:513], kc == 0, kc == 7)
                    ga = WK[4][:, 0:512]
                    gb = WK[4][:, 512:1024]
                    S.act(ga, PS[1].a, AF.Sigmoid)
                    S.act(gb, PS[3].a, AF.Sigmoid)
                    S.tt(ga, ga, PS[0].a, ALU.mult)
                    S.tt(gb, gb, PS[2].a, ALU.mult)
                    S.tt(mT[:, mc, :], ga, gb, ALU.add)
            S.cp(xnT[:, :, 0:1], xnT[:, :, 512:513], e="pool")
            wo = [slab(Wout, h * 512, 512) for h in range(2)]
            for ti in range(4):
                ts_ = slice(ti * 128, (ti + 1) * 128)
                for h in range(2):
                    for kc in range(8):
                        S.mm(PS[h].a, mT[:, kc, ts_], wo[h][:, kc, :], kc == 0, kc == 7)
                y = WK[5]
                S.cp(y[:, 0:512], PS[0].a, e="act")
                S.cp(y[:, 512:1024], PS[1].a)
                rstd_of(y.a, 2)
                S.stt(y.a, y.a, small[:, 2:3], g_post.a, ALU.mult, ALU.mult)
                S.tt(x1[:, ti, :], x1[:, ti, :], y.a, ALU.add)
                rstd_of(x1[:, ti, :], 4)
                S.ts(xb.a, x1[:, ti, :], small[:, 4:5], op0=ALU.mult)
                to_T(xn2T[:, :, ti * 128:(ti + 1) * 128], xb, ti)
            for hf in range(2):
                fs = slice(hf * 256, (hf + 1) * 256)
                for cg in range(0, 22, 4):
                    ncg = min(4, 22 - cg)
                    wg = slab(Wup, cg * 128, ncg * 128)
                    wv = slab(Wup, 2816 + cg * 128, ncg * 128)
                    for c in range(ncg):
                        ch = cg + c
                        res = []
                        for which, wsl in ((0, wg), (1, wv)):
                            chan = ch + 22 * which
                            ps = PS[2 + which]
                            for kc in range(8):
                                S.mm(ps[:, 0:256], wsl[:, kc, c * 128:(c + 1) * 128], xn2T[:, kc, fs], kc == 0, kc == 7)
                            ub = WK[6 + which]
                            S.cp(ub[:, 0:2], ucarry[:, chan, :], e="pool")
                            S.cp(ub[:, 2:258], ps[:, 0:256], e="act")
                            S.cp(ucarry[:, chan, :], ub[:, 256:258], e="pool")
                            cv = WK[8 + which]
                            S.ts(cv[:, 0:256], ub[:, 2:258], convw[:, chan, 2:3], convb[:, chan:chan + 1], ALU.mult, ALU.add)
                            S.stt(cv[:, 0:256], ub[:, 1:257], convw[:, chan, 1:2], cv[:, 0:256], ALU.mult, ALU.add)
                            S.stt(cv[:, 0:256], ub[:, 0:256], convw[:, chan, 0:1], cv[:, 0:256], ALU.mult, ALU.add)
                            res.append(cv)
                        S.act(res[0][:, 256:512], res[0][:, 0:256], AF.Silu)
                        S.tt(actT[:, ch, :], res[0][:, 256:512], res[1][:, 0:256], ALU.mult)
                for oh in range(2):
                    for kg in range(0, 22, 8):
                        kn = min(8, 22 - kg)
                        wd_ = slab(Wdn, oh * 512, 512, kcs=kn, k0=kg)
                        for tl in range(2):
                            for k in range(kn):
                                S.mm(PS[4 + tl].a, actT[:, kg + k, tl * 128:(tl + 1) * 128], wd_[:, k, :],
                                     kg + k == 0, kg + k == 21)
                    for tl in range(2):
                        ti = hf * 2 + tl
                        y = WK[5 + tl]
                        S.cp(y[:, oh * 512:(oh + 1) * 512], PS[4 + tl].a, e="act" if tl == 0 else "dve")
                for tl in range(2):
                    ti = hf * 2 + tl
                    y = WK[5 + tl]
                    rstd_of(y.a, 6 + 2 * tl)
                    S.stt(y.a, y.a, small[:, 6 + 2 * tl:7 + 2 * tl], g_fpost.a, ALU.mult, ALU.mult)
                    S.tt(y.a, y.a, x1[:, ti, :], ALU.add)
                    S.dma("sp", V(OUT, OUT.t[t0 + ti * 128:t0 + (ti + 1) * 128, :]), y.a)

        S.barrier()
        print("L4 instructions:", S.n_inst)
    return nc


def build_l2(S_LEN, STAGE=99, ctx=None):
    NQ = S_LEN // 512
    fused = ctx is not None
    names = ["x", "norm_mix_pre", "w_in", "rwkv_mu", "rwkv_w0", "rwkv_w2", "rwkv_a0", "rwkv_a2", "rwkv_g2",
             "rwkv_k_k", "rwkv_k_a", "rwkv_r_k", "rwkv_ln_w", "rwkv_ln_b"]
    if not fused:
        nc = bass.Bass("TRN2", target_bir_lowering=False)
        ins = {}
        for n in names:
            shp = [S_LEN, D] if n == "x" else IN_SHAPES[n]
            ins[n] = nc.dram_tensor(n, shp, F32, kind="ExternalInput").ap()
        out_ap = nc.dram_tensor("oa", [S_LEN, D], F32, kind="ExternalOutput").ap()
    with ExitStack() as st:
        if not fused:
            S = Sched(nc, st)
            X = {n: S.ext(ins[n], n) for n in names}
            OUT = S.ext(out_ap, "oa")
        else:
            nc, S, X, OUT = ctx
            S.stack = st
        identF = S.sbuf([128, 128], F32, "identF")
        identB = S.sbuf([128, 128], BF16, "identB")
        S.ms(identF.a, 1.0, e="pool")
        S.asel(identF.a, identF.a, [[-1, 128]], ALU.is_equal, 0.0, 0, 1)
        S.cp(identB.a, identF.a)
        MUi = S.sbuf([128, 128], F32, "MUi")
        MUs = S.sbuf([128, 128], F32, "MUs")
        MLs = S.sbuf([128, 128], F32, "MLs")
        for m_, base, cm, pat in ((MUi, 0, -1, 1), (MUs, -1, -1, 1), (MLs, -1, 1, -1)):
            S.ms(m_.a, 1.0, e="pool")
            S.asel(m_.a, m_.a, [[pat, 128]], ALU.is_ge, 0.0, base, cm)
        S.ms(MUi[0:64, 64:128], 0.0, e="pool")
        S.ms(MUs[0:64, 64:128], 0.0, e="pool")
        S.ms(MLs[64:128, 0:64], 0.0, e="pool")
        Blk = S.sbuf([128, 128], F32, "Blk")
        S.ms(Blk.a, 0.0, e="pool")
        S.ms(Blk[0:64, 0:64], 1.0, e="pool")
        S.ms(Blk[64:128, 64:128], 1.0, e="pool")
        Elast = S.sbuf([128, 2, 64], F32, "Elast")
        S.ms(Elast.a, 0.0, e="pool")
        S.ms(Elast[32:64, 0, :], 1.0, e="pool")
        S.asel(Elast[32:64, 0, :], Elast[32:64, 0, :], [[0, 64]], ALU.is_equal, 0.0, -31, 1)
        S.ms(Elast[96:128, 1, :], 1.0, e="pool")
        S.asel(Elast[96:128, 1, :], Elast[96:128, 1, :], [[0, 64]], ALU.is_equal, 0.0, -31, 1)

        def bload(name, n=1024):
            t = S.sbuf([128, n], F32, name)
            S.dma("sp", t.a, V(X[name], X[name].t[0:1, :].to_broadcast([128, n])))
            return t

        g_pre = S.sbuf([128, 8], F32, "gpre")
        S.dma("sp", g_pre.a, V(X["norm_mix_pre"], X["norm_mix_pre"].t[0, :].rearrange("(kc p) -> p kc", p=128)),
              allow_slow_non_contiguous=True)
        w0b, a0b, kkb, kab, rkb, lnw, lnb = [bload(n) for n in
                                             ("rwkv_w0", "rwkv_a0", "rwkv_k_k", "rwkv_k_a", "rwkv_r_k", "rwkv_ln_w", "rwkv_ln_b")]
        w2 = S.sbuf([64, 1024], F32, "w2")
        a2 = S.sbuf([64, 1024], F32, "a2")
        g2 = S.sbuf([128, 2, 1024], F32, "g2")
        S.dma("sp", w2.a, V(X["rwkv_w2"], X["rwkv_w2"].t[:, :]))
        S.dma("sp", a2.a, V(X["rwkv_a2"], X["rwkv_a2"].t[:, :]))
        S.dma("sp", g2[:, 0, :], V(X["rwkv_g2"], X["rwkv_g2"].t[0:128, :]))
        S.dma("sp", g2[0:32, 1, :], V(X["rwkv_g2"], X["rwkv_g2"].t[128:160, :]))
        eps_t = S.sbuf([128, 1], F32, "eps")
        S.ms(eps_t.a, 1e-6)
        gneps = S.sbuf([128, 1], F32, "gneps")
        S.ms(gneps.a, 64e-5)

        WinA = S.dram([128, 8, 3360], BF16, "WinA")
        WinB = S.dram([128, 8, 3360], BF16, "WinB")
        WK = [S.sbuf([128, 1024], F32, f"wk{i}") for i in range(14)]
        PS = [S.psum([128, 512], F32, f"psb{i}") for i in range(7)]
        PSB = S.psum([128, 1024], BF16, "psB")
        ring = [S.sbuf([128, 8, 288], BF16, f"ring{i}") for i in range(3)]
        stg_b = [ring[0][:, :, 0:128], ring[1][:, :, 0:128]]
        mu_bc, omu_bc = WK[12], WK[13]
        step_i = [0]

        def prepass(ncols, dst, colscale, c_src0):
            src = X["w_in"]
            for c0 in range(0, ncols, 128):
                cw = min(128, ncols - c0)
                i = step_i[0]
                step_i[0] += 1
                sf = WK[i % 4]
                sb_ = stg_b[i % 2]
                sview = sf.a.r("p (k c) -> p k c", c=128)
                srcap = src.t[:, c_src0 + c0:c_src0 + c0 + cw].rearrange("(k p) c -> p k c", p=128)
                S.dma("sp" if i % 2 == 0 else "act", sview[:, :, 0:cw], V(src, srcap))
                for k in range(8):
                    S.stt(sb_[:, k, 0:cw], sview[:, k, 0:cw], g_pre[:, k:k + 1], colscale[:, c0:c0 + cw], ALU.mult, ALU.mult)
                S.dma("act" if i % 2 == 0 else "sp", dst[:, :, c0:c0 + cw], sb_[:, :, 0:cw])

        for cb in range(0, 3360, 1024):
            cbw = min(1024, 3360 - cb)
            S.dma("sp", mu_bc[:, 0:cbw], V(X["rwkv_mu"], X["rwkv_mu"].t[0:1, cb:cb + cbw].to_broadcast([128, cbw])))
            S.ts(omu_bc[:, 0:cbw], mu_bc[:, 0:cbw], -1.0, 1.0, ALU.mult, ALU.add)
            prepass(cbw, V(WinA, WinA.t[:, :, cb:cb + cbw]), omu_bc, cb)
            prepass(cbw, V(WinB, WinB.t[:, :, cb:cb + cbw]), mu_bc, cb)

        xnT = S.sbuf([128, 8, 513], BF16, "xnT")
        S.ms(xnT[:, :, 0:1], 0.0)
        ring_i = [0]
        small = S.sbuf([128, 128], F32, "small")
        xb = S.sbuf([128, 1024], BF16, "xb")
        thT = S.sbuf([64, 512], F32, "thT")
        adT = S.sbuf([64, 512], F32, "adT")
        sgT = S.sbuf([128, 2, 512], F32, "sgT")
        Sst = S.sbuf([64, 16, 64], F32, "Sst")
        S.ms(Sst.a, 0.0)
        XF = [S.sbuf([64, 8, 128], F32, f"xf{i}") for i in range(4)]
        class Split:
            def __init__(self, tiles):
                self.tiles = tiles

            def __getitem__(self, idx):
                p, h, t = idx
                if isinstance(h, int):
                    return self.tiles[h // 4][p, h % 4, t]
                assert h.start % 4 == 0 and h.stop - h.start == 4
                return self.tiles[h.start // 4][p, :, t]

        Xm = Split([S.sbuf([128, 4, 128], F32, f"Xm{i}") for i in range(2)])
        XTm = Split([S.sbuf([128, 4, 128], F32, f"XTm{i}") for i in range(2)])
        Tm = Split([S.sbuf([128, 4, 128], F32, f"Tm{i}") for i in range(2)])
        Mbr = S.sbuf([128, 8, 128], F32, "Mbr")
        Mkr = S.sbuf([128, 8, 128], F32, "Mkr")
        MakT = S.sbuf([128, 8, 128], F32, "MakT")
        ATt = S.sbuf([128, 8, 64], F32, "ATt")
        Gm = S.sbuf([64, 2, 8, 64], F32, "Gm")
        Hm = S.sbuf([128, 8, 64], F32, "Hm")
        RyC = S.sbuf([64, 2, 8, 128], F32, "RyC")
        pcd = S.sbuf([64, 8, 64], F32, "pcd")
        S.ms(RyC.a, 0.0)
        ATc = S.sbuf([128, 2, 8, 64], F32, "ATc")
        Hc = S.sbuf([128, 2, 8, 64], F32, "Hc")
        cmask = S.sbuf([128, 2], F32, "cmask")
        S.ms(cmask.a, 0.0)
        S.ms(cmask[0:64, 0:1], 1.0)
        S.ms(cmask[64:128, 1:2], 1.0)

        def slab(src, c0, cw):
            r = ring[ring_i[0] % len(ring)]
            ring_i[0] += 1
            q = ["sp", "act"][ring_i[0] % 2]
            S.dma(q, r[:, :, 0:cw], src[:, :, c0:c0 + cw])
            return r

        def h16(v):
            return v.r("p (h k) -> p h k", k=64)

        for Q in range(NQ):
            t0 = Q * 512
            for ti in range(4):
                xt = WK[0].a
                S.dma("sp", xt, V(X["x"], X["x"].t[t0 + ti * 128:t0 + (ti + 1) * 128, :]))
                S.act(WK[1].a, xt, AF.Square, accum=small[:, 0:1])
                S.act(small[:, 1:2], small[:, 0:1], AF.Sqrt, scale=1.0 / 1024, bias=1e-6)
                S.recip(small[:, 0:1], small[:, 1:2])
                S.ts(xb.a, xt, small[:, 0:1], op0=ALU.mult)
                for kc in range(8):
                    S.tr(PSB[:, kc * 128:(kc + 1) * 128], xb[:, kc * 128:(kc + 1) * 128], identB.a)
                S.cp(xnT[:, :, 1 + ti * 128:1 + (ti + 1) * 128], PSB.a.r("p (k t) -> p k t", t=128), e="act")
            sa = slab(WinA, C_WD, 288)
            sb2 = slab(WinB, C_WD, 288)
            for (m0, M, dst, fn) in ((0, 64, thT.a, AF.Tanh), (64, 64, adT.a, AF.Copy),
                                     (128, 128, sgT[:, 0, :], AF.Sigmoid), (256, 32, sgT[0:32, 1, :], AF.Sigmoid)):
                for kc in range(8):
                    S.mm(PS[0][0:M, :], sa[:, kc, m0:m0 + M], xnT[:, kc, 1:513], kc == 0, False)
                for kc in range(8):
                    S.mm(PS[0][0:M, :], sb2[:, kc, m0:m0 + M], xnT[:, kc, 0:512], False, kc == 7)
                S.act(dst, PS[0][0:M, :], fn)

            for ti in range(4):
                tsl = slice(ti * 128, (ti + 1) * 128)
                zr, zk, zv, sg, av, kk, kmod, bb, Lps, e2, At, Rt, Bt, Kt = [w.a for w in WK]
                for xi, dst in enumerate((zr, zk, zv)):
                    for qc in range(4):
                        c0 = xi * 1024 + qc * 256
                        sa_ = slab(WinA, c0, 256)
                        sb_ = slab(WinB, c0, 256)
                        ps = PS[qc % 2]
                        for kc in range(8):
                            S.mm(ps[:, 0:256], xnT[:, kc, 1 + ti * 128:1 + (ti + 1) * 128], sa_[:, kc, 0:256], kc == 0, False)
                        for kc in range(8):
                            S.mm(ps[:, 0:256], xnT[:, kc, ti * 128:(ti + 1) * 128], sb_[:, kc, 0:256], False, kc == 7)
                        S.cp(dst[:, qc * 256:(qc + 1) * 256], ps[:, 0:256], e="act" if qc % 2 == 0 else "dve")
                if STAGE < 2:
                    S.dma("sp", V(OUT, OUT.t[t0 + ti * 128:t0 + (ti + 1) * 128, :]), zr)
                    continue
                for h in range(2):
                    S.mm(PS[2 + h].a, thT[:, tsl], w2[:, h * 512:(h + 1) * 512])
                    S.tt(sg[:, h * 512:(h + 1) * 512], PS[2 + h].a, w0b[:, h * 512:(h + 1) * 512], ALU.add)
                S.act(sg, sg, AF.Sigmoid)
                for h in range(2):
                    S.mm(PS[2 + h].a, adT[:, tsl], a2[:, h * 512:(h + 1) * 512])
                    S.tt(av[:, h * 512:(h + 1) * 512], PS[2 + h].a, a0b[:, h * 512:(h + 1) * 512], ALU.add)
                S.act(av, av, AF.Sigmoid)
                for h in range(2):
                    S.mm(PS[2 + h].a, MUi.a, sg[:, h * 512:(h + 1) * 512])
                    S.cp(Lps[:, h * 512:(h + 1) * 512], PS[2 + h].a, e="act")
                    S.mm(PS[4 + h].a, Blk.a, sg[:, h * 512:(h + 1) * 512])
                S.tt(e2, Lps, sg, ALU.subtract)
                S.act(e2, e2, AF.Exp, scale=-CDEC)
                S.tt(kk, zk, kkb.a, ALU.mult)
                S.tt(bb, kk, kk, ALU.mult)
                S.red(small[:, 16:32], h16(bb))
                S.act(small[:, 16:32], small[:, 16:32], AF.Sqrt)
                S.ts(small[:, 16:32], small[:, 16:32], 1e-12, op0=ALU.max)
                S.recip(small[:, 32:48], small[:, 16:32])
                S.tt(h16(kk), h16(kk), small[:, 32:48].us(2).bc([128, 16, 64]), ALU.mult)
                S.stt(kmod, av, -1.0, kab.a, ALU.add, ALU.mult)
                S.stt(kmod, kmod, 1.0, zk, ALU.add, ALU.mult)
                S.tt(bb, kk, av, ALU.mult)
                S.stt(At, kk, -1.0, e2, ALU.mult, ALU.mult)
                eL = zk
                S.act(eL, Lps, AF.Exp, scale=-CDEC)
                S.tt(Rt, zr, eL, ALU.mult)
                S.act(e2, Lps, AF.Exp, scale=CDEC)
                S.tt(Bt, bb, e2, ALU.mult)
                S.tt(Kt, kmod, e2, ALU.mult)
                for h in range(2):
                    S.tt(e2[:, h * 512:(h + 1) * 512], PS[4 + h].a, Lps[:, h * 512:(h + 1) * 512], ALU.subtract)
                S.act(e2, e2, AF.Exp, scale=-CDEC)
                Bh, Kh = av, sg
                S.tt(Bh, bb, e2, ALU.mult)
                S.tt(Kh, kmod, e2, ALU.mult)
                if STAGE < 3:
                    S.dma("sp", V(OUT, OUT.t[t0 + ti * 128:t0 + (ti + 1) * 128, :]), Kt)
                    continue
                AtF, RtF, BtF, KtF = XF
                if STAGE < 4:
                    S.dma("sp", V(OUT, OUT.t[t0 + ti * 128:t0 + (ti + 1) * 128, :]), Kt)
                    continue
                yps = [PS[5], PS[6]]
                for hh in range(2):
                    for xi, src in enumerate((At, Rt, Bt, Kt)):
                        for q4 in range(2):
                            pb_ = PS[2 + q4]
                            for hq in range(4):
                                h = hh * 8 + q4 * 4 + hq
                                S.tr(pb_[0:64, hq * 128:(hq + 1) * 128], src[:, h * 64:(h + 1) * 64], identF.a)
                            S.cp(XF[xi][:, q4 * 4:(q4 + 1) * 4, :], pb_[0:64, :].r("p (a t) -> p a t", t=128),
                                 e="act" if q4 % 2 == 0 else "dve")

                    def fm(Xf, hl):
                        return Xf[:, hl, :]
                    def gram(dst, Lf, Rf, mask):
                        for g4 in range(2):
                            for hl4 in range(4):
                                hl = g4 * 4 + hl4
                                S.mm(PS[g4][:, hl4 * 128:(hl4 + 1) * 128], fm(Lf, hl), fm(Rf, hl))
                            S.tt(dst[:, g4 * 4:(g4 + 1) * 4, :], PS[g4].a.r("p (a t) -> p a t", t=128),
                                 mask.a.us(1).bc([128, 4, 128]), ALU.mult)
                    gram(Xm, BtF, AtF, MUs)
                    gram(XTm, AtF, BtF, MLs)
                    gram(MakT, AtF, KtF, MLs)
                    gram(Mbr, BtF, RtF, MUi)
                    gram(Mkr, KtF, RtF, MUi)
                    if STAGE < 5:
                        continue
                    for g4 in range(2):
                        S.tt(Tm[:, g4 * 4:(g4 + 1) * 4, :], Xm[:, g4 * 4:(g4 + 1) * 4, :],
                             identF.a.us(1).bc([128, 4, 128]), ALU.add)
                    for it in range(5):
                        for g4 in range(2):
                            for hl4 in range(4):
                                hl = g4 * 4 + hl4
                                S.mm(PS[g4][:, hl4 * 128:(hl4 + 1) * 128], XTm[:, hl, :], Xm[:, hl, :])
                                S.mm(PS[2 + g4][:, hl4 * 128:(hl4 + 1) * 128], Xm[:, hl, :], XTm[:, hl, :])
                        for g4 in range(2):
                            S.cp(Xm[:, g4 * 4:(g4 + 1) * 4, :], PS[g4].a.r("p (a t) -> p a t", t=128), e="act")
                            S.cp(XTm[:, g4 * 4:(g4 + 1) * 4, :], PS[2 + g4].a.r("p (a t) -> p a t", t=128))
                        for g4 in range(2):
                            for hl4 in range(4):
                                hl = g4 * 4 + hl4
                                S.mm(PS[g4][:, hl4 * 128:(hl4 + 1) * 128], XTm[:, hl, :], Tm[:, hl, :])
                            S.tt(Tm[:, g4 * 4:(g4 + 1) * 4, :], Tm[:, g4 * 4:(g4 + 1) * 4, :],
                                 PS[g4].a.r("p (a t) -> p a t", t=128), ALU.add)
                    if STAGE < 6:
                        continue
                    for hl in range(8):
                        h = hh * 8 + hl
                        S.mm(PS[0][:, hl * 64:(hl + 1) * 64], Tm[:, hl, :], At[:, h * 64:(h + 1) * 64])
                    S.cp(ATt.a, PS[0].a.r("p (a k) -> p a k", k=64), e="act")
                    for g4 in range(2):
                        for hl4 in range(4):
                            hl = g4 * 4 + hl4
                            S.mm(PS[1 + g4][:, hl4 * 128:(hl4 + 1) * 128], Tm[:, hl, :], MakT[:, hl, :])
                    for g4 in range(2):
                        S.cp(Xm[:, g4 * 4:(g4 + 1) * 4, :], PS[1 + g4].a.r("p (a t) -> p a t", t=128), e="act" if g4 else "dve")
                    N1T = Xm
                    for g4 in range(2):
                        for hl4 in range(4):
                            hl = g4 * 4 + hl4
                            S.mm(PS[3 + g4][:, hl4 * 128:(hl4 + 1) * 128], N1T[:, hl, :], Mbr[:, hl, :])
                        S.tt(XTm[:, g4 * 4:(g4 + 1) * 4, :], Mkr[:, g4 * 4:(g4 + 1) * 4, :],
                             PS[3 + g4].a.r("p (a t) -> p a t", t=128), ALU.add)
                    Ny = XTm
                    for hl in range(8):
                        h = hh * 8 + hl
                        S.mm(PS[0][:, hl * 64:(hl + 1) * 64], N1T[:, hl, :], Bh[:, h * 64:(h + 1) * 64])
                    S.tt(Hm.a, PS[0].a.r("p (a k) -> p a k", k=64), Kh[:, hh * 512:(hh + 1) * 512].r("p (a k) -> p a k", k=64), ALU.add)
                    for c in range(2):
                        S.ts(ATc[:, c, :, :], ATt.a, cmask[:, c:c + 1], op0=ALU.mult, e="pool")
                        S.ts(Hc[:, c, :, :], Hm.a, cmask[:, c:c + 1], op0=ALU.mult, e="pool")
                    for g4 in range(2):
                        for hl4 in range(4):
                            hl = g4 * 4 + hl4
                            h = hh * 8 + hl
                            o = PS[3 + g4][0:64, hl4 * 128:(hl4 + 1) * 128]
                            S.mm(o, ATt[:, hl, :], Mbr[:, hl, :], True, False)
                            S.mm(o, Rt[:, h * 64:(h + 1) * 64], identF.a, False, True)
                        pv = PS[3 + g4][0:64, :].r("p (a t) -> p a t", t=128)
                        S.cp(RyC[:, 0, g4 * 4:(g4 + 1) * 4, 0:64], pv[:, :, 0:64], e="act")
                        S.cp(RyC[:, 1, g4 * 4:(g4 + 1) * 4, 64:128], pv[:, :, 64:128])
                    for c in range(2):
                        for hl in range(8):
                            h = hh * 8 + hl
                            S.mm(PS[1][0:64, hl * 64:(hl + 1) * 64], ATc[:, c, hl, :], Bh[:, h * 64:(h + 1) * 64])
                        S.mm(PS[2][0:64, :], Elast[:, c, :], eL[:, hh * 512:(hh + 1) * 512])
                        S.tt(pcd.a, PS[2][0:64, :].r("p (a k) -> p a k", k=64), identF[0:64, 0:64].us(1).bc([64, 8, 64]), ALU.mult)
                        S.tt(Gm[:, c, :, :], PS[1][0:64, :].r("p (a k) -> p a k", k=64), pcd.a, ALU.add)
                    if STAGE < 7:
                        continue
                    yp = yps[hh]
                    for hl in range(8):
                        h = hh * 8 + hl
                        S.mm(yp[:, hl * 64:(hl + 1) * 64], Ny[:, hl, :], zv[:, h * 64:(h + 1) * 64], hl == 0, False)
                    for c in range(2):
                        for hl in range(8):
                            h = hh * 8 + hl
                            S.mm(yp[:, hl * 64:(hl + 1) * 64], RyC[:, c, hl, :], Sst[:, h, :], False, c == 1 and hl == 7)
                        for hl in range(8):
                            h = hh * 8 + hl
                            o = PS[0][0:64, hl * 64:(hl + 1) * 64]
                            S.mm(o, Gm[:, c, hl, :], Sst[:, h, :], True, False)
                            S.mm(o, Hc[:, c, hl, :], zv[:, h * 64:(h + 1) * 64], False, True)
                        S.cp(Sst[:, hh * 8:(hh + 1) * 8, :], PS[0][0:64, :].r("p (a k) -> p a k", k=64), e="act")
                if STAGE < 8:
                    S.dma("sp", V(OUT, OUT.t[t0 + ti * 128:t0 + (ti + 1) * 128, :]), Mkr.a.r("p a t -> p (a t)"))
                    continue
                yv, t1, t2 = kk, bb, Lps
                S.cp(yv[:, 0:512], yps[0].a, e="act")
                S.cp(yv[:, 512:1024], yps[1].a)
                S.red(small[:, 48:64], h16(yv))
                S.ts(small[:, 48:64], small[:, 48:64], 1.0 / 64, op0=ALU.mult)
                S.tt(h16(yv), h16(yv), small[:, 48:64].us(2).bc([128, 16, 64]), ALU.subtract)
                S.tt(t1, yv, yv, ALU.mult)
                S.red(small[:, 64:80], h16(t1))
                S.act(small[:, 64:80], small[:, 64:80], AF.Sqrt, scale=1.0 / 64, bias=64e-5)
                S.recip(small[:, 80:96], small[:, 64:80])
                S.tt(h16(yv), h16(yv), small[:, 80:96].us(2).bc([128, 16, 64]), ALU.mult)
                S.tt(yv, yv, lnw.a, ALU.mult)
                S.tt(yv, yv, lnb.a, ALU.add)
                S.tt(t1, zr, kmod, ALU.mult)
                S.tt(t1, t1, rkb.a, ALU.mult)
                S.red(small[:, 96:112], h16(t1))
                S.tt(h16(t1), h16(zv), small[:, 96:112].us(2).bc([128, 16, 64]), ALU.mult)
                S.tt(yv, yv, t1, ALU.add)
                for h in range(2):
                    S.mm(PS[2 + h].a, sgT[:, 0, tsl], g2[:, 0, h * 512:(h + 1) * 512], True, False)
                    S.mm(PS[2 + h].a, sgT[0:32, 1, tsl], g2[0:32, 1, h * 512:(h + 1) * 512], False, True)
                    S.tt(t2[:, h * 512:(h + 1) * 512], yv[:, h * 512:(h + 1) * 512], PS[2 + h].a, ALU.mult)
                S.dma("sp", V(OUT, OUT.t[t0 + ti * 128:t0 + (ti + 1) * 128, :]), t2)
            S.cp(xnT[:, :, 0:1], xnT[:, :, 512:513], e="pool")
        S.barrier()
        print("L2 instructions:", S.n_inst)
    return nc


def build_l3(S_LEN, STG=99, ctx=None):
    NQ = S_LEN // 512
    fused = ctx is not None
    names = ["x", "norm_mix_pre", "w_in", "cmp_pos_k", "cmp_w1_k", "cmp_w2_k", "cmp_pos_v", "cmp_w1_v", "cmp_w2_v"]
    if not fused:
        nc = bass.Bass("TRN2", target_bir_lowering=False)
        ins = {}
        for n in names:
            shp = [S_LEN, D] if n == "x" else IN_SHAPES[n]
            ins[n] = nc.dram_tensor(n, shp, F32, kind="ExternalInput").ap()
        out_ap = nc.dram_tensor("ob", [S_LEN, D], F32, kind="ExternalOutput").ap()
    with ExitStack() as st:
        if not fused:
            S = Sched(nc, st)
            X = {n: S.ext(ins[n], n) for n in names}
            OUT = S.ext(out_ap, "ob")
        else:
            nc, S, X, OUT = ctx
            S.stack = st
        identF = S.sbuf([128, 128], F32, "identF")
        identB = S.sbuf([128, 128], BF16, "identB")
        S.ms(identF.a, 1.0, e="pool")
        S.asel(identF.a, identF.a, [[-1, 128]], ALU.is_equal, 0.0, 0, 1)
        S.cp(identB.a, identF.a)
        g_pre = S.sbuf([128, 8], F32, "gpre")
        S.dma("sp", g_pre.a, V(X["norm_mix_pre"], X["norm_mix_pre"].t[0, :].rearrange("(kc p) -> p kc", p=128)),
              allow_slow_non_contiguous=True)
        eps_t = S.sbuf([128, 1], F32, "eps")
        S.ms(eps_t.a, 1e-6)
        WK = [S.sbuf([128, 1024], F32, f"wk{i}") for i in range(4)]
        PS = [S.psum([128, 512], F32, f"psb{i}") for i in range(7)]
        PSB = S.psum([128, 1024], BF16, "psB")
        stg_b = [S.sbuf([128, 8, 128], BF16, f"stgb{i}") for i in range(2)]
        NC_ = 2608
        WinA = S.dram([128, 8, NC_], BF16, "WinA")
        for i, c0 in enumerate(range(0, NC_, 128)):
            cw = min(128, NC_ - c0)
            sf = WK[i % 4]
            sb_ = stg_b[i % 2]
            sview = sf.a.r("p (k c) -> p k c", c=128)
            srcap = X["w_in"].t[:, NS0 + c0:NS0 + c0 + cw].rearrange("(k p) c -> p k c", p=128)
            S.dma("sp" if i % 2 == 0 else "act", sview[:, :, 0:cw], V(X["w_in"], srcap))
            for k in range(8):
                S.ts(sb_[:, k, 0:cw], sview[:, k, 0:cw], g_pre[:, k:k + 1], op0=ALU.mult, e="dve" if k % 2 else "pool")
            S.dma("pool", WinA[:, :, c0:c0 + cw], sb_[:, :, 0:cw])
        KS = S.dram([67, 4, S_LEN], BF16, "KS")
        VS = S.dram([128, S_LEN // 128, 4, 65], BF16, "VS")

        Emat = S.sbuf([128, 64, 128], BF16, "Emat")
        S.ms(Emat.a, 1.0, e="pool")
        S.asel(Emat.a.r("p i (a b) -> p i a b", b=64), Emat.a.r("p i (a b) -> p i a b", b=64),
               [[-2, 64], [-1, 2], [0, 64]], ALU.is_equal, 0.0, 0, 1)
        NEGcur = S.sbuf([128, 4, 512], BF16, "NEGcur")
        NEGprev = S.sbuf([128, 4, 512], BF16, "NEGprev")
        NEGcmp = S.sbuf([128, 4, 512], BF16, "NEGcmp")
        S.ms(NEGcur.a, 0.0, e="pool")
        S.ms(NEGprev.a, 0.0, e="pool")
        S.ms(NEGcmp.a, 0.0, e="pool")
        for a in range(4):
            S.asel(NEGcur[:, a, :], NEGcur[:, a, :], [[1, 512]], ALU.is_ge, NEGM, -128 * a, -1)
            S.asel(NEGprev[:, a, :], NEGprev[:, a, :], [[-1, 512]], ALU.is_ge, NEGM, 128 * a - 1, 1)
            S.asel(NEGcmp[:, a, :], NEGcmp[:, a, :], [[1, 512]], ALU.is_ge, NEGM, 512 * a - 15, -16)
        Amap = S.sbuf([128, 4, 128], BF16, "Amap")
        A2 = S.sbuf([128, 4, 128], BF16, "A2")
        S.ms(Amap.a, 1.0, e="pool")
        S.ms(A2.a, 1.0, e="pool")
        for N in range(4):
            S.asel(Amap[:, N, :], Amap[:, N, :], [[-4, 128]], ALU.is_ge, 0.0, 128 * N - 1 + 1, 1)
            S.asel(Amap[:, N, :], Amap[:, N, :], [[4, 128]], ALU.is_ge, 0.0, 3 - (128 * N - 1), -1)
            S.asel(A2[:, N, :], A2[:, N, :], [[-4, 128]], ALU.is_ge, 0.0, 128 * N - 1, 1)
            S.asel(A2[:, N, :], A2[:, N, :], [[4, 128]], ALU.is_ge, 0.0, 2 - (128 * N - 1), -1)
        S.tt(Amap.a, Amap.a, A2.a, ALU.add, e="pool")
        pd_i = S.sbuf([128, 67], mybir.dt.int32, "pdi")
        pd = S.sbuf([128, 67], F32, "pd")
        S.op("pool", lambda g: g.iota(pd_i.t[:], pattern=[[-128, 67]], base=384, channel_multiplier=1), [], [pd_i.a])
        S.cp(pd.a, pd_i.a)
        biasSel = S.sbuf([128, 16, 67], F32, "biasSel")
        pc_i = S.sbuf([128, 16], mybir.dt.int32, "pci")
        pcf = S.sbuf([128, 16], F32, "pcf")
        S.op("pool", lambda g: g.iota(pc_i.t[:], pattern=[[-512, 16]], base=0, channel_multiplier=16), [], [pc_i.a])
        S.cp(pcf.a, pc_i.a)
        biasCmp = S.sbuf([128, 16, 16], F32, "biasCmp")
        biasCmp0 = S.sbuf([128, 16, 16], F32, "biasCmp0")
        for h in range(16):
            S.ts(biasSel[:, h, :], pd.a, SLOPES[h], op0=ALU.mult)
            S.ts(biasCmp[:, h, :], pcf.a, -0.5, SLOPES[h], ALU.add, ALU.mult)
        S.cp(biasCmp0.a, biasCmp.a)
        S.ms(biasCmp0[0:1, :, :], NEGM)
        tiebrk = S.sbuf([128, 128], F32, "tiebrk")
        tb_i = S.sbuf([128, 128], mybir.dt.int32, "tbi")
        S.op("pool", lambda g: g.iota(tb_i.t[:], pattern=[[1, 128]], base=0, channel_multiplier=0), [], [tb_i.a])
        S.cp(tiebrk.a, tb_i.a)
        S.ts(tiebrk.a, tiebrk.a, -1e-30, op0=ALU.mult)
        Qaug = S.sbuf([67, 16, 512], BF16, "Qaug")
        fi = S.sbuf([67, 512], mybir.dt.int32, "fi")
        ff = S.sbuf([67, 512], F32, "ff")
        S.op("pool", lambda g: g.iota(fi.t[:], pattern=[[1, 512]], base=0, channel_multiplier=0), [], [fi.a])
        S.cp(ff.a, fi.a)
        r0 = S.sbuf([67, 512], F32, "r0")
        r1 = S.sbuf([67, 512], F32, "r1")
        qs = S.sbuf([67, 3, 512], BF16, "qs")
        for h in range(16):
            S.ts(r0[64:65, :], ff[64:65, :], -SLOPES[h], op0=ALU.mult)
            S.cp(qs[64:65, 0, :], r0[64:65, :])
            S.tt(r1[64:65, :], r0[64:65, :], qs[64:65, 0, :], ALU.subtract)
            S.cp(qs[64:65, 1, :], r1[64:65, :])
            S.tt(r0[64:65, :], r1[64:65, :], qs[64:65, 1, :], ALU.subtract)
            S.cp(qs[64:65, 2, :], r0[64:65, :])
            for j in range(3):
                S.dma("sp", Qaug[64 + j:65 + j, h, :], qs[64:65, j, :])

        def aug_ones(tile_v):
            S.ms(tile_v, 1.0)

        KSst = S.sbuf([67, 4, 512], BF16, "KSst")
        aug_ones(KSst[64:67, :, :])
        kwT = S.sbuf([67, 4, 2, 512], BF16, "kwT")
        aug_ones(kwT[64:67, :, :, :])
        kcmpT = S.sbuf([67, 4, 512], BF16, "kcmpT")
        S.ms(kcmpT.a, 0.0)
        aug_ones(kcmpT[64:67, :, :])
        VSst = S.sbuf([128, 4, 4, 65], BF16, "VSst")
        S.ms(VSst.a, 1.0)
        vwA = S.sbuf([128, 2, 4, 4, 65], BF16, "vwA")
        S.ms(vwA.a, 1.0)
        vcmpA = S.sbuf([128, 4, 4, 65], BF16, "vcmpA")
        S.ms(vcmpA.a, 0.0)
        S.ms(vcmpA[:, :, :, 64:65], 1.0)
        kcT = S.sbuf([64, 4, 528], BF16, "kcT")
        vcT = S.sbuf([64, 4, 528], BF16, "vcT")
        S.ms(kcT.a, 0.0)
        S.ms(vcT.a, 0.0)
        W1c = {}
        W2c = {}
        pbias = {}
        for kv in ("k", "v"):
            w1 = S.sbuf([64, 32, 128], BF16, "W1" + kv)
            S.dma("pool", w1.a, V(X["cmp_w1_" + kv], X["cmp_w1_" + kv].t[:, :].rearrange("(l d) h -> d l h", d=64)))
            W1c[kv] = w1
            w2 = S.sbuf([128, 64], BF16, "W2" + kv)
            S.dma("pool", w2.a, V(X["cmp_w2_" + kv], X["cmp_w2_" + kv].t[:, :]))
            W2c[kv] = w2
            w1f = WK[0].a.r("p (c h) -> p c h", h=128)[:, 0:16 // 2, :]
            pb = S.sbuf([128, 1], F32, "pb" + kv)
            posT = S.sbuf([128, 16], F32, "posT" + kv)
            S.dma("sp", posT.a, V(X["cmp_pos_" + kv], X["cmp_pos_" + kv].t[0, :].rearrange("(c p) -> p c", p=128)),
                  allow_slow_non_contiguous=True)
            for half in range(2):
                S.dma("sp", w1f, V(X["cmp_w1_" + kv], X["cmp_w1_" + kv].t[half * 1024:(half + 1) * 1024, :].rearrange("(c p) h -> p c h", p=128)))
                for c in range(8):
                    S.mm(PS[0][:, 0:1], w1f[:, c, :], posT[:, half * 8 + c:half * 8 + c + 1], half == 0 and c == 0, half == 1 and c == 7)
            S.cp(pb.a, PS[0][:, 0:1])
            pbias[kv] = pb

        xnT = S.sbuf([128, 8, 512], BF16, "xnT")
        ring = [S.sbuf([128, 8, 512], BF16, f"ring{i}") for i in range(3)]
        ring_i = [0]
        small = S.sbuf([128, 64], F32, "small")
        xb = S.sbuf([128, 1024], BF16, "xb")
        gates = S.sbuf([128, 4, 48], F32, "gates")
        obuf = S.sbuf([128, 4, 1024], F32, "obuf")
        NBT = S.sbuf([128, 4, 512], BF16, "NBT")
        Kc = [S.sbuf([67, 512], BF16, f"Kc{i}") for i in range(2)]
        Vc = [S.sbuf([128, 4, 65], BF16, f"Vc{i}") for i in range(2)]
        Pt = [S.sbuf([128, 512], BF16, f"Pt{i}") for i in range(3)]
        Sb = [S.sbuf([128, 512], F32, f"Sb{i}") for i in range(3)]
        PSS = [PS[0], PS[1], PS[5]]
        p_i = [0]
        hid = S.sbuf([128, 4, 32], F32, "hid")
        hidpad = S.sbuf([128, 128], BF16, "hidpad")
        S.ms(hidpad.a, 0.0)
        sc = S.sbuf([128, 128], F32, "sc")
        sc2 = S.sbuf([128, 128], F32, "sc2")
        mx = S.sbuf([128, 16], F32, "mx")
        rs = S.sbuf([128, 64], F32, "rs")

        def slab(c0, cw):
            r = ring[ring_i[0] % len(ring)]
            ring_i[0] += 1
            q = ["sp", "act"][ring_i[0] % 2]
            S.dma(q, r[:, :, 0:cw], WinA[:, :, c0 - NS0:c0 - NS0 + cw])
            return r

        def attn_tile(h, lhsT, masks, bias_ap, vrhs, acc, first, last, imp=None, imp_rhs=None, s_lo=0, s_hi=4):
            ps = PS[p_i[0] % 2]
            pt = Pt[p_i[0] % 2]
            sb_ = Sb[p_i[0] % 2]
            p_i[0] += 1
            fs = slice(128 * s_lo, 128 * s_hi)
            S.mm(ps[:, fs], lhsT, Qaug[:, h, fs], True, len(masks) == 0)
            for mi, (ml, mr) in enumerate(masks):
                S.mm(ps[:, fs], ml, mr[:, fs], False, mi == len(masks) - 1)
            S.ts(sb_[:, fs], ps[:, fs], bias_ap, op0=ALU.add)
            S.act(pt[:, fs], sb_[:, fs], AF.Exp)
            for s in range(s_lo, s_hi):
                S.mm(acc[:, s * 65:(s + 1) * 65], pt[:, s * 128:(s + 1) * 128], vrhs,
                     first and s == s_lo, last and s == s_hi - 1)
            if imp is not None:
                for s in range(4):
                    S.mm(imp[:, s * 128:(s + 1) * 128], pt[:, s * 128:(s + 1) * 128], imp_rhs, first and s == 0, last and s == 3)

        def finish(h, br, acc, first_branch):
            accv = acc[:, 0:260].r("p (s c) -> p s c", c=65)
            S.ts(rs[:, 0:4], accv[:, :, 64], 1e-30, op0=ALU.max)
            S.recip(rs[:, 4:8], rs[:, 0:4])
            S.tt(rs[:, 8:12], rs[:, 4:8], gates[:, :, 3 * h + br], ALU.mult)
            for s in range(4):
                o = obuf[:, s, h * 64:(h + 1) * 64]
                if first_branch:
                    S.ts(o, accv[:, s, 0:64], rs[:, 8 + s:9 + s], op0=ALU.mult)
                else:
                    S.stt(o, accv[:, s, 0:64], rs[:, 8 + s:9 + s], o, ALU.mult, ALU.add)

        for Q in range(NQ):
            t0 = Q * 512
            par = Q % 2
            for ti in range(4):
                xt = WK[0].a
                S.dma("sp", xt, V(X["x"], X["x"].t[t0 + ti * 128:t0 + (ti + 1) * 128, :]))
                S.act(WK[1].a, xt, AF.Square, accum=small[:, 0:1])
                S.act(small[:, 1:2], small[:, 0:1], AF.Sqrt, scale=1.0 / 1024, bias=1e-6)
                S.recip(small[:, 0:1], small[:, 1:2])
                S.ts(xb.a, xt, small[:, 0:1], op0=ALU.mult)
                for kc in range(8):
                    S.tr(PSB[:, kc * 128:(kc + 1) * 128], xb[:, kc * 128:(kc + 1) * 128], identB.a)
                S.cp(xnT[:, :, ti * 128:(ti + 1) * 128], PSB.a.r("p (k t) -> p k t", t=128), e="act")
            S.cp(kcT[:, :, 0:16], kcT[:, :, 512:528], e="pool")
            S.cp(vcT[:, :, 0:16], vcT[:, :, 512:528], e="pool")

            def fm_proj(c0, M, dst, scale=1.0):
                sl_ = slab(c0, M)
                for kc in range(8):
                    S.mm(PS[0][0:M, :], sl_[:, kc, 0:M], xnT[:, kc, :], kc == 0, kc == 7)
                S.act(dst, PS[0][0:M, :], AF.Copy, scale=scale)

            for h in range(16):
                fm_proj(C_Q + h * 64, 64, Qaug[0:64, h, :], 0.125)
            for g in range(4):
                fm_proj(C_KS + g * 64, 64, KSst[0:64, g, :])
                fm_proj(C_KW + g * 64, 64, kwT[0:64, g, par, :])
                fm_proj(C_KC + g * 64, 64, kcT[:, g, 16:528])
                fm_proj(C_VC + g * 64, 64, vcT[:, g, 16:528])
            S.dma("sp", KS[:, :, t0:t0 + 512], KSst.a)
            s_vs = slab(C_VS, 256)
            s_vw = slab(C_VW, 304)
            for ti in range(4):
                tsl = slice(ti * 128, (ti + 1) * 128)
                for kc in range(8):
                    S.mm(PS[0][:, 0:256], xnT[:, kc, tsl], s_vs[:, kc, 0:256], kc == 0, kc == 7)
                S.cp(VSst[:, ti, :, 0:64], PS[0][:, 0:256].r("p (g d) -> p g d", d=64), e="act")
                for kc in range(8):
                    S.mm(PS[1][:, 0:304], xnT[:, kc, tsl], s_vw[:, kc, 0:304], kc == 0, kc == 7)
                S.cp(vwA[:, par, ti, :, 0:64], PS[1][:, 0:256].r("p (g d) -> p g d", d=64))
                S.act(gates[:, ti, :], PS[1][:, 256:304], AF.Sigmoid)
            S.dma("act", VS[:, Q * 4:(Q + 1) * 4, :, :], VSst.a)

            if STG < 1:
                continue
            i0 = 1 if Q == 0 else 0
            ni = 32 - i0
            for kv, src in (("k", kcT), ("v", vcT)):
                for g in range(4):
                    for l in range(32):
                        S.mm(PS[0][:, 0:ni], W1c[kv][:, l, :], src[:, g, l + 16 * i0:l + 16 * 31 + 1:16], l == 0, l == 31)
                    xs = hid[:, 0, 0:ni]
                    tq = hid[:, 1, 0:ni]
                    S.ts(xs, PS[0][:, 0:ni], pbias[kv][:, 0:1], op0=ALU.add)
                    S.tt(tq, xs, xs, ALU.mult)
                    S.ts(tq, tq, 0.044715, 1.0, ALU.mult, ALU.add)
                    S.tt(tq, tq, xs, ALU.mult)
                    S.act(tq, tq, AF.Sigmoid, scale=1.5957691216057308)
                    pr = 32 * (Q % 4)
                    S.tt(hidpad[:, pr + i0:pr + 32], xs, tq, ALU.mult)
                    if kv == "k":
                        S.mm(PS[1][0:64, 0:32], W2c[kv].a, hidpad[:, pr:pr + 32])
                        S.cp(kcmpT[0:64, g, 32 * Q:32 * Q + 32], PS[1][0:64, 0:32], e="act")
                    else:
                        S.mm(PS[1][:, 0:64], hidpad.a, W2c[kv].a)
                        S.cp(vcmpA[pr:pr + 32, Q // 4, g, 0:64], PS[1][pr:pr + 32, 0:64], e="act")

            if STG < 2:
                continue
            NN = Q // 4 + 1
            for g in range(4):
                for hl in range(4):
                    h = g * 4 + hl
                    acc = PS[2]
                    imp = PS[3 + (hl % 2)]
                    for N in range(NN):
                        d = Q - 4 * N
                        masks = [(identB.a, NEGcmp[:, d, :])] if d <= 3 else []
                        btab = biasCmp0 if N == 0 else biasCmp
                        attn_tile(h, kcmpT[:, g, N * 128:(N + 1) * 128], masks, btab[:, h, d:d + 1],
                                  vcmpA[:, N, g, :], acc, N == 0, N == NN - 1, imp=imp, imp_rhs=Amap[:, N, :])
                    finish(h, 0, acc, True)
                    for s in range(4):
                        scg = WK[2 + (s // 2)][:, (s % 2) * 512 + g * 128:(s % 2) * 512 + (g + 1) * 128]
                        S.stt(scg, imp[:, s * 128:(s + 1) * 128], rs[:, 4 + s:5 + s], tiebrk.a if hl == 0 else scg, ALU.mult, ALU.add)
                c0b = Q * 8
                for s in range(4):
                    scg = WK[2 + (s // 2)][:, (s % 2) * 512 + g * 128:(s % 2) * 512 + (g + 1) * 128]
                    cb = c0b + 2 * s
                    S.cp(sc.a, scg)
                    if cb + 2 < 128:
                        S.ms(sc[:, cb + 2:128], -1e30)
                    S.ms(sc[0:64, cb + 1:cb + 2], -1e30)
                    S.ms(sc[:, 0:1], 1e9)
                    S.ms(sc[:, cb:cb + 1], 1e9)
                    if cb >= 1:
                        S.ms(sc[0:64, cb - 1:cb], 1e9)
                    S.ms(sc[64:128, cb + 1:cb + 2], 1e9)
                    S.op("dve", lambda e_: e_.max(out=mx.t[:, 0:8], in_=sc.t[:]), [sc.a], [mx.a])
                    S.op("dve", lambda e_: e_.match_replace(out=sc2.t[:], in_to_replace=mx.t[:, 0:8], in_values=sc.t[:], imm_value=-1e30),
                         [sc.a, mx.a], [sc2.a])
                    S.op("dve", lambda e_: e_.max(out=mx.t[:, 8:16], in_=sc2.t[:]), [sc2.a], [mx.a])
                    S.ts(sc2.a, sc.a, mx[:, 15:16], op0=ALU.is_ge)
                    S.tr(PS[5][:, 0:128], sc2.a, identF.a)
                    S.act(NBT[:, g, s * 128:(s + 1) * 128], PS[5][:, 0:128], AF.Identity, scale=-NEGM, bias=NEGM)

            if STG < 3:
                continue
            def cmin(h):
                keep = int((150.0 / SLOPES[h] - 1.0) // 512) + 1
                return max(0, Q - keep)
            for g in range(4):
                accs = [PS[2], PS[3], PS[4], PS[6]]
                cm = [cmin(g * 4 + hl) for hl in range(4)]
                for C in range(min(cm), Q + 1):
                    kc_ = Kc[C % 2]
                    vc_ = Vc[C % 2]
                    S.dma("sp", kc_.a, KS[:, g, C * 512:(C + 1) * 512])
                    S.dma("act", vc_.a, VS[:, C * 4:(C + 1) * 4, g, :])
                    for hl in range(4):
                        h = g * 4 + hl
                        if C < cm[hl]:
                            continue
                        for a in range(4):
                            i = 4 * C + a
                            masks = [(Emat[:, i, :], NBT[:, g, :])]
                            if C == Q:
                                masks.append((identB.a, NEGcur[:, a, :]))
                            dd = 4 * (Q - C) - a
                            attn_tile(h, kc_[:, a * 128:(a + 1) * 128], masks, biasSel[:, h, dd + 3:dd + 4],
                                      vc_[:, a, :], accs[hl], C == cm[hl] and a == 0, C == Q and a == 3,
                                      s_lo=(a if C == Q else 0))
                for hl in range(4):
                    finish(g * 4 + hl, 1, accs[hl], False)
            if STG < 4:
                if STG == 3:
                    for ti in range(4):
                        S.dma('sp', V(OUT, OUT.t[t0 + ti * 128:t0 + (ti + 1) * 128, :]), obuf[:, ti, :])
                continue
            for h in range(16):
                g = h // 4
                acc = PS[2 + (h % 2)]
                tiles = []
                if Q > 0:
                    for a in range(4):
                        tiles.append((kwT[:, g, 1 - par, a * 128:(a + 1) * 128], NEGprev[:, a, :], 4 - a, vwA[:, 1 - par, a, g, :], 0, a + 1))
                for a in range(4):
                    tiles.append((kwT[:, g, par, a * 128:(a + 1) * 128], NEGcur[:, a, :], -a, vwA[:, par, a, g, :], a, 4))
                if Q > 0:
                    pass
                for k_, (kl, mk, dd, vr, slo, shi) in enumerate(tiles):
                    attn_tile(h, kl, [(identB.a, mk)], biasSel[:, h, dd + 3:dd + 4], vr, acc, k_ == 0, k_ == len(tiles) - 1,
                              s_lo=slo, s_hi=shi)
                finish(h, 2, acc, False)
            for ti in range(4):
                S.dma("sp", V(OUT, OUT.t[t0 + ti * 128:t0 + (ti + 1) * 128, :]), obuf[:, ti, :])
        S.barrier()
        print("L3 instructions:", S.n_inst)
    return nc


_CACHE = {}


def build_fused(S_LEN):
    nc = bass.Bass("TRN2", target_bir_lowering=False)
    ins = {}
    for n in IN_NAMES:
        shp = [S_LEN, D] if n == "x" else IN_SHAPES[n]
        ins[n] = nc.dram_tensor(n, shp, F32, kind="ExternalInput").ap()
    out_ap = nc.dram_tensor("out", [S_LEN, D], F32, kind="ExternalOutput").ap()
    with ExitStack() as outer:
        S = Sched(nc, outer)
        X = {n: S.ext(ins[n], n) for n in IN_NAMES}
        OUT = S.ext(out_ap, "out")
        OA = S.dram([S_LEN, D], F32, "oa_s")
        OB = S.dram([S_LEN, D], F32, "ob_s")
        build_l2(S_LEN, ctx=(nc, S, X, OA))
        build_l3(S_LEN, ctx=(nc, S, X, OB))
        build_l4(S_LEN, ctx=(nc, S, X, OUT, OA, OB))
    return nc


def kernel(**inputs):
    x = np.asarray(inputs["x"], dtype=np.float32)
    B, S_LEN = x.shape[0], x.shape[1]
    if S_LEN not in _CACHE:
        _CACHE[S_LEN] = build_fused(S_LEN)
    nc = _CACHE[S_LEN]
    shared = {n: np.ascontiguousarray(np.asarray(inputs[n], dtype=np.float32).reshape(IN_SHAPES[n]))
              for n in IN_NAMES if n != "x"}
    maps = [dict(shared, x=np.ascontiguousarray(x[b])) for b in range(B)]
    res = run_bass_kernel_spmd(nc, maps, core_ids=list(range(B)))
    return np.stack([np.asarray(r["out"], dtype=np.float32) for r in res.results], axis=0)
```
